# Optimizing a Trainium2 kernel written in Bass

```python
import math
import jax, jax.numpy as jnp
from jax import lax
import numpy as np


D_MODEL = 1024
BATCH = 8
SEQ = 8192
DEPTH = 4

HEAD_DIM = 64
N_DIFF_HEADS = 4
DIFF_QK_WIDTH = N_DIFF_HEADS * 2 * HEAD_DIM
DIFF_V_WIDTH = N_DIFF_HEADS * 2 * HEAD_DIM
N_DIL_HEADS = 8
DIL_WIDTH = N_DIL_HEADS * HEAD_DIM
MIX_WIDTH = DIFF_V_WIDTH + DIL_WIDTH
PROJ_WIDTH = 2 * DIFF_QK_WIDTH + DIFF_V_WIDTH + 3 * DIL_WIDTH
DIL_PATTERNS = ((128, 1), (512, 4), (2048, 16))
Q_BLOCK = 128
D_FF = 2816
ROPE_THETA = 10000.0
EPS = 1e-6
SUBLN_EPS = 1e-5
LAMBDA_STD = 0.1
NEG_INF = -1e30

kernel_name = 'hybrid_diffattn_dilated_macaron'


def rmsnorm(x, g, eps=EPS):
    xf = x.astype(jnp.float32)
    y = xf * lax.rsqrt(jnp.mean(xf * xf, axis=-1, keepdims=True) + eps)
    return (y * g.astype(jnp.float32)).astype(x.dtype)


def swiglu(x, w_gate, w_up, w_down):
    return (jax.nn.silu(x @ w_gate) * (x @ w_up)) @ w_down


def rope_tables(positions):
    inv = 1.0 / (ROPE_THETA ** (jnp.arange(0, HEAD_DIM, 2, dtype=jnp.float32) / HEAD_DIM))
    ang = positions.astype(jnp.float32)[:, None] * inv[None, :]
    return jnp.cos(ang), jnp.sin(ang)


def apply_rope(t, cos, sin):
    t1, t2 = jnp.split(t.astype(jnp.float32), 2, axis=-1)
    c = cos[None, :, None, :]
    s = sin[None, :, None, :]
    return jnp.concatenate([t1 * c - t2 * s, t1 * s + t2 * c], axis=-1).astype(t.dtype)


def diff_attention(q, k, v, lam):
    B, S, H, _, E = q.shape
    nb = S // Q_BLOCK
    scale = E ** -0.5
    vf = v.astype(jnp.float32)
    kpos = jnp.arange(S)
    qb = q.reshape(B, nb, Q_BLOCK, H, 2, E).transpose(1, 0, 2, 3, 4, 5)

    def one_block(args):
        qblk, b = args
        s = jnp.einsum('bqhcd,bkhcd->bhcqk', qblk, k, preferred_element_type=jnp.float32) * scale
        qpos = b * Q_BLOCK + jnp.arange(Q_BLOCK)
        causal = kpos[None, :] <= qpos[:, None]
        s = jnp.where(causal[None, None, None], s, NEG_INF)
        p = jax.nn.softmax(s, axis=-1)
        a = p[:, :, 0] - lam * p[:, :, 1]
        return jnp.einsum('bhqk,bkhe->bqhe', a, vf)

    out = lax.map(one_block, (qb, jnp.arange(nb)))
    return out.transpose(1, 0, 2, 3, 4).reshape(B, S, H, 2 * E)


def dilated_branch(q, k, v, window, dilation):
    B, S, H, E = q.shape
    n = window // dilation
    L = S // dilation
    Lp = -(-L // n) * n
    nb = Lp // n

    def to_blocks(t):
        t = t.reshape(B, L, dilation, H, E).transpose(0, 2, 1, 3, 4)
        t = jnp.pad(t, ((0, 0), (0, 0), (0, Lp - L), (0, 0), (0, 0)))
        return t.reshape(B, dilation, nb, n, H, E)

    def band(t):
        prev = jnp.pad(t, ((0, 0), (0, 0), (1, 0), (0, 0), (0, 0), (0, 0)))[:, :, :-1]
        return jnp.concatenate([prev, t], axis=3)

    qb = to_blocks(q)
    kband = band(to_blocks(k))
    vband = band(to_blocks(v)).astype(jnp.float32)
    s = jnp.einsum('brnqhe,brnkhe->brnqhk', qb, kband, preferred_element_type=jnp.float32) * (E ** -0.5)
    qi = jnp.arange(n)[:, None] + n
    ki = jnp.arange(2 * n)[None, :]
    dist = qi - ki
    in_band = (dist >= 0) & (dist <= n)
    has_prev = (jnp.arange(nb) > 0)[:, None, None] | (ki >= n)[None]
    mask = in_band[None] & has_prev
    s = jnp.where(mask[None, None, :, :, None, :], s, NEG_INF)
    m = jnp.max(s, axis=-1)
    p = jnp.exp(s - m[..., None])
    den = jnp.sum(p, axis=-1)
    o = jnp.einsum('brnqhk,brnkhe->brnqhe', p, vband) / den[..., None]

    def from_blocks(t):
        t = t.reshape(B, dilation, Lp, H, -1)[:, :, :L]
        return t.transpose(0, 2, 1, 3, 4).reshape(B, S, H, -1)

    return from_blocks(o), from_blocks(m[..., None])[..., 0], from_blocks(den[..., None])[..., 0]


def dilated_mixture(q, k, v):
    outs = [dilated_branch(q, k, v, w, d) for (w, d) in DIL_PATTERNS]
    m_all = jnp.max(jnp.stack([m for (_, m, _) in outs], axis=0), axis=0)
    wts = [den * jnp.exp(m - m_all) for (_, m, den) in outs]
    num = sum(w[..., None] * o for w, (o, _, _) in zip(wts, outs))
    return num / sum(wts)[..., None]


def mixer(h, cos, sin, w_in, w_out, lq1, lk1, lq2, lk2, subln_gain, dil_gain, lam_init):
    B, S, _ = h.shape
    proj = h @ w_in
    o1 = DIFF_QK_WIDTH
    o2 = o1 + DIFF_QK_WIDTH
    o3 = o2 + DIFF_V_WIDTH
    o4 = o3 + DIL_WIDTH
    o5 = o4 + DIL_WIDTH
    dq, dk, dv, aq, ak, av = jnp.split(proj, [o1, o2, o3, o4, o5], axis=-1)

    dq = apply_rope(dq.reshape(B, S, 2 * N_DIFF_HEADS, HEAD_DIM), cos, sin).reshape(B, S, N_DIFF_HEADS, 2, HEAD_DIM)
    dk = apply_rope(dk.reshape(B, S, 2 * N_DIFF_HEADS, HEAD_DIM), cos, sin).reshape(B, S, N_DIFF_HEADS, 2, HEAD_DIM)
    dv = dv.reshape(B, S, N_DIFF_HEADS, 2 * HEAD_DIM)
    lam = (jnp.exp(jnp.sum(lq1.astype(jnp.float32) * lk1.astype(jnp.float32)))
           - jnp.exp(jnp.sum(lq2.astype(jnp.float32) * lk2.astype(jnp.float32))) + lam_init)
    d_out = diff_attention(dq, dk, dv, lam)
    d_out = rmsnorm(d_out, subln_gain, SUBLN_EPS) * (1.0 - lam_init)
    d_out = d_out.reshape(B, S, DIFF_V_WIDTH)

    aq = apply_rope(aq.reshape(B, S, N_DIL_HEADS, HEAD_DIM), cos, sin)
    ak = apply_rope(ak.reshape(B, S, N_DIL_HEADS, HEAD_DIM), cos, sin)
    av = av.reshape(B, S, N_DIL_HEADS, HEAD_DIM)
    a_out = dilated_mixture(aq, ak, av).reshape(B, S, DIL_WIDTH)
    a_out = rmsnorm(a_out, dil_gain)

    merged = jnp.concatenate([d_out, a_out], axis=-1).astype(h.dtype)
    return merged @ w_out


def setup_inputs(seed: int = 0) -> dict:
    key = jax.random.key(seed)
    ks = jax.random.split(key, 24)
    f32 = jnp.float32

    def nrm(k, shape, scale):
        return jax.random.normal(k, shape, f32) * scale

    def gain(k, shape):
        return 1.0 + 0.02 * jax.random.normal(k, shape, f32)

    return {
        'x': jax.random.normal(ks[0], (BATCH, SEQ, D_MODEL), f32),
        'positions': jnp.arange(SEQ, dtype=jnp.int32),
        'ffn1_norm': gain(ks[1], (DEPTH, D_MODEL)),
        'ffn1_gate': nrm(ks[2], (DEPTH, D_MODEL, D_FF), D_MODEL ** -0.5),
        'ffn1_up': nrm(ks[3], (DEPTH, D_MODEL, D_FF), D_MODEL ** -0.5),
        'ffn1_down': nrm(ks[4], (DEPTH, D_FF, D_MODEL), D_FF ** -0.5),
        'mix_norm': gain(ks[5], (DEPTH, D_MODEL)),
        'w_in': nrm(ks[6], (DEPTH, D_MODEL, PROJ_WIDTH), D_MODEL ** -0.5),
        'lambda_q1': nrm(ks[7], (DEPTH, HEAD_DIM), LAMBDA_STD),
        'lambda_k1': nrm(ks[8], (DEPTH, HEAD_DIM), LAMBDA_STD),
        'lambda_q2': nrm(ks[9], (DEPTH, HEAD_DIM), LAMBDA_STD),
        'lambda_k2': nrm(ks[10], (DEPTH, HEAD_DIM), LAMBDA_STD),
        'subln_gain': gain(ks[11], (DEPTH, 2 * HEAD_DIM)),
        'dil_gain': gain(ks[12], (DEPTH, DIL_WIDTH)),
        'w_out': nrm(ks[13], (DEPTH, MIX_WIDTH, D_MODEL), MIX_WIDTH ** -0.5),
        'ffn2_norm': gain(ks[14], (DEPTH, D_MODEL)),
        'ffn2_gate': nrm(ks[15], (DEPTH, D_MODEL, D_FF), D_MODEL ** -0.5),
        'ffn2_up': nrm(ks[16], (DEPTH, D_MODEL, D_FF), D_MODEL ** -0.5),
        'ffn2_down': nrm(ks[17], (DEPTH, D_FF, D_MODEL), D_FF ** -0.5),
        'final_norm': gain(ks[18], (D_MODEL,)),
    }


def reference(x, positions, ffn1_norm, ffn1_gate, ffn1_up, ffn1_down, mix_norm, w_in,
              lambda_q1, lambda_k1, lambda_q2, lambda_k2, subln_gain, dil_gain, w_out,
              ffn2_norm, ffn2_gate, ffn2_up, ffn2_down, final_norm):
    cos, sin = rope_tables(positions)
    for l in range(DEPTH):
        lam_init = 0.8 - 0.6 * math.exp(-0.3 * l)
        x = x + 0.5 * swiglu(rmsnorm(x, ffn1_norm[l]), ffn1_gate[l], ffn1_up[l], ffn1_down[l])
        x = x + mixer(rmsnorm(x, mix_norm[l]), cos, sin, w_in[l], w_out[l],
                      lambda_q1[l], lambda_k1[l], lambda_q2[l], lambda_k2[l],
                      subln_gain[l], dil_gain[l], lam_init)
        x = x + 0.5 * swiglu(rmsnorm(x, ffn2_norm[l]), ffn2_gate[l], ffn2_up[l], ffn2_down[l])
    return rmsnorm(x, final_norm)
```

```python
import contextlib
import math
import numpy as np
import concourse.bass as bass
import concourse.mybir as mybir
from concourse.bass_utils import run_bass_kernel_spmd

F32 = mybir.dt.float32
BF16 = mybir.dt.bfloat16
I32 = mybir.dt.int32
AF = mybir.ActivationFunctionType
ALU = mybir.AluOpType
AX = mybir.AxisListType

D = 1024
DFF = 2816
NFC = DFF // 128
DEPTH = 4
PROJ = 3072
EPS = 1e-6
SUBLN_EPS = 1e-5
TT = 512


class Buf:
    __slots__ = ("name", "w", "r")

    def __init__(self, name):
        self.name = name
        self.w = None
        self.r = {}


class KB:
    def __init__(self, nc):
        self.nc = nc
        self.eng = {"pe": nc.tensor, "act": nc.scalar, "dve": nc.vector, "pool": nc.gpsimd, "sp": nc.sync}
        self.psem = {}
        self.cnt = {}
        self.seen = {e: {} for e in self.eng}
        self.sems = []
        for e in ("pe", "act", "dve", "pool"):
            self.psem[e] = self.new_sem("prog_" + e)
            self.cnt[e] = 0
        self.named = {}
        self.unsig = {e: False for e in self.eng}
        self.dcnt = {}
        self.dma_sems = []
        self.n_inst = 0

    def new_sem(self, name):
        cm = self.nc.semaphore(name)
        s = cm.__enter__()
        self.sems.append(cm)
        return s

    def dma_sem(self, name):
        if name in self.named:
            return self.named[name]
        s = self.new_sem("d_" + name)
        self.dcnt[id(s)] = 0
        self.dma_sems.append(s)
        self.named[name] = s
        return s

    def group(self, bufs):
        last = None
        for b in bufs:
            if b.w is not None and (last is None or b.w[1] > last[1]):
                last = b.w
        for b in bufs:
            b.w = last

    def _wait(self, e, tok):
        if tok is None:
            return
        sem, val, src = tok
        if src == e and e == "pe":
            return
        seen = self.seen[e]
        if seen.get(id(sem), 0) >= val:
            return
        seen[id(sem)] = val
        self.eng[e].wait_ge(sem, val)
        self.n_inst += 1

    def _deps(self, e, reads, writes):
        for b in reads:
            self._wait(e, b.w)
        for b in writes:
            self._wait(e, b.w)
            for t in b.r.values():
                self._wait(e, t)

    def _mark(self, tok, reads, writes):
        for b in reads:
            b.r[id(tok[0])] = tok
        for b in writes:
            b.w = tok
            b.r = {}

    def op(self, e, fn, reads=(), writes=(), sig=True):
        self._deps(e, reads, writes)
        ins = fn(self.eng[e])
        if sig:
            self.cnt[e] += 1
            ins.then_inc(self.psem[e], 1)
            tok = (self.psem[e], self.cnt[e], e)
            self.unsig[e] = False
        else:
            tok = (self.psem[e], self.cnt[e] + 1, e)
            self.unsig[e] = True
        self._mark(tok, reads, writes)
        self.n_inst += 1
        return tok

    def dma(self, e, sem, out, in_, reads=(), writes=(), **kw):
        self._deps(e, reads, writes)
        ins = self.eng[e].dma_start(out=out, in_=in_, **kw)
        self.dcnt[id(sem)] += 16
        ins.then_inc(sem, 16)
        tok = (sem, self.dcnt[id(sem)], "dma")
        self._mark(tok, reads, writes)
        self.n_inst += 1
        return tok

    def barrier(self):
        assert not any(self.unsig.values()), self.unsig
        toks = [(self.psem[e], self.cnt[e], e) for e in self.psem if self.cnt[e] > 0]
        toks += [(s, self.dcnt[id(s)], "dma") for s in self.dma_sems if self.dcnt[id(s)] > 0]
        for e in self.eng:
            for t in toks:
                if t[2] == e:
                    sem, val, _ = t
                    if self.seen[e].get(id(sem), 0) < val:
                        self.seen[e][id(sem)] = val
                        self.eng[e].wait_ge(sem, val)
                else:
                    self._wait(e, t)

    def finish(self):
        self.barrier()


class Ctx:
    pass


def build_nc(S, layers, n_layers_total=DEPTH, first=True, last=True, phases=None):
    nc = bass.Bass("TRN2", target_bir_lowering=False)
    NT = S // TT
    c = Ctx()
    c.nc = nc
    c.S = S
    x_in = nc.dram_tensor("x", [S, D], F32, kind="ExternalInput").ap()
    y_out = nc.dram_tensor("y", [S, D], F32, kind="ExternalOutput").ap()
    L = n_layers_total
    w = {}
    for nm, shp in (("ffn1_norm", [L, D]), ("ffn1_gate", [L, D, DFF]), ("ffn1_up", [L, D, DFF]),
                    ("ffn1_down", [L, DFF, D]), ("ffn2_norm", [L, D]), ("ffn2_gate", [L, D, DFF]),
                    ("ffn2_up", [L, D, DFF]), ("ffn2_down", [L, DFF, D]), ("final_norm", [1, D])):
        w[nm] = nc.dram_tensor(nm, shp, F32, kind="ExternalInput").ap()
    for nm, shp in (("mix_norm", [L, D]), ("w_in", [L, D, PROJ]), ("lambda_q1", [L, 64]), ("lambda_k1", [L, 64]),
                    ("lambda_q2", [L, 64]), ("lambda_k2", [L, 64]), ("subln_gain", [L, 128]),
                    ("dil_gain", [L, 512]), ("w_out", [L, D, D])):
        w[nm] = nc.dram_tensor(nm, shp, F32, kind="ExternalInput").ap()
    positions = nc.dram_tensor("positions", [S], I32, kind="ExternalInput").ap()
    ident_d = nc.dram_tensor("ident", [128, 128], F32, kind="ExternalInput").ap()
    ropec_d = nc.dram_tensor("ropec", [128, 2], F32, kind="ExternalInput").ap()
    masks_d = nc.dram_tensor("masks", [128, 1536], F32, kind="ExternalInput").ap()
    eones_d = nc.dram_tensor("eones", [128, 256], F32, kind="ExternalInput").ap()
    prot_d = nc.dram_tensor("prot", [128, 128], F32, kind="ExternalInput").ap()
    xs = nc.dram_tensor("xs", [S, D], F32, kind="Internal").ap()
    cosT = nc.dram_tensor("cosT", [128, S], F32, kind="Internal").ap()
    sinT = nc.dram_tensor("sinT", [128, S], F32, kind="Internal").ap()
    qkT = nc.dram_tensor("qkT", [16, 128, S], BF16, kind="Internal").ap()
    vd = nc.dram_tensor("vd", [S, 512], BF16, kind="Internal").ap()
    va = nc.dram_tensor("va", [S, 4, 2, 128], BF16, kind="Internal").ap()
    mT = nc.dram_tensor("mT", [4, 128, S], BF16, kind="Internal").ap()
    aT = nc.dram_tensor("aT", [4, 128, S], F32, kind="Internal").ap()
    b_cs, b_qkT, b_vd, b_va, b_mT, b_aT = (Buf(n) for n in ("cs", "qkT", "vd", "va", "mT", "aT"))

    kb = KB(nc)
    c.kb = kb

    ident = nc.alloc_sbuf_tensor("ident_b", [128, 128], BF16)
    ropec = nc.alloc_sbuf_tensor("ropec_sb", [128, 2], F32)
    negm = nc.alloc_sbuf_tensor("negm", [128, 512], BF16)
    negd = nc.alloc_sbuf_tensor("negd", [128, 256], BF16)
    eones = nc.alloc_sbuf_tensor("eones_b", [128, 2, 128], BF16)
    ones_b = nc.alloc_sbuf_tensor("ones_b", [128, 128], BF16)
    prot = nc.alloc_sbuf_tensor("prot_b", [128, 128], BF16)
    b_ident, b_ropec, b_mask = Buf("ident"), Buf("ropec"), Buf("mask")
    s_const = kb.dma_sem("const")
    with (nc.sbuf_tensor("ident_f", [128, 128], F32) as ident_f,
          nc.sbuf_tensor("masks_f", [128, 1536], F32) as masks_f,
          nc.sbuf_tensor("eones_f", [128, 256], F32) as eones_f,
          nc.sbuf_tensor("prot_f", [128, 128], F32) as prot_f):
        kb.dma("sp", s_const, ident_f[:, :], ident_d[:, :], writes=[b_ident])
        kb.dma("sp", s_const, ropec[:, :], ropec_d[:, :], writes=[b_ropec])
        kb.dma("sp", s_const, masks_f[:, :], masks_d[:, :], writes=[b_mask])
        kb.dma("sp", s_const, eones_f[:, :], eones_d[:, :], writes=[b_mask])
        kb.dma("sp", s_const, prot_f[:, :], prot_d[:, :], writes=[b_mask])
        kb.group([b_ident, b_ropec, b_mask])
        kb.op("dve", lambda e: e.tensor_copy(ident[:, :], ident_f[:, :]), reads=[b_ident], writes=[b_ident])
        kb.op("dve", lambda e: e.tensor_copy(eones[:, :, :], eones_f[:, :].rearrange("p (k q) -> p k q", q=128)),
              reads=[b_mask], writes=[b_mask])
        kb.op("dve", lambda e: e.memset(ones_b[:, :], 1.0), reads=[b_mask], writes=[b_mask])
        kb.op("dve", lambda e: e.tensor_copy(prot[:, :], prot_f[:, :]), reads=[b_mask], writes=[b_mask])
        kb.op("dve", lambda e: e.tensor_copy(negm[:, :], masks_f[:, 768:1280]), reads=[b_mask], writes=[b_mask])
        kb.op("dve", lambda e: e.tensor_copy(negd[:, :], masks_f[:, 1280:1536]), reads=[b_mask], writes=[b_mask])
        kb.barrier()

    def rstd_ops(st, col, b_st):
        kb.op("dve", lambda e: e.tensor_scalar(out=st[:, 8 + col:9 + col], in0=st[:, col:col + 1],
                                               scalar1=1.0 / D, scalar2=EPS, op0=ALU.mult, op1=ALU.add),
              reads=[b_st], writes=[b_st])
        kb.op("act", lambda e: e.activation(out=st[:, 8 + col:9 + col], in_=st[:, 8 + col:9 + col], func=AF.Sqrt),
              reads=[b_st], writes=[b_st])
        kb.op("dve", lambda e: e.reciprocal(out=st[:, 8 + col:9 + col], in_=st[:, 8 + col:9 + col]),
              reads=[b_st], writes=[b_st])

    def ffn_phase(l, which, src, dst, final):
        kb.barrier()
        nm = "ffn%d" % which
        u = "_%d_%d" % (l, which)
        with (nc.sbuf_tensor("wg" + u, [128, 8, DFF], BF16) as wg,
              nc.sbuf_tensor("wu" + u, [128, 8, DFF], BF16) as wu,
              nc.sbuf_tensor("wd" + u, [128, NFC, D], BF16) as wd,
              nc.sbuf_tensor("gbc" + u, [128, D], F32) as gbc,
              nc.sbuf_tensor("gfin" + u, [128, D], F32) as gfin,
              nc.sbuf_tensor("xblk" + u, [128, 2, D], F32) as xblk,
              nc.sbuf_tensor("hb" + u, [128, 2, D], BF16) as hb,
              nc.sbuf_tensor("hT" + u, [128, 8, TT], BF16) as hT,
              nc.sbuf_tensor("actb" + u, [128, NFC, TT], BF16) as actb,
              nc.sbuf_tensor("sg" + u, [128, 2, TT], F32) as sg,
              nc.sbuf_tensor("xr" + u, [128, 2, D], F32) as xr,
              nc.sbuf_tensor("junk" + u, [128, D], BF16) as junk,
              nc.sbuf_tensor("st" + u, [128, 16], F32) as st,
              nc.psum_tensor("pT" + u, [128, 2, 8, 128], BF16) as pT,
              nc.psum_tensor("pg" + u, [128, 2, 512], F32) as pg,
              nc.psum_tensor("pu" + u, [128, 2, 512], F32) as pu,
              nc.psum_tensor("po" + u, [128, 2, 512], F32) as po):
            b_wg = [Buf("wg%d" % k) for k in range(8)]
            b_wu = [Buf("wu%d" % k) for k in range(8)]
            b_wd = [Buf("wd%d" % k) for k in range(NFC)]
            b_g = Buf("g")
            b_x = [Buf("x0"), Buf("x1")]
            s_x = [kb.dma_sem("x0"), kb.dma_sem("x1")]
            b_h = [Buf("h0"), Buf("h1")]
            b_hT4 = [Buf("hT%d" % b) for b in range(4)]
            b_act = [Buf("act%d" % f) for f in range(NFC)]
            b_sg = [Buf("sg0"), Buf("sg1")]
            b_xr = [Buf("xr0"), Buf("xr1")]
            s_xr = [kb.dma_sem("xr0"), kb.dma_sem("xr1")]
            s_xo = [kb.dma_sem("xo0"), kb.dma_sem("xo1")]
            b_st = Buf("st")
            b_junk = Buf("junk")
            b_pT = [Buf("pT0"), Buf("pT1")]
            b_pg = [Buf("pg0"), Buf("pg1")]
            b_pu = [Buf("pu0"), Buf("pu1")]
            b_po = [Buf("po0"), Buf("po1")]
            s_g = kb.dma_sem("g")
            b_gf = Buf("gf")
            kb.dma("sp", s_g, gbc[:, :], w[nm + "_norm"][l, :].partition_broadcast(128), writes=[b_g])
            if final:
                kb.dma("sp", s_g, gfin[:, :], w["final_norm"][0, :].partition_broadcast(128), writes=[b_gf])
                kb.group([b_g, b_gf])
            s_wgs, s_wus, s_wds = kb.dma_sem("wgs"), kb.dma_sem("wus"), kb.dma_sem("wds")
            for k in range(8):
                kb.dma("pool", s_wgs, wg[:, k, :], w[nm + "_gate"][l, k * 128:(k + 1) * 128, :], writes=[b_wg[k]])
            kb.group(b_wg)
            for k in range(8):
                kb.dma("pool", s_wus, wu[:, k, :], w[nm + "_up"][l, k * 128:(k + 1) * 128, :], writes=[b_wu[k]])
            kb.group(b_wu)
            for f in range(NFC):
                kb.dma("pool", s_wds, wd[:, f, :], w[nm + "_down"][l, f * 128:(f + 1) * 128, :], writes=[b_wd[f]])
            kb.group(b_wd)

            blk_ctr = [0]
            pend = {}

            def norm_block(i, b):
                n = blk_ctr[0]
                blk_ctr[0] += 1
                s = n % 2
                r0 = i * TT + b * 128
                kb.dma("sp", s_x[s], xblk[:, s, :], src[r0:r0 + 128, :], writes=[b_x[s]])
                col = (n % 8)
                kb.op("act", lambda e: e.activation(out=junk[:, :], in_=xblk[:, s, :], func=AF.Square,
                                                    accum_out=st[:, col:col + 1]),
                      reads=[b_x[s]], writes=[b_junk, b_st])
                rstd_ops(st, col, b_st)
                kb.op("dve", lambda e: e.scalar_tensor_tensor(out=hb[:, s, :], in0=xblk[:, s, :],
                                                              scalar=st[:, 8 + col:9 + col], in1=gbc[:, :],
                                                              op0=ALU.mult, op1=ALU.mult),
                      reads=[b_x[s], b_st, b_g], writes=[b_h[s]])
                pend[(i, b)] = s

            def norm_post(i, b):
                s = pend.pop((i, b))
                for k in range(8):
                    kb.op("pe", lambda e, k=k: e.transpose(out=pT[:, s, k, :], in_=hb[:, s, k * 128:(k + 1) * 128],
                                                           identity=ident[:, :]),
                          reads=[b_h[s], b_ident], writes=[b_pT[s]], sig=(k == 7))
                kb.op("act", lambda e: e.copy(out=hT[:, :, b * 128:(b + 1) * 128], in_=pT[:, s, :, :]),
                      reads=[b_pT[s]], writes=[b_hT4[b]])

            gu_ctr = [0]

            def gate_up(fc):
                n = gu_ctr[0]
                gu_ctr[0] += 1
                s = n % 2
                for k in range(8):
                    kb.op("pe", lambda e, k=k: e.matmul(pg[:, s, :], lhsT=wg[:, k, fc * 128:(fc + 1) * 128],
                                                        rhs=hT[:, k, :], start=(k == 0), stop=(k == 7)),
                          reads=[b_wg[k]] + b_hT4, writes=[b_pg[s]], sig=(k == 7))
                for k in range(8):
                    kb.op("pe", lambda e, k=k: e.matmul(pu[:, s, :], lhsT=wu[:, k, fc * 128:(fc + 1) * 128],
                                                        rhs=hT[:, k, :], start=(k == 0), stop=(k == 7)),
                          reads=[b_wu[k]] + b_hT4, writes=[b_pu[s]], sig=(k == 7))
                kb.op("act", lambda e: e.activation(out=sg[:, s, :], in_=pg[:, s, :], func=AF.Silu),
                      reads=[b_pg[s]], writes=[b_sg[s]])
                kb.op("dve", lambda e: e.tensor_tensor(out=actb[:, fc, :], in0=pu[:, s, :], in1=sg[:, s, :],
                                                       op=ALU.mult),
                      reads=[b_pu[s], b_sg[s]], writes=[b_act[fc]])

            dn_ctr = [0]

            def down_block(i, b):
                n = dn_ctr[0]
                dn_ctr[0] += 1
                s = n % 2
                r0 = i * TT + b * 128
                kb.dma("sp", s_xr[s], xr[:, s, :], src[r0:r0 + 128, :], reads=[], writes=[b_xr[s]])
                for half in range(2):
                    for f in range(NFC):
                        kb.op("pe", lambda e, f=f: e.matmul(po[:, half, :], lhsT=actb[:, f, b * 128:(b + 1) * 128],
                                                            rhs=wd[:, f, half * 512:(half + 1) * 512],
                                                            start=(f == 0), stop=(f == NFC - 1)),
                              reads=[b_act[f], b_wd[f]], writes=[b_po[half]], sig=(f == NFC - 1))
                    kb.op("dve", lambda e: e.scalar_tensor_tensor(out=xr[:, s, half * 512:(half + 1) * 512],
                                                                  in0=po[:, half, :], scalar=0.5,
                                                                  in1=xr[:, s, half * 512:(half + 1) * 512],
                                                                  op0=ALU.mult, op1=ALU.add),
                          reads=[b_po[half], b_xr[s]], writes=[b_xr[s]])
                if final:
                    col = 4 + (n % 4)
                    kb.op("act", lambda e: e.activation(out=junk[:, :], in_=xr[:, s, :], func=AF.Square,
                                                        accum_out=st[:, col:col + 1]),
                          reads=[b_xr[s]], writes=[b_junk, b_st])
                    rstd_ops(st, col, b_st)
                    kb.op("dve", lambda e: e.scalar_tensor_tensor(out=xr[:, s, :], in0=xr[:, s, :],
                                                                  scalar=st[:, 8 + col:9 + col], in1=gfin[:, :],
                                                                  op0=ALU.mult, op1=ALU.mult),
                          reads=[b_xr[s], b_st, b_gf], writes=[b_xr[s]])
                kb.dma("sp", s_xo[s], dst[r0:r0 + 128, :], xr[:, s, :], reads=[b_xr[s]], writes=[])

            for b in range(4):
                norm_block(0, b)
                norm_post(0, b)
            for i in range(NT):
                for fc in range(NFC):
                    gate_up(fc)
                    if i + 1 < NT and fc == NFC - 2:
                        norm_block(i + 1, 0)
                for b in range(4):
                    if i + 1 < NT and b + 1 < 4:
                        norm_block(i + 1, b + 1)
                    down_block(i, b)
                    if i + 1 < NT:
                        norm_post(i + 1, b)
            kb.barrier()


    NSPAN = max(1, S // 2048)
    SPAN = min(S, 2048)

    def rope_tables():
        kb.barrier()
        CH = 2048 if S >= 2048 else S
        TWO_PI = 2.0 * math.pi
        C1 = 6.28125
        C2 = TWO_PI - C1
        PI_LO = 3.1415925
        MAGIC = 12582912.0
        with (nc.sbuf_tensor("rp_pos", [128, CH], I32) as pos_i,
              nc.sbuf_tensor("rp_ang", [128, CH], F32) as ang,
              nc.sbuf_tensor("rp_k", [128, CH], F32) as kk,
              nc.sbuf_tensor("rp_r", [128, CH], F32) as rr,
              nc.sbuf_tensor("rp_r2", [128, CH], F32) as r2,
              nc.sbuf_tensor("rp_o", [128, 2, CH], F32) as oo):
            b_pos, b_ang, b_k, b_r, b_r2, b_o = (Buf(n) for n in ("pos", "ang", "k", "r", "r2", "o"))
            s_pos = kb.dma_sem("rp_pos")
            s_o = kb.dma_sem("rp_o")
            for ci in range(S // CH):
                t0 = ci * CH
                kb.dma("sp", s_pos, pos_i[:, :], positions[t0:t0 + CH].partition_broadcast(128), writes=[b_pos])
                kb.op("dve", lambda e: e.tensor_copy(ang[:, :], pos_i[:, :]), reads=[b_pos], writes=[b_ang])
                kb.op("dve", lambda e: e.tensor_scalar_mul(out=ang[:, :], in0=ang[:, :], scalar1=ropec[:, 0:1]),
                      reads=[b_ang, b_ropec], writes=[b_ang])
                kb.op("dve", lambda e: e.tensor_scalar(out=kk[:, :], in0=ang[:, :], scalar1=1.0 / TWO_PI,
                                                       scalar2=MAGIC, op0=ALU.mult, op1=ALU.add),
                      reads=[b_ang], writes=[b_k])
                kb.op("dve", lambda e: e.tensor_scalar_add(out=kk[:, :], in0=kk[:, :], scalar1=-MAGIC),
                      reads=[b_k], writes=[b_k])
                kb.op("dve", lambda e: e.scalar_tensor_tensor(out=rr[:, :], in0=kk[:, :], scalar=-C1, in1=ang[:, :],
                                                              op0=ALU.mult, op1=ALU.add),
                      reads=[b_k, b_ang], writes=[b_r])
                kb.op("dve", lambda e: e.scalar_tensor_tensor(out=rr[:, :], in0=kk[:, :], scalar=-C2, in1=rr[:, :],
                                                              op0=ALU.mult, op1=ALU.add),
                      reads=[b_k, b_r], writes=[b_r])
                kb.op("dve", lambda e: e.tensor_scalar(out=r2[:, :], in0=rr[:, :], scalar1=math.pi / 2,
                                                       scalar2=-TWO_PI, op0=ALU.is_gt, op1=ALU.mult),
                      reads=[b_r], writes=[b_r2])
                kb.op("dve", lambda e: e.scalar_tensor_tensor(out=r2[:, :], in0=rr[:, :], scalar=math.pi / 2,
                                                              in1=r2[:, :], op0=ALU.add, op1=ALU.add),
                      reads=[b_r, b_r2], writes=[b_r2])
                kb.op("dve", lambda e: e.tensor_scalar(out=rr[:, :], in0=rr[:, :], scalar1=PI_LO, scalar2=-PI_LO,
                                                       op0=ALU.min, op1=ALU.max), reads=[b_r], writes=[b_r])
                kb.op("dve", lambda e: e.tensor_scalar(out=r2[:, :], in0=r2[:, :], scalar1=PI_LO, scalar2=-PI_LO,
                                                       op0=ALU.min, op1=ALU.max), reads=[b_r2], writes=[b_r2])
                kb.op("act", lambda e: e.activation(out=oo[:, 0, :], in_=r2[:, :], func=AF.Sin),
                      reads=[b_r2], writes=[b_o])
                kb.op("act", lambda e: e.activation(out=oo[:, 1, :], in_=rr[:, :], func=AF.Sin,
                                                    scale=ropec[:, 1:2]),
                      reads=[b_r, b_ropec], writes=[b_o])
                kb.dma("sp", s_o, cosT[:, t0:t0 + CH], oo[:, 0, :], reads=[b_o], writes=[b_cs])
                kb.dma("sp", s_o, sinT[:, t0:t0 + CH], oo[:, 1, :], reads=[b_o], writes=[b_cs])
                kb.group([b_cs])
        kb.barrier()

    def proj_phase(l, src):
        kb.barrier()
        u = "_m1_%d" % l
        with (nc.sbuf_tensor("win" + u, [128, 8, PROJ], BF16) as win,
              nc.sbuf_tensor("qb" + u, [128, 2, TT], BF16) as qb,
              nc.sbuf_tensor("gbc" + u, [128, D], F32) as gbc,
              nc.sbuf_tensor("xblk" + u, [128, 2, D], F32) as xblk,
              nc.sbuf_tensor("hb" + u, [128, 2, D], BF16) as hb,
              nc.sbuf_tensor("hT" + u, [128, 8, TT], BF16) as hT,
              nc.sbuf_tensor("junk" + u, [128, D], BF16) as junk,
              nc.sbuf_tensor("st" + u, [128, 16], F32) as st,
              nc.sbuf_tensor("cs" + u, [128, 2, 2, TT], F32) as cs,
              nc.sbuf_tensor("t1" + u, [128, 2, TT], F32) as t1,
              nc.sbuf_tensor("t2" + u, [128, 2, TT], F32) as t2,
              nc.sbuf_tensor("qks" + u, [128, 2, 16, TT], BF16) as qks,
              nc.sbuf_tensor("vds" + u, [128, 2, 4, 512], BF16) as vds,
              nc.sbuf_tensor("vas" + u, [128, 4, 4, 2, 128], BF16) as vas,
              nc.psum_tensor("pT" + u, [128, 2, 8, 128], BF16) as pT,
              nc.psum_tensor("pq" + u, [128, 2, 512], F32) as pq,
              nc.psum_tensor("pr" + u, [128, 2, 512], F32) as pr,
              nc.psum_tensor("pv" + u, [128, 2, 512], F32) as pv):
            b_win = [Buf("win%d" % k) for k in range(8)]
            b_g = Buf("g")
            b_x = [Buf("x0"), Buf("x1")]
            s_x = [kb.dma_sem("x0"), kb.dma_sem("x1")]
            b_h = [Buf("h0"), Buf("h1")]
            b_hT4 = [Buf("hT%d" % b) for b in range(4)]
            b_st = Buf("st")
            b_junk = Buf("junk")
            b_pT = [Buf("pT0"), Buf("pT1")]
            b_pq = [Buf("pq0"), Buf("pq1")]
            b_pr = [Buf("pr0"), Buf("pr1")]
            b_pv = [Buf("pv0"), Buf("pv1")]
            b_csb = [Buf("cs0"), Buf("cs1")]
            s_cs = [kb.dma_sem("cs0"), kb.dma_sem("cs1")]
            b_t1 = [Buf("t10"), Buf("t11")]
            b_t2 = [Buf("t20"), Buf("t21")]
            b_qks = [Buf("qks0"), Buf("qks1")]
            s_qks = [kb.dma_sem("qks0"), kb.dma_sem("qks1")]
            b_vds = [Buf("vds0"), Buf("vds1")]
            s_vds = [kb.dma_sem("vds0"), kb.dma_sem("vds1")]
            b_vas = Buf("vas")
            s_vas = kb.dma_sem("vas")
            s_g = kb.dma_sem("g")
            kb.dma("sp", s_g, gbc[:, :], w["mix_norm"][l, :].partition_broadcast(128), writes=[b_g])
            s_wi = kb.dma_sem("wgs")
            for k in range(8):
                kb.dma("pool", s_wi, win[:, k, :], w["w_in"][l, k * 128:(k + 1) * 128, :], writes=[b_win[k]])
            kb.group(b_win)
            kb.op("pool", lambda e: e.memset(vas[:, :, :, :, :], 0.0), writes=[b_vas])
            blk_ctr = [0]
            pend = {}

            def norm_block(i, b):
                n = blk_ctr[0]
                blk_ctr[0] += 1
                s = n % 2
                r0 = i * TT + b * 128
                kb.dma("sp", s_x[s], xblk[:, s, :], src[r0:r0 + 128, :], writes=[b_x[s]])
                col = (n % 8)
                kb.op("act", lambda e: e.activation(out=junk[:, :], in_=xblk[:, s, :], func=AF.Square,
                                                    accum_out=st[:, col:col + 1]),
                      reads=[b_x[s]], writes=[b_junk, b_st])
                rstd_ops(st, col, b_st)
                kb.op("dve", lambda e: e.scalar_tensor_tensor(out=hb[:, s, :], in0=xblk[:, s, :],
                                                              scalar=st[:, 8 + col:9 + col], in1=gbc[:, :],
                                                              op0=ALU.mult, op1=ALU.mult),
                      reads=[b_x[s], b_st, b_g], writes=[b_h[s]])
                pend[(i, b)] = s

            def norm_post(i, b):
                s = pend.pop((i, b))
                for k in range(8):
                    kb.op("pe", lambda e, k=k: e.transpose(out=pT[:, s, k, :], in_=hb[:, s, k * 128:(k + 1) * 128],
                                                           identity=ident[:, :]),
                          reads=[b_h[s], b_ident], writes=[b_pT[s]], sig=(k == 7))
                kb.op("act", lambda e: e.copy(out=hT[:, :, b * 128:(b + 1) * 128], in_=pT[:, s, :, :]),
                      reads=[b_pT[s]], writes=[b_hT4[b]])

            b_qb = [Buf("qb0"), Buf("qb1")]

            def qk_mm(i, ci):
                s = ci % 2
                c0 = ci * 128 if ci < 8 else 1536 + (ci - 8) * 128
                for k in range(8):
                    kb.op("pe", lambda e, k=k: e.matmul(pq[:, s, :], lhsT=win[:, k, c0:c0 + 128], rhs=hT[:, k, :],
                                                        start=(k == 0), stop=(k == 7)),
                          reads=[b_win[k]] + b_hT4, writes=[b_pq[s]], sig=(k == 7))
                kb.op("act", lambda e: e.copy(out=qb[:, s, :], in_=pq[:, s, :]), reads=[b_pq[s]], writes=[b_qb[s]])

            def qk_rot(i, ci, ts):
                s = ci % 2
                kb.op("pe", lambda e: e.matmul(pr[:, s, :], lhsT=prot[:, :], rhs=qb[:, s, :], start=True, stop=True),
                      reads=[b_qb[s], b_mask], writes=[b_pr[s]])
                kb.op("dve", lambda e: e.tensor_tensor(out=t1[:, s, :], in0=pq[:, s, :], in1=cs[:, ts, 0, :],
                                                       op=ALU.mult),
                      reads=[b_pq[s], b_csb[ts], b_qb[s]], writes=[b_t1[s]])
                kb.op("dve", lambda e: e.tensor_tensor(out=t2[:, s, :], in0=pr[:, s, :], in1=cs[:, ts, 1, :],
                                                       op=ALU.mult),
                      reads=[b_pr[s], b_csb[ts]], writes=[b_t2[s]])
                kb.op("dve", lambda e: e.tensor_tensor(out=qks[:, ts, ci, :], in0=t1[:, s, :], in1=t2[:, s, :],
                                                       op=ALU.add),
                      reads=[b_t1[s], b_t2[s]], writes=[b_qks[ts]])

            vc = [0]

            def v_block(i, b, ts):
                r0 = i * TT + b * 128
                s = vc[0] % 2
                vc[0] += 1
                for k in range(8):
                    kb.op("pe", lambda e, k=k: e.matmul(pv[:, s, :], lhsT=hT[:, k, b * 128:(b + 1) * 128],
                                                        rhs=win[:, k, 1024:1536], start=(k == 0), stop=(k == 7)),
                          reads=[b_win[k], b_hT4[b]], writes=[b_pv[s]], sig=(k == 7))
                kb.op("act", lambda e: e.copy(out=vds[:, ts, b, :], in_=pv[:, s, :]),
                      reads=[b_pv[s]], writes=[b_vds[ts]])
                s = vc[0] % 2
                vc[0] += 1
                for k in range(8):
                    kb.op("pe", lambda e, k=k: e.matmul(pv[:, s, :], lhsT=hT[:, k, b * 128:(b + 1) * 128],
                                                        rhs=win[:, k, 2560:3072], start=(k == 0), stop=(k == 7)),
                          reads=[b_win[k], b_hT4[b]], writes=[b_pv[s]], sig=(k == 7))
                pvv = pv[:, s, :].rearrange("p (j t d) -> p j t d", t=2, d=64)
                kb.op("act", lambda e: e.copy(out=vas[:, b, :, 0, 0:64], in_=pvv[:, :, 0, :]),
                      reads=[b_pv[s]], writes=[b_vas])
                kb.op("act", lambda e: e.copy(out=vas[:, b, :, 1, 64:128], in_=pvv[:, :, 1, :]),
                      reads=[b_pv[s]], writes=[b_vas])

            for b in range(4):
                norm_block(0, b)
                norm_post(0, b)
            def csload(i):
                ts = i % 2
                c0 = i * TT
                kb.dma("sp", s_cs[ts], cs[:, ts, 0, :], cosT[:, c0:c0 + TT], reads=[b_cs], writes=[b_csb[ts]])
                kb.dma("sp", s_cs[ts], cs[:, ts, 1, :], sinT[:, c0:c0 + TT], reads=[b_cs], writes=[b_csb[ts]])
                kb.group([b_csb[ts]])

            csload(0)
            for i in range(NT):
                ts = i % 2
                c0 = i * TT
                if i + 1 < NT:
                    csload(i + 1)
                qk_mm(i, 0)
                for ci in range(16):
                    if ci + 1 < 16:
                        qk_mm(i, ci + 1)
                    if ci < 15:
                        qk_rot(i, ci, ts)
                if i + 1 < NT:
                    norm_block(i + 1, 0)
                for b in range(4):
                    if i + 1 < NT and b + 1 < 4:
                        norm_block(i + 1, b + 1)
                    v_block(i, b, ts)
                    if b == 0:
                        qk_rot(i, 15, ts)
                        kb.dma("sp", s_qks[ts], qkT[:, :, c0:c0 + TT].rearrange("c p t -> p c t"), qks[:, ts, :, :],
                               reads=[b_qks[ts]], writes=[b_qkT])
                    if i + 1 < NT:
                        norm_post(i + 1, b)
                kb.dma("sp", s_vds[ts], vd[c0:c0 + TT, :].rearrange("(b p) e -> p b e", p=128), vds[:, ts, :, :],
                       reads=[b_vds[ts]], writes=[b_vd])
                kb.dma("sp", s_vas, va[c0:c0 + TT, :, :, :].rearrange("(b p) j t e -> p b (j t e)", p=128),
                       vas[:, :, :, :, :].rearrange("p b j t e -> p b (j t e)"),
                       reads=[b_vas], writes=[b_va])
            kb.barrier()

    def lam_ops(l, lam_init, lamt, b_lam):
        u = "_lam_%d" % l
        with (nc.sbuf_tensor("lv" + u, [128, 4, 64], F32) as lv,
              nc.sbuf_tensor("lt" + u, [128, 8], F32) as lt):
            b_lv = Buf("lv")
            s_lv = kb.dma_sem("lv")
            for i, nmv in enumerate(("lambda_q1", "lambda_k1", "lambda_q2", "lambda_k2")):
                kb.dma("sp", s_lv, lv[:, i, :], w[nmv][l, :].partition_broadcast(128), writes=[b_lv])
            kb.group([b_lv])
            kb.op("dve", lambda e: e.tensor_tensor(out=lv[:, 0, :], in0=lv[:, 0, :], in1=lv[:, 1, :], op=ALU.mult),
                  reads=[b_lv], writes=[b_lv])
            kb.op("dve", lambda e: e.tensor_tensor(out=lv[:, 2, :], in0=lv[:, 2, :], in1=lv[:, 3, :], op=ALU.mult),
                  reads=[b_lv], writes=[b_lv])
            kb.op("dve", lambda e: e.reduce_sum(out=lt[:, 0:1], in_=lv[:, 0, :], axis=AX.X), reads=[b_lv], writes=[b_lam])
            kb.op("dve", lambda e: e.reduce_sum(out=lt[:, 1:2], in_=lv[:, 2, :], axis=AX.X), reads=[b_lv], writes=[b_lam])
            kb.op("act", lambda e: e.activation(out=lt[:, 2:4], in_=lt[:, 0:2], func=AF.Exp), reads=[b_lam], writes=[b_lam])
            kb.op("dve", lambda e: e.scalar_tensor_tensor(out=lamt[:, 0:1], in0=lt[:, 3:4], scalar=-float(lam_init),
                                                          in1=lt[:, 2:3], op0=ALU.add, op1=ALU.subtract),
                  reads=[b_lam], writes=[b_lam])
            kb.barrier()

    def diff_phase(l, lam_init):
        kb.barrier()
        u = "_m2a_%d" % l
        NB = S // 128
        NQT = S // 512
        LOOK = 2
        with (nc.sbuf_tensor("kT" + u, [128, 2, S], BF16) as kT,
              nc.sbuf_tensor("v1" + u, [128, 2, NB, 130], BF16) as v1,
              nc.sbuf_tensor("qT" + u, [128, 2, 512], BF16) as qT,
              nc.sbuf_tensor("PT" + u, [128, 3, 2, 512], BF16) as PT,
              nc.sbuf_tensor("accs" + u, [128, 4, 3, 512], F32) as accs,
              nc.sbuf_tensor("lamt" + u, [128, 8], F32) as lamt,
              nc.sbuf_tensor("gsub" + u, [128, 128], F32) as gsub,
              nc.sbuf_tensor("fst" + u, [128, 4, 16], F32) as fst,
              nc.sbuf_tensor("fo" + u, [128, 4, 4, 128], F32) as fo,
              nc.sbuf_tensor("fj" + u, [128, 128], F32) as fj,
              nc.sbuf_tensor("fob" + u, [128, 4, 4, 128], BF16) as fob,
              nc.sbuf_tensor("otr" + u, [128, 2, 512], BF16) as otr,
              nc.psum_tensor("ps" + u, [128, 2, 2, 512], F32) as ps,
              nc.psum_tensor("acc" + u, [128, 3, 512], F32) as acc,
              nc.psum_tensor("ptr" + u, [128, 4, 128], BF16) as ptr):
            b_lam = Buf("lam")
            lam_ops(l, lam_init, lamt, b_lam)
            b_gs = Buf("gsub")
            s_gs = kb.dma_sem("g")
            kb.dma("sp", s_gs, gsub[:, :], w["subln_gain"][l, :].partition_broadcast(128), writes=[b_gs])
            kb.op("dve", lambda e: e.tensor_scalar_mul(out=gsub[:, :], in0=gsub[:, :], scalar1=float(1.0 - lam_init)),
                  reads=[b_gs], writes=[b_gs])
            b_kT = [Buf("kT0"), Buf("kT1")]
            s_kT = [kb.dma_sem("kT0"), kb.dma_sem("kT1")]
            b_v1 = [Buf("v10"), Buf("v11")]
            s_v1 = [kb.dma_sem("v10"), kb.dma_sem("v11")]
            b_qT = [Buf("qT0"), Buf("qT1")]
            s_qT = [kb.dma_sem("qT0"), kb.dma_sem("qT1")]
            b_ps = [Buf("ps0"), Buf("ps1")]
            b_PT = [Buf("PT0"), Buf("PT1"), Buf("PT2")]
            b_acc = Buf("acc")
            b_accs = [Buf("accs%d" % i) for i in range(4)]
            b_fst = [Buf("fst%d" % i) for i in range(4)]
            b_fj, b_ptr = Buf("fj"), Buf("ptr")
            b_fo = [Buf("fo%d" % i) for i in range(4)]
            b_fob = [Buf("fob%d" % i) for i in range(4)]
            b_otr = [Buf("otr0"), Buf("otr1")]
            s_otr = [kb.dma_sem("otr0"), kb.dma_sem("otr1")]
            for hs in range(2):
                kb.op("pool", lambda e, hs=hs: e.memset(v1[:, hs, :, 128:130], 1.0), writes=[b_v1[hs]])

            def acc_view(a):
                return acc[:, a // 3, (a % 3) * 160:(a % 3) * 160 + 129]

            def accs_view(sl, a, lo, hi):
                return accs[:, sl, a // 3, (a % 3) * 160 + lo:(a % 3) * 160 + hi]

            units = []
            for h in range(4):
                for qt in range(NQT):
                    nkb = 4 * qt + 4
                    for kbi in range(nkb):
                        units.append((h, qt, kbi, kbi == 0, kbi == nkb - 1))
            NU = len(units)

            def load_head(h):
                hs = h % 2
                kb.dma("sp", s_kT[hs], kT[:, hs, :], qkT[4 + h, :, :], reads=[b_qkT], writes=[b_kT[hs]])
                kb.dma("sp", s_v1[hs], v1[:, hs, :, 0:128],
                       vd[:, h * 128:(h + 1) * 128].rearrange("(b p) e -> p b e", p=128),
                       reads=[b_vd], writes=[b_v1[hs]])

            def load_q(gq):
                h, qt = gq // NQT, gq % NQT
                qs = gq % 2
                kb.dma("sp", s_qT[qs], qT[:, qs, :], qkT[h, :, qt * 512:(qt + 1) * 512], reads=[b_qkT],
                       writes=[b_qT[qs]])

            def qk_stage(n):
                h, qt, kbi, first, last = units[n]
                hs = h % 2
                gq = h * NQT + qt
                qs = gq % 2
                if first:
                    if qt == 0:
                        if h == 0:
                            load_head(0)
                            load_q(0)
                    if gq + 1 < 4 * NQT:
                        load_q(gq + 1)
                j = kbi - 4 * qt
                q0 = 128 * j if j > 0 else 0
                s = n % 2
                s3 = n % 3
                for cc in range(2):
                    kb.op("pe", lambda e, cc=cc: e.matmul(ps[:, s, cc, q0:512],
                                                          lhsT=kT[cc * 64:(cc + 1) * 64, hs, kbi * 128:(kbi + 1) * 128],
                                                          rhs=qT[cc * 64:(cc + 1) * 64, qs, q0:512],
                                                          start=True, stop=(j < 0), skip_group_check=True),
                          reads=[b_kT[hs], b_qT[qs]], writes=[b_ps[s]], sig=(cc == 1 and j < 0))
                if j >= 0:
                    for cc in range(2):
                        kb.op("pe", lambda e, cc=cc: e.matmul(ps[:, s, cc, q0:512], lhsT=ident[:, :],
                                                              rhs=negm[:, 0:512 - q0], start=False, stop=True,
                                                              skip_group_check=True),
                              reads=[b_ident, b_mask], writes=[b_ps[s]], sig=(cc == 1))
                kb.op("act", lambda e: e.activation(out=PT[:, s3, :, q0:512], in_=ps[:, s, :, q0:512],
                                                    func=AF.Exp, scale=0.125),
                      reads=[b_ps[s]], writes=[b_PT[s3]])

            def pv_stage(n):
                h, qt, kbi, first, last = units[n]
                hs = h % 2
                j = kbi - 4 * qt
                s3 = n % 3
                if first and qt == 0 and h + 1 < 4:
                    load_head(h + 1)
                for cc in range(2):
                    for jj in range(max(j, 0), 4):
                        a = cc * 4 + jj
                        kb.op("pe", lambda e, cc=cc, jj=jj, a=a: e.matmul(
                            acc_view(a), lhsT=PT[:, s3, cc, jj * 128:(jj + 1) * 128],
                            rhs=v1[:, hs, kbi, 0:129], start=(kbi == 0 and a % 3 == 0),
                            stop=(kbi == 4 * qt + jj), skip_group_check=True),
                              reads=[b_PT[s3], b_v1[hs]], writes=[b_acc], sig=(cc == 1 and jj == 3))
                if last:
                    finalize(h, qt)

            pending = []
            cur_n = [0]

            def finalize(h, qt):
                gq = h * NQT + qt
                sl = gq % 4
                for bk in range(3):
                    kb.op("dve", lambda e, bk=bk: e.tensor_copy(accs[:, sl, bk, :], acc[:, bk, :]),
                          reads=[b_acc], writes=[b_accs[sl]])
                for jj in range(4):
                    kb.op("dve", lambda e, jj=jj: e.reciprocal(out=fst[:, sl, jj:jj + 1],
                                                               in_=accs_view(sl, jj, 128, 129)),
                          reads=[b_accs[sl]], writes=[b_fst[sl]])
                    kb.op("dve", lambda e, jj=jj: e.reciprocal(out=fst[:, sl, 4 + jj:5 + jj],
                                                               in_=accs_view(sl, 4 + jj, 128, 129)),
                          reads=[b_accs[sl]], writes=[b_fst[sl]])
                kb.op("dve", lambda e: e.tensor_scalar_mul(out=fst[:, sl, 4:8], in0=fst[:, sl, 4:8],
                                                           scalar1=lamt[:, 0:1]),
                      reads=[b_fst[sl], b_lam], writes=[b_fst[sl]])
                for jj in range(4):
                    kb.op("dve", lambda e, jj=jj: e.tensor_scalar_mul(out=fo[:, sl, jj, :],
                                                                      in0=accs_view(sl, jj, 0, 128),
                                                                      scalar1=fst[:, sl, jj:jj + 1]),
                          reads=[b_accs[sl], b_fst[sl]], writes=[b_fo[sl]])
                for jj in range(4):
                    kb.op("dve", lambda e, jj=jj: e.scalar_tensor_tensor(out=fo[:, sl, jj, :],
                                                                         in0=accs_view(sl, 4 + jj, 0, 128),
                                                                         scalar=fst[:, sl, 4 + jj:5 + jj],
                                                                         in1=fo[:, sl, jj, :],
                                                                         op0=ALU.mult, op1=ALU.add),
                          reads=[b_accs[sl], b_fst[sl], b_fo[sl]], writes=[b_fo[sl]])
                for jj in range(4):
                    kb.op("dve", lambda e, jj=jj: e.tensor_tensor(out=fj[:, :], in0=fo[:, sl, jj, :],
                                                                  in1=fo[:, sl, jj, :], op=ALU.mult),
                          reads=[b_fo[sl]], writes=[b_fj])
                    kb.op("dve", lambda e, jj=jj: e.reduce_sum(out=fst[:, sl, 8 + jj:9 + jj], in_=fj[:, :], axis=AX.X),
                          reads=[b_fj], writes=[b_fst[sl]])
                kb.op("dve", lambda e: e.tensor_scalar(out=fst[:, sl, 12:16], in0=fst[:, sl, 8:12], scalar1=1.0 / 128,
                                                       scalar2=SUBLN_EPS, op0=ALU.mult, op1=ALU.add),
                      reads=[b_fst[sl]], writes=[b_fst[sl]])

                def stage_act():
                    kb.op("act", lambda e: e.activation(out=fst[:, sl, 12:16], in_=fst[:, sl, 12:16], func=AF.Ln),
                          reads=[b_fst[sl]], writes=[b_fst[sl]])
                    kb.op("act", lambda e: e.activation(out=fst[:, sl, 12:16], in_=fst[:, sl, 12:16], func=AF.Exp,
                                                        scale=-0.5),
                          reads=[b_fst[sl]], writes=[b_fst[sl]])

                def stage_dve():
                    for jj in range(4):
                        kb.op("dve", lambda e, jj=jj: e.scalar_tensor_tensor(
                            out=fob[:, sl, jj, :], in0=fo[:, sl, jj, :], scalar=fst[:, sl, 12 + jj:13 + jj],
                            in1=gsub[:, :], op0=ALU.mult, op1=ALU.mult),
                              reads=[b_fo[sl], b_fst[sl], b_gs], writes=[b_fob[sl]])

                def stage_pe():
                    for jj in range(4):
                        kb.op("pe", lambda e, jj=jj: e.transpose(out=ptr[:, jj, :], in_=fob[:, sl, jj, :],
                                                                 identity=ident[:, :]),
                              reads=[b_fob[sl], b_ident], writes=[b_ptr], sig=(jj == 3))

                def stage_out():
                    osl = gq % 2
                    kb.op("dve", lambda e: e.tensor_copy(otr[:, osl, :], ptr[:, :, :]),
                          reads=[b_ptr], writes=[b_otr[osl]])
                    kb.dma("sp", s_otr[osl], mT[h, :, qt * 512:(qt + 1) * 512], otr[:, osl, :],
                           reads=[b_otr[osl]], writes=[b_mT])

                n0 = cur_n[0]
                pending.append((n0 + 10, stage_act))
                pending.append((n0 + 12, stage_dve))
                pending.append((n0 + 14, stage_pe))
                pending.append((n0 + 16, stage_out))

            def flush(upto):
                while pending and (upto is None or pending[0][0] <= upto):
                    pending.pop(0)[1]()

            for n in range(NU + LOOK):
                cur_n[0] = n
                if n < NU:
                    qk_stage(n)
                if n - LOOK >= 0:
                    pv_stage(n - LOOK)
                flush(n)
            flush(None)
            kb.barrier()

    def dil_phase(l):
        kb.barrier()
        u = "_m2b_%d" % l
        LOOK = 2
        import os
        DIL = tuple(int(t) for t in os.environ.get("KDIL", "1,4,16").split(","))
        with (nc.sbuf_tensor("dk" + u, [128, 3, SPAN], BF16) as dk,
              nc.sbuf_tensor("dq" + u, [128, 2, SPAN], BF16) as dq,
              nc.sbuf_tensor("dkg" + u, [128, 2, 3, SPAN], BF16) as dkg,
              nc.sbuf_tensor("dqg" + u, [128, 2, 2, SPAN], BF16) as dqg,
              nc.sbuf_tensor("vg" + u, [128, 3, 3, 16, 256], BF16) as vg,
              nc.sbuf_tensor("dPT" + u, [128, 3, 2, 2, 128], BF16) as dPT,
              nc.sbuf_tensor("an" + u, [128, 2, SPAN], F32) as an,
              nc.sbuf_tensor("ad" + u, [128, 2, SPAN], F32) as ad,
              nc.psum_tensor("dps" + u, [128, 2, 2, 512], F32) as dps,
              nc.psum_tensor("dpo" + u, [128, 2, 512], F32) as dpo):
            b_dk = [Buf("dk%d" % i) for i in range(3)]
            s_dk = [kb.dma_sem("dk%d" % i) for i in range(3)]
            b_dq = [Buf("dq0"), Buf("dq1")]
            s_dq = [kb.dma_sem("dq0"), kb.dma_sem("dq1")]
            b_dkg = [[Buf("dkg%d%d" % (gi, sl)) for sl in range(3)] for gi in range(2)]
            b_dqg = [[Buf("dqg%d%d" % (gi, sl)) for sl in range(2)] for gi in range(2)]
            b_vg = [[Buf("vg%d%d" % (bi, sl)) for sl in range(3)] for bi in range(3)]
            s_vg = [[kb.dma_sem("vg%d%d" % (bi, sl)) for sl in range(3)] for bi in range(3)]
            b_dps = [Buf("dps0"), Buf("dps1")]
            b_dPT = [Buf("dPT0"), Buf("dPT1"), Buf("dPT2")]
            b_dpo = [Buf("dpo0"), Buf("dpo1")]
            b_an = [Buf("an0"), Buf("an1")]
            s_an = [kb.dma_sem("an0"), kb.dma_sem("an1")]
            b_ad = [Buf("ad0"), Buf("ad1")]
            iters = [(j, sp_) for j in range(4) for sp_ in range(NSPAN)]
            units = []
            for it, (j, sp_) in enumerate(iters):
                lst = []
                for bi, d in enumerate(DIL):
                    nbl = SPAN // (128 * d)
                    for r in range(d):
                        for bl in range(nbl):
                            lst.append([it, bi, d, r, bl, False, False])
                lst[0][5] = True
                lst[-1][6] = True
                units += lst
            NU = len(units)

            def loads(it):
                j, sp_ = iters[it]
                sl = it % 3
                ql = it % 2
                base = sp_ * SPAN
                kb.dma("sp", s_dk[sl], dk[:, sl, :], qkT[12 + j, :, base:base + SPAN], reads=[b_qkT],
                       writes=[b_dk[sl]])
                kb.dma("sp", s_dq[ql], dq[:, ql, :], qkT[8 + j, :, base:base + SPAN], reads=[b_qkT],
                       writes=[b_dq[ql]])
                for bi, d in enumerate(DIL):
                    nbl = SPAN // (128 * d)
                    srcv = va[base:base + SPAN, j, :, :].rearrange("(bl i r) t e -> i r bl (t e)", i=128, r=d)
                    dstv = vg[:, bi, sl, 0:d * nbl, :].rearrange("p (r bl) e -> p r bl e", r=d)
                    for r in range(d):
                        kb.dma("sp", s_vg[bi][sl], dstv[:, r, :, :], srcv[:, r, :, :], reads=[b_va],
                               writes=[b_vg[bi][sl]])
                    kb.group([b_vg[bi][sl]])
                for bi, d in enumerate(DIL):
                    if d == 1:
                        continue
                    gi = 0 if d == 4 else 1
                    kb.op("pool", lambda e, gi=gi, d=d: e.tensor_copy(
                        dkg[:, gi, sl, :].rearrange("p (r i) -> p r i", r=d),
                        dk[:, sl, :].rearrange("p (i r) -> p r i", r=d)),
                          reads=[b_dk[sl]], writes=[b_dkg[gi][sl]])
                    kb.op("pool", lambda e, gi=gi, d=d: e.tensor_copy(
                        dqg[:, gi, ql, :].rearrange("p (r i) -> p r i", r=d),
                        dq[:, ql, :].rearrange("p (i r) -> p r i", r=d)),
                          reads=[b_dq[ql]], writes=[b_dqg[gi][ql]])

            def operands(n):
                it, bi, d, r, bl, first, last = units[n]
                j, sp_ = iters[it]
                sl = it % 3
                pl = (it - 1) % 3
                ql = it % 2
                nbl = SPAN // (128 * d)
                gi = 0 if d == 4 else 1
                goff = r * (SPAN // d) + bl * 128
                o = Ctx()
                o.has_prev = (bl > 0) or (sp_ > 0)
                if d == 1:
                    o.qv = dq[:, ql, goff:goff + 128]
                    o.kcur = dk[:, sl, goff:goff + 128]
                    o.rd_cur = [b_dk[sl], b_dq[ql]]
                else:
                    o.qv = dqg[:, gi, ql, goff:goff + 128]
                    o.kcur = dkg[:, gi, sl, goff:goff + 128]
                    o.rd_cur = [b_dkg[gi][sl], b_dqg[gi][ql]]
                o.vcur = vg[:, bi, sl, r * nbl + bl, :]
                o.rd_vcur = [b_vg[bi][sl]]
                if bl > 0:
                    o.kprev = dk[:, sl, goff - 128:goff] if d == 1 else dkg[:, gi, sl, goff - 128:goff]
                    o.vprev = vg[:, bi, sl, r * nbl + bl - 1, :]
                    o.rd_kprev = []
                    o.rd_vprev = []
                elif sp_ > 0:
                    poff = r * (SPAN // d) + (nbl - 1) * 128
                    o.kprev = dk[:, pl, poff:poff + 128] if d == 1 else dkg[:, gi, pl, poff:poff + 128]
                    o.vprev = vg[:, bi, pl, r * nbl + nbl - 1, :]
                    o.rd_kprev = [b_dk[pl]] if d == 1 else [b_dkg[gi][pl]]
                    o.rd_vprev = [b_vg[bi][pl]]
                o.off = bl * 128 * d + r
                return o

            def qk_stage(n):
                it, bi, d, r, bl, first, last = units[n]
                if n == 0:
                    loads(0)
                o = operands(n)
                s = n % 2
                s3 = n % 3
                ksel = ([0] if o.has_prev else []) + [1]
                k0 = ksel[0]
                for idx, ks in enumerate(ksel):
                    for hh in range(2):
                        kk_ = o.kprev if ks == 0 else o.kcur
                        kb.op("pe", lambda e, hh=hh, ks=ks, kk_=kk_, idx=idx: e.matmul(
                            dps[:, s, hh, ks * 128:(ks + 1) * 128],
                            lhsT=kk_[hh * 64:(hh + 1) * 64, :], rhs=o.qv[hh * 64:(hh + 1) * 64, :],
                            start=(idx == 0), stop=False, skip_group_check=True),
                              reads=o.rd_cur + (o.rd_kprev if ks == 0 else []), writes=[b_dps[s]], sig=False)
                for hh in range(2):
                    kb.op("pe", lambda e, hh=hh: e.matmul(dps[:, s, hh, k0 * 128:256], lhsT=ident[:, :],
                                                          rhs=negd[:, k0 * 128:256], start=False, stop=True,
                                                          skip_group_check=True),
                          reads=[b_ident, b_mask], writes=[b_dps[s]], sig=(hh == 1))
                kb.op("act", lambda e: e.activation(out=dPT[:, s3, :, k0:2, :],
                                                    in_=dps[:, s, :, k0 * 128:256].rearrange(
                                                        "p h (k q) -> p h k q", q=128),
                                                    func=AF.Exp, scale=0.125),
                      reads=[b_dps[s]], writes=[b_dPT[s3]])

            def pv_stage(n):
                it, bi, d, r, bl, first, last = units[n]
                j, sp_ = iters[it]
                if first and it + 1 < len(iters):
                    loads(it + 1)
                o = operands(n)
                s = n % 2
                s3 = n % 3
                asl = it % 2
                ksel = ([0] if o.has_prev else []) + [1]
                firstmm = True
                for hh in range(2):
                    for ks in ksel:
                        vv = o.vprev if ks == 0 else o.vcur
                        kb.op("pe", lambda e, hh=hh, ks=ks, vv=vv, firstmm=firstmm: e.matmul(
                            dpo[:, s, 0:128], lhsT=vv[:, hh * 128:(hh + 1) * 128],
                            rhs=dPT[:, s3, hh, ks, :], start=firstmm, stop=False, skip_group_check=True),
                              reads=[b_dPT[s3]] + (o.rd_vprev if ks == 0 else o.rd_vcur), writes=[b_dpo[s]],
                              sig=False)
                        firstmm = False
                for hh in range(2):
                    for ks in ksel:
                        kb.op("pe", lambda e, hh=hh, ks=ks: e.matmul(
                            dpo[:, s, 128:256], lhsT=eones[:, hh, :],
                            rhs=dPT[:, s3, hh, ks, :], start=False, stop=False, skip_group_check=True),
                              reads=[b_dPT[s3], b_mask], writes=[b_dpo[s]], sig=(hh == 1 and ks == 1))
                off = o.off
                anv = an[:, asl, off:off + 127 * d + 1:d]
                adv = ad[:, asl, off:off + 127 * d + 1:d]
                if bi == 0:
                    kb.op("dve", lambda e: e.tensor_copy(anv, dpo[:, s, 0:128]),
                          reads=[b_dpo[s]], writes=[b_an[asl]])
                    kb.op("dve", lambda e: e.tensor_copy(adv, dpo[:, s, 128:256]),
                          reads=[b_dpo[s]], writes=[b_ad[asl]])
                else:
                    kb.op("dve", lambda e: e.tensor_tensor(out=anv, in0=dpo[:, s, 0:128], in1=anv, op=ALU.add),
                          reads=[b_dpo[s], b_an[asl]], writes=[b_an[asl]])
                    kb.op("dve", lambda e: e.tensor_tensor(out=adv, in0=dpo[:, s, 128:256], in1=adv, op=ALU.add),
                          reads=[b_dpo[s], b_ad[asl]], writes=[b_ad[asl]])
                if last:
                    base = sp_ * SPAN
                    kb.op("dve", lambda e: e.reciprocal(out=ad[:, asl, :], in_=ad[:, asl, :]),
                          reads=[b_ad[asl]], writes=[b_ad[asl]])
                    kb.op("pool", lambda e: e.tensor_tensor(out=an[:, asl, :], in0=an[:, asl, :], in1=ad[:, asl, :],
                                                            op=ALU.mult),
                          reads=[b_ad[asl], b_an[asl]], writes=[b_an[asl]])
                    kb.dma("sp", s_an[asl], aT[j, :, base:base + SPAN], an[:, asl, :], reads=[b_an[asl]],
                           writes=[b_aT])

            for n in range(NU + LOOK):
                if n < NU:
                    qk_stage(n)
                if n - LOOK >= 0:
                    pv_stage(n - LOOK)
            kb.barrier()

    def wout_phase(l, src, dst):
        kb.barrier()
        u = "_m3_%d" % l
        with (nc.sbuf_tensor("wo" + u, [128, 8, D], BF16) as wo,
              nc.sbuf_tensor("gd" + u, [128, 4], F32) as gd,
              nc.sbuf_tensor("mt" + u, [128, 3, 4, TT], BF16) as mt,
              nc.sbuf_tensor("at" + u, [128, 3, 4, TT], F32) as at,
              nc.sbuf_tensor("sq" + u, [128, 2, TT], BF16) as sq,
              nc.sbuf_tensor("rs" + u, [128, 2, TT], F32) as rs,
              nc.sbuf_tensor("m2" + u, [128, 2, 4, TT], BF16) as m2,
              nc.sbuf_tensor("xr" + u, [128, 2, D], F32) as xr,
              nc.psum_tensor("pss" + u, [128, 2, 512], F32) as pss,
              nc.psum_tensor("po" + u, [128, 2, 512], F32) as po):
            b_wo = [Buf("wo%d" % k) for k in range(8)]
            b_gd = Buf("gd")
            b_mt = [Buf("mt%d" % i) for i in range(3)]
            s_mt = [kb.dma_sem("mt%d" % i) for i in range(3)]
            b_at = [Buf("at%d" % i) for i in range(3)]
            s_at = [kb.dma_sem("at%d" % i) for i in range(3)]
            b_sq = [Buf("sq0"), Buf("sq1")]
            b_rs = [Buf("rs0"), Buf("rs1")]
            b_m2 = [Buf("m20"), Buf("m21")]
            b_xr = [Buf("xr0"), Buf("xr1")]
            s_xr = [kb.dma_sem("xr0"), kb.dma_sem("xr1")]
            s_xo = [kb.dma_sem("xo0"), kb.dma_sem("xo1")]
            b_pss = [Buf("pss0"), Buf("pss1")]
            b_po = [Buf("po0"), Buf("po1")]
            s_w = kb.dma_sem("wgs")
            for k in range(8):
                kb.dma("pool", s_w, wo[:, k, :], w["w_out"][l, k * 128:(k + 1) * 128, :], writes=[b_wo[k]])
            kb.group(b_wo)
            s_g = kb.dma_sem("g")
            kb.dma("sp", s_g, gd[:, :], w["dil_gain"][l, :].rearrange("(j p) -> p j", p=128), writes=[b_gd],
                   allow_slow_non_contiguous=True)
            sqc = [0]
            dn = [0]
            def wloads(i):
                t3 = i % 3
                c0 = i * TT
                kb.dma("sp", s_mt[t3], mt[:, t3, :, :], mT[:, :, c0:c0 + TT].rearrange("c p t -> p c t"),
                       reads=[b_mT], writes=[b_mt[t3]])
                kb.dma("sp", s_at[t3], at[:, t3, :, :], aT[:, :, c0:c0 + TT].rearrange("c p t -> p c t"),
                       reads=[b_aT], writes=[b_at[t3]])

            def chain(i):
                ts = i % 2
                t3 = i % 3
                for j in range(4):
                    s = sqc[0] % 2
                    sqc[0] += 1
                    kb.op("act", lambda e, j=j: e.activation(out=sq[:, s, :], in_=at[:, t3, j, :], func=AF.Square),
                          reads=[b_at[t3]], writes=[b_sq[s]])
                    kb.op("pe", lambda e, j=j: e.matmul(pss[:, ts, :], lhsT=ones_b[:, :], rhs=sq[:, s, :],
                                                        start=(j == 0), stop=(j == 3)),
                          reads=[b_sq[s], b_mask], writes=[b_pss[ts]])
                kb.op("dve", lambda e: e.tensor_scalar(out=rs[:, ts, :], in0=pss[:, ts, :], scalar1=1.0 / 512,
                                                       scalar2=EPS, op0=ALU.mult, op1=ALU.add),
                      reads=[b_pss[ts]], writes=[b_rs[ts]])
                kb.op("act", lambda e: e.activation(out=rs[:, ts, :], in_=rs[:, ts, :], func=AF.Sqrt),
                      reads=[b_rs[ts]], writes=[b_rs[ts]])
                kb.op("dve", lambda e: e.reciprocal(out=rs[:, ts, :], in_=rs[:, ts, :]),
                      reads=[b_rs[ts]], writes=[b_rs[ts]])
                for j in range(4):
                    kb.op("dve", lambda e, j=j: e.scalar_tensor_tensor(out=m2[:, ts, j, :], in0=at[:, t3, j, :],
                                                                       scalar=gd[:, j:j + 1], in1=rs[:, ts, :],
                                                                       op0=ALU.mult, op1=ALU.mult),
                          reads=[b_at[t3], b_gd, b_rs[ts]], writes=[b_m2[ts]])

            def main(i):
                ts = i % 2
                t3 = i % 3
                c0 = i * TT
                for b in range(4):
                    n = dn[0]
                    dn[0] += 1
                    s = n % 2
                    r0 = c0 + b * 128
                    kb.dma("sp", s_xr[s], xr[:, s, :], src[r0:r0 + 128, :], writes=[b_xr[s]])
                    for half in range(2):
                        for cch in range(8):
                            lhs = mt[:, t3, cch, b * 128:(b + 1) * 128] if cch < 4 else \
                                m2[:, ts, cch - 4, b * 128:(b + 1) * 128]
                            kb.op("pe", lambda e, cch=cch, lhs=lhs: e.matmul(
                                po[:, half, :], lhsT=lhs, rhs=wo[:, cch, half * 512:(half + 1) * 512],
                                start=(cch == 0), stop=(cch == 7)),
                                  reads=[b_wo[cch], b_mt[t3], b_m2[ts]], writes=[b_po[half]], sig=(cch == 7))
                        kb.op("dve", lambda e: e.tensor_tensor(out=xr[:, s, half * 512:(half + 1) * 512],
                                                               in0=po[:, half, :],
                                                               in1=xr[:, s, half * 512:(half + 1) * 512], op=ALU.add),
                              reads=[b_po[half], b_xr[s]], writes=[b_xr[s]])
                    kb.dma("sp", s_xo[s], dst[r0:r0 + 128, :], xr[:, s, :], reads=[b_xr[s]], writes=[])

            wloads(0)
            if NT > 1:
                wloads(1)
            chain(0)
            for i in range(NT):
                if i + 2 < NT:
                    wloads(i + 2)
                if i + 1 < NT:
                    chain(i + 1)
                main(i)
            kb.barrier()

    cur = x_in if first else xs
    def on(p):
        return phases is None or p in phases
    if on("rope"):
        rope_tables()
    for li, l in enumerate(layers):
        is_last = last and (li == len(layers) - 1)
        lam_init = 0.8 - 0.6 * math.exp(-0.3 * l)
        if on("ffn1"):
            ffn_phase(l, 1, cur, xs, False)
            cur = xs
        if on("proj"):
            proj_phase(l, cur)
        if on("diff"):
            diff_phase(l, lam_init)
        if on("dil"):
            dil_phase(l)
        if on("wout"):
            wout_phase(l, cur, xs)
            cur = xs
        if on("ffn2"):
            ffn_phase(l, 2, cur, y_out if is_last else xs, is_last)
            cur = xs
    kb.finish()
    c.n_inst = kb.n_inst
    return nc, c


def make_consts():
    p = np.arange(128)
    inv = (1.0 / (10000.0 ** (np.arange(0, 64, 2, dtype=np.float32) / np.float32(64)))).astype(np.float32)
    ropec = np.zeros((128, 2), np.float32)
    ropec[:, 0] = inv[p % 32]
    ropec[:, 1] = np.where((p % 64) < 32, -1.0, 1.0)
    i = p[:, None]
    masks = np.zeros((128, 1536), np.float32)
    masks[:, 0:512] = (np.arange(512)[None, :] >= i)
    cc = np.arange(128)[None, :]
    masks[:, 512:640] = (i >= cc)
    masks[:, 640:768] = (i <= cc)
    masks[:, 768:1536] = np.where(masks[:, 0:768] > 0, 0.0, -30000.0)
    eones = np.zeros((128, 256), np.float32)
    eones[:, 0:64] = 1.0
    eones[:, 128 + 64:256] = 1.0
    prot = np.zeros((128, 128), np.float32)
    prot[p ^ 32, p] = 1.0
    return {"ident": np.eye(128, dtype=np.float32), "ropec": ropec, "masks": masks, "eones": eones, "prot": prot}


W_NAMES = ("ffn1_norm", "ffn1_gate", "ffn1_up", "ffn1_down", "mix_norm", "w_in", "lambda_q1", "lambda_k1",
           "lambda_q2", "lambda_k2", "subln_gain", "dil_gain", "w_out", "ffn2_norm", "ffn2_gate", "ffn2_up",
           "ffn2_down")

_NC_CACHE = {}


def run_layers(x, inputs, layers, first, last, n_cores=8):
    B, S, _ = x.shape
    LT = int(np.asarray(inputs["w_in"]).shape[0])
    key = (S, tuple(layers), first, last, LT)
    if key not in _NC_CACHE:
        _NC_CACHE[key] = build_nc(S, list(layers), n_layers_total=LT, first=first, last=last)[0]
    nc = _NC_CACHE[key]
    consts = make_consts()
    shared = {k: np.ascontiguousarray(np.asarray(inputs[k], dtype=np.float32)) for k in W_NAMES}
    shared["final_norm"] = np.ascontiguousarray(np.asarray(inputs["final_norm"], dtype=np.float32).reshape(1, D))
    shared["positions"] = np.ascontiguousarray(np.asarray(inputs["positions"], dtype=np.int32))
    shared.update(consts)
    in_maps = []
    for b in range(B):
        m = dict(shared)
        m["x"] = np.ascontiguousarray(x[b])
        in_maps.append(m)
    res = run_bass_kernel_spmd(nc, in_maps, core_ids=list(range(B)))
    return np.stack([np.asarray(r["y"]) for r in res.results], axis=0)


def kernel(**inputs):
    x = np.asarray(inputs["x"], dtype=np.float32)
    return run_layers(x, inputs, list(range(DEPTH)), True, True).astype(np.float32)
```

```python
import contextlib
import math
import numpy as np
import concourse.bass as bass
import concourse.mybir as mybir
from concourse.bass_utils import run_bass_kernel_spmd

F32 = mybir.dt.float32
BF16 = mybir.dt.bfloat16
I32 = mybir.dt.int32
AF = mybir.ActivationFunctionType
ALU = mybir.AluOpType
AX = mybir.AxisListType

D = 1024
DFF = 2816
NFC = DFF // 128
DEPTH = 4
PROJ = 3072
EPS = 1e-6
SUBLN_EPS = 1e-5
TT = 512


class Buf:
    __slots__ = ("name", "w", "r")

    def __init__(self, name):
        self.name = name
        self.w = None
        self.r = {}


class KB:
    def __init__(self, nc):
        self.nc = nc
        self.eng = {"pe": nc.tensor, "act": nc.scalar, "dve": nc.vector, "pool": nc.gpsimd, "sp": nc.sync}
        self.psem = {}
        self.cnt = {}
        self.seen = {e: {} for e in self.eng}
        self.sems = []
        for e in ("pe", "act", "dve", "pool"):
            self.psem[e] = self.new_sem("prog_" + e)
            self.cnt[e] = 0
        self.named = {}
        self.unsig = {e: False for e in self.eng}
        self.dcnt = {}
        self.dma_sems = []
        self.n_inst = 0

    def new_sem(self, name):
        cm = self.nc.semaphore(name)
        s = cm.__enter__()
        self.sems.append(cm)
        return s

    def dma_sem(self, name):
        if name in self.named:
            return self.named[name]
        s = self.new_sem("d_" + name)
        self.dcnt[id(s)] = 0
        self.dma_sems.append(s)
        self.named[name] = s
        return s

    def group(self, bufs):
        last = None
        for b in bufs:
            if b.w is not None and (last is None or b.w[1] > last[1]):
                last = b.w
        for b in bufs:
            b.w = last

    def _wait(self, e, tok):
        if tok is None:
            return
        sem, val, src = tok
        if src == e and e == "pe":
            return
        seen = self.seen[e]
        if seen.get(id(sem), 0) >= val:
            return
        seen[id(sem)] = val
        self.eng[e].wait_ge(sem, val)
        self.n_inst += 1

    def _deps(self, e, reads, writes):
        for b in reads:
            self._wait(e, b.w)
        for b in writes:
            self._wait(e, b.w)
            for t in b.r.values():
                self._wait(e, t)

    def _mark(self, tok, reads, writes):
        for b in reads:
            b.r[id(tok[0])] = tok
        for b in writes:
            b.w = tok
            b.r = {}

    def op(self, e, fn, reads=(), writes=(), sig=True):
        self._deps(e, reads, writes)
        ins = fn(self.eng[e])
        if sig:
            self.cnt[e] += 1
            ins.then_inc(self.psem[e], 1)
            tok = (self.psem[e], self.cnt[e], e)
            self.unsig[e] = False
        else:
            tok = (self.psem[e], self.cnt[e] + 1, e)
            self.unsig[e] = True
        self._mark(tok, reads, writes)
        self.n_inst += 1
        return tok

    def dma(self, e, sem, out, in_, reads=(), writes=(), **kw):
        self._deps(e, reads, writes)
        ins = self.eng[e].dma_start(out=out, in_=in_, **kw)
        self.dcnt[id(sem)] += 16
        ins.then_inc(sem, 16)
        tok = (sem, self.dcnt[id(sem)], "dma")
        self._mark(tok, reads, writes)
        self.n_inst += 1
        return tok

    def barrier(self):
        assert not any(self.unsig.values()), self.unsig
        toks = [(self.psem[e], self.cnt[e], e) for e in self.psem if self.cnt[e] > 0]
        toks += [(s, self.dcnt[id(s)], "dma") for s in self.dma_sems if self.dcnt[id(s)] > 0]
        for e in self.eng:
            for t in toks:
                if t[2] == e:
                    sem, val, _ = t
                    if self.seen[e].get(id(sem), 0) < val:
                        self.seen[e][id(sem)] = val
                        self.eng[e].wait_ge(sem, val)
                else:
                    self._wait(e, t)

    def finish(self):
        self.barrier()


class Ctx:
    pass


def build_nc(S, layers, n_layers_total=DEPTH, first=True, last=True, phases=None):
    nc = bass.Bass("TRN2", target_bir_lowering=False)
    NT = S // TT
    c = Ctx()
    c.nc = nc
    c.S = S
    x_in = nc.dram_tensor("x", [S, D], F32, kind="ExternalInput").ap()
    y_out = nc.dram_tensor("y", [S, D], F32, kind="ExternalOutput").ap()
    L = n_layers_total
    w = {}
    for nm, shp in (("ffn1_norm", [L, D]), ("ffn1_gate", [L, D, DFF]), ("ffn1_up", [L, D, DFF]),
                    ("ffn1_down", [L, DFF, D]), ("ffn2_norm", [L, D]), ("ffn2_gate", [L, D, DFF]),
                    ("ffn2_up", [L, D, DFF]), ("ffn2_down", [L, DFF, D]), ("final_norm", [1, D])):
        w[nm] = nc.dram_tensor(nm, shp, F32, kind="ExternalInput").ap()
    for nm, shp in (("mix_norm", [L, D]), ("w_in", [L, D, PROJ]), ("lambda_q1", [L, 64]), ("lambda_k1", [L, 64]),
                    ("lambda_q2", [L, 64]), ("lambda_k2", [L, 64]), ("subln_gain", [L, 128]),
                    ("dil_gain", [L, 512]), ("w_out", [L, D, D])):
        w[nm] = nc.dram_tensor(nm, shp, F32, kind="ExternalInput").ap()
    positions = nc.dram_tensor("positions", [S], I32, kind="ExternalInput").ap()
    ident_d = nc.dram_tensor("ident", [128, 128], F32, kind="ExternalInput").ap()
    ropec_d = nc.dram_tensor("ropec", [128, 2], F32, kind="ExternalInput").ap()
    masks_d = nc.dram_tensor("masks", [128, 1536], F32, kind="ExternalInput").ap()
    eones_d = nc.dram_tensor("eones", [128, 256], F32, kind="ExternalInput").ap()
    prot_d = nc.dram_tensor("prot", [128, 128], F32, kind="ExternalInput").ap()
    xs = nc.dram_tensor("xs", [S, D], F32, kind="Internal").ap()
    cosT = nc.dram_tensor("cosT", [128, S], F32, kind="Internal").ap()
    sinT = nc.dram_tensor("sinT", [128, S], F32, kind="Internal").ap()
    qkT = nc.dram_tensor("qkT", [16, 128, S], BF16, kind="Internal").ap()
    vd = nc.dram_tensor("vd", [S, 512], BF16, kind="Internal").ap()
    va = nc.dram_tensor("va", [S, 4, 2, 128], BF16, kind="Internal").ap()
    mT = nc.dram_tensor("mT", [4, 128, S], BF16, kind="Internal").ap()
    aT = nc.dram_tensor("aT", [4, 128, S], F32, kind="Internal").ap()
    b_cs, b_qkT, b_vd, b_va, b_mT, b_aT = (Buf(n) for n in ("cs", "qkT", "vd", "va", "mT", "aT"))

    kb = KB(nc)
    c.kb = kb

    ident = nc.alloc_sbuf_tensor("ident_b", [128, 128], BF16)
    ropec = nc.alloc_sbuf_tensor("ropec_sb", [128, 2], F32)
    negm = nc.alloc_sbuf_tensor("negm", [128, 512], BF16)
    negd = nc.alloc_sbuf_tensor("negd", [128, 256], BF16)
    eones = nc.alloc_sbuf_tensor("eones_b", [128, 2, 128], BF16)
    ones_b = nc.alloc_sbuf_tensor("ones_b", [128, 128], BF16)
    prot = nc.alloc_sbuf_tensor("prot_b", [128, 128], BF16)
    b_ident, b_ropec, b_mask = Buf("ident"), Buf("ropec"), Buf("mask")
    s_const = kb.dma_sem("const")
    with (nc.sbuf_tensor("ident_f", [128, 128], F32) as ident_f,
          nc.sbuf_tensor("masks_f", [128, 1536], F32) as masks_f,
          nc.sbuf_tensor("eones_f", [128, 256], F32) as eones_f,
          nc.sbuf_tensor("prot_f", [128, 128], F32) as prot_f):
        kb.dma("sp", s_const, ident_f[:, :], ident_d[:, :], writes=[b_ident])
        kb.dma("sp", s_const, ropec[:, :], ropec_d[:, :], writes=[b_ropec])
        kb.dma("sp", s_const, masks_f[:, :], masks_d[:, :], writes=[b_mask])
        kb.dma("sp", s_const, eones_f[:, :], eones_d[:, :], writes=[b_mask])
        kb.dma("sp", s_const, prot_f[:, :], prot_d[:, :], writes=[b_mask])
        kb.group([b_ident, b_ropec, b_mask])
        kb.op("dve", lambda e: e.tensor_copy(ident[:, :], ident_f[:, :]), reads=[b_ident], writes=[b_ident])
        kb.op("dve", lambda e: e.tensor_copy(eones[:, :, :], eones_f[:, :].rearrange("p (k q) -> p k q", q=128)),
              reads=[b_mask], writes=[b_mask])
        kb.op("dve", lambda e: e.memset(ones_b[:, :], 1.0), reads=[b_mask], writes=[b_mask])
        kb.op("dve", lambda e: e.tensor_copy(prot[:, :], prot_f[:, :]), reads=[b_mask], writes=[b_mask])
        kb.op("dve", lambda e: e.tensor_copy(negm[:, :], masks_f[:, 768:1280]), reads=[b_mask], writes=[b_mask])
        kb.op("dve", lambda e: e.tensor_copy(negd[:, :], masks_f[:, 1280:1536]), reads=[b_mask], writes=[b_mask])
        kb.barrier()

    def rstd_ops(st, col, b_st):
        kb.op("dve", lambda e: e.tensor_scalar(out=st[:, 8 + col:9 + col], in0=st[:, col:col + 1],
                                               scalar1=1.0 / D, scalar2=EPS, op0=ALU.mult, op1=ALU.add),
              reads=[b_st], writes=[b_st])
        kb.op("act", lambda e: e.activation(out=st[:, 8 + col:9 + col], in_=st[:, 8 + col:9 + col], func=AF.Sqrt),
              reads=[b_st], writes=[b_st])
        kb.op("dve", lambda e: e.reciprocal(out=st[:, 8 + col:9 + col], in_=st[:, 8 + col:9 + col]),
              reads=[b_st], writes=[b_st])

    def ffn_phase(l, which, src, dst, final):
        kb.barrier()
        nm = "ffn%d" % which
        u = "_%d_%d" % (l, which)
        with (nc.sbuf_tensor("wg" + u, [128, 8, DFF], BF16) as wg,
              nc.sbuf_tensor("wu" + u, [128, 8, DFF], BF16) as wu,
              nc.sbuf_tensor("wd" + u, [128, NFC, D], BF16) as wd,
              nc.sbuf_tensor("gbc" + u, [128, D], F32) as gbc,
              nc.sbuf_tensor("gfin" + u, [128, D], F32) as gfin,
              nc.sbuf_tensor("xblk" + u, [128, 2, D], F32) as xblk,
              nc.sbuf_tensor("hb" + u, [128, 2, D], BF16) as hb,
              nc.sbuf_tensor("hT" + u, [128, 8, TT], BF16) as hT,
              nc.sbuf_tensor("actb" + u, [128, NFC, TT], BF16) as actb,
              nc.sbuf_tensor("sg" + u, [128, 2, TT], F32) as sg,
              nc.sbuf_tensor("xr" + u, [128, 2, D], F32) as xr,
              nc.sbuf_tensor("junk" + u, [128, D], BF16) as junk,
              nc.sbuf_tensor("st" + u, [128, 16], F32) as st,
              nc.psum_tensor("pT" + u, [128, 2, 8, 128], BF16) as pT,
              nc.psum_tensor("pg" + u, [128, 2, 512], F32) as pg,
              nc.psum_tensor("pu" + u, [128, 2, 512], F32) as pu,
              nc.psum_tensor("po" + u, [128, 2, 512], F32) as po):
            b_wg = [[Buf("wg%d_%d" % (g, k)) for k in range(8)] for g in range(4)]
            b_wu = [[Buf("wu%d_%d" % (g, k)) for k in range(8)] for g in range(4)]
            b_wd = [Buf("wd%d" % k) for k in range(NFC)]
            b_g = Buf("g")
            b_x = [Buf("x0"), Buf("x1")]
            s_x = [kb.dma_sem("x0"), kb.dma_sem("x1")]
            b_h = [Buf("h0"), Buf("h1")]
            b_hT4 = [Buf("hT%d" % b) for b in range(4)]
            b_act = [Buf("act%d" % f) for f in range(NFC)]
            b_sg = [Buf("sg0"), Buf("sg1")]
            b_xr = [Buf("xr0"), Buf("xr1")]
            s_xr = [kb.dma_sem("xr0"), kb.dma_sem("xr1")]
            s_xo = [kb.dma_sem("xo0"), kb.dma_sem("xo1")]
            b_st = Buf("st")
            b_junk = Buf("junk")
            b_pT = [Buf("pT0"), Buf("pT1")]
            b_pg = [Buf("pg0"), Buf("pg1")]
            b_pu = [Buf("pu0"), Buf("pu1")]
            b_po = [Buf("po0"), Buf("po1")]
            s_g = kb.dma_sem("g")
            b_gf = Buf("gf")
            kb.dma("sp", s_g, gbc[:, :], w[nm + "_norm"][l, :].partition_broadcast(128), writes=[b_g])
            if final:
                kb.dma("sp", s_g, gfin[:, :], w["final_norm"][0, :].partition_broadcast(128), writes=[b_gf])
                kb.group([b_g, b_gf])
            s_wds = kb.dma_sem("wds")
            GB = (0, 6, 12, 17, 22)
            for g in range(4):
                c_lo, c_hi = GB[g] * 128, GB[g + 1] * 128
                sg_, su_ = kb.dma_sem("wgs%d" % g), kb.dma_sem("wus%d" % g)
                for k in range(8):
                    kb.dma("pool", sg_, wg[:, k, c_lo:c_hi], w[nm + "_gate"][l, k * 128:(k + 1) * 128, c_lo:c_hi],
                           writes=[b_wg[g][k]])
                kb.group(b_wg[g])
                for k in range(8):
                    kb.dma("pool", su_, wu[:, k, c_lo:c_hi], w[nm + "_up"][l, k * 128:(k + 1) * 128, c_lo:c_hi],
                           writes=[b_wu[g][k]])
                kb.group(b_wu[g])
            for f in range(NFC):
                kb.dma("pool", s_wds, wd[:, f, :], w[nm + "_down"][l, f * 128:(f + 1) * 128, :], writes=[b_wd[f]])
            kb.group(b_wd)

            blk_ctr = [0]
            pend = {}

            def norm_block(i, b):
                n = blk_ctr[0]
                blk_ctr[0] += 1
                s = n % 2
                r0 = i * TT + b * 128
                kb.dma("sp", s_x[s], xblk[:, s, :], src[r0:r0 + 128, :], writes=[b_x[s]])
                col = (n % 8)
                kb.op("act", lambda e: e.activation(out=junk[:, :], in_=xblk[:, s, :], func=AF.Square,
                                                    accum_out=st[:, col:col + 1]),
                      reads=[b_x[s]], writes=[b_junk, b_st])
                rstd_ops(st, col, b_st)
                kb.op("dve", lambda e: e.scalar_tensor_tensor(out=hb[:, s, :], in0=xblk[:, s, :],
                                                              scalar=st[:, 8 + col:9 + col], in1=gbc[:, :],
                                                              op0=ALU.mult, op1=ALU.mult),
                      reads=[b_x[s], b_st, b_g], writes=[b_h[s]])
                pend[(i, b)] = s

            def norm_post(i, b):
                s = pend.pop((i, b))
                for k in range(8):
                    kb.op("pe", lambda e, k=k: e.transpose(out=pT[:, s, k, :], in_=hb[:, s, k * 128:(k + 1) * 128],
                                                           identity=ident[:, :]),
                          reads=[b_h[s], b_ident], writes=[b_pT[s]], sig=(k == 7))
                kb.op("act", lambda e: e.copy(out=hT[:, :, b * 128:(b + 1) * 128], in_=pT[:, s, :, :]),
                      reads=[b_pT[s]], writes=[b_hT4[b]])

            gu_ctr = [0]

            def gate_up(fc):
                n = gu_ctr[0]
                gu_ctr[0] += 1
                s = n % 2
                g = 0 if fc < 6 else (1 if fc < 12 else (2 if fc < 17 else 3))
                for k in range(8):
                    kb.op("pe", lambda e, k=k: e.matmul(pg[:, s, :], lhsT=wg[:, k, fc * 128:(fc + 1) * 128],
                                                        rhs=hT[:, k, :], start=(k == 0), stop=(k == 7)),
                          reads=[b_wg[g][k]] + b_hT4, writes=[b_pg[s]], sig=(k == 7))
                for k in range(8):
                    kb.op("pe", lambda e, k=k: e.matmul(pu[:, s, :], lhsT=wu[:, k, fc * 128:(fc + 1) * 128],
                                                        rhs=hT[:, k, :], start=(k == 0), stop=(k == 7)),
                          reads=[b_wu[g][k]] + b_hT4, writes=[b_pu[s]], sig=(k == 7))
                kb.op("act", lambda e: e.activation(out=sg[:, s, :], in_=pg[:, s, :], func=AF.Silu),
                      reads=[b_pg[s]], writes=[b_sg[s]])
                kb.op("dve", lambda e: e.tensor_tensor(out=actb[:, fc, :], in0=pu[:, s, :], in1=sg[:, s, :],
                                                       op=ALU.mult),
                      reads=[b_pu[s], b_sg[s]], writes=[b_act[fc]])

            dn_ctr = [0]

            def down_block(i, b):
                n = dn_ctr[0]
                dn_ctr[0] += 1
                s = n % 2
                r0 = i * TT + b * 128
                kb.dma("sp", s_xr[s], xr[:, s, :], src[r0:r0 + 128, :], reads=[], writes=[b_xr[s]])
                for half in range(2):
                    for f in range(NFC):
                        kb.op("pe", lambda e, f=f: e.matmul(po[:, half, :], lhsT=actb[:, f, b * 128:(b + 1) * 128],
                                                            rhs=wd[:, f, half * 512:(half + 1) * 512],
                                                            start=(f == 0), stop=(f == NFC - 1)),
                              reads=[b_act[f], b_wd[f]], writes=[b_po[half]], sig=(f == NFC - 1))
                    kb.op("dve", lambda e: e.scalar_tensor_tensor(out=xr[:, s, half * 512:(half + 1) * 512],
                                                                  in0=po[:, half, :], scalar=0.5,
                                                                  in1=xr[:, s, half * 512:(half + 1) * 512],
                                                                  op0=ALU.mult, op1=ALU.add),
                          reads=[b_po[half], b_xr[s]], writes=[b_xr[s]])
                if final:
                    col = 4 + (n % 4)
                    kb.op("act", lambda e: e.activation(out=junk[:, :], in_=xr[:, s, :], func=AF.Square,
                                                        accum_out=st[:, col:col + 1]),
                          reads=[b_xr[s]], writes=[b_junk, b_st])
                    rstd_ops(st, col, b_st)
                    kb.op("dve", lambda e: e.scalar_tensor_tensor(out=xr[:, s, :], in0=xr[:, s, :],
                                                                  scalar=st[:, 8 + col:9 + col], in1=gfin[:, :],
                                                                  op0=ALU.mult, op1=ALU.mult),
                          reads=[b_xr[s], b_st, b_gf], writes=[b_xr[s]])
                kb.dma("sp", s_xo[s], dst[r0:r0 + 128, :], xr[:, s, :], reads=[b_xr[s]], writes=[])

            for b in range(4):
                norm_block(0, b)
                norm_post(0, b)
            for i in range(NT):
                for fc in range(NFC):
                    gate_up(fc)
                    if i + 1 < NT and fc == NFC - 2:
                        norm_block(i + 1, 0)
                for b in range(4):
                    if i + 1 < NT and b + 1 < 4:
                        norm_block(i + 1, b + 1)
                    down_block(i, b)
                    if i + 1 < NT:
                        norm_post(i + 1, b)
            kb.barrier()


    NSPAN = max(1, S // 2048)
    SPAN = min(S, 2048)

    def rope_tables():
        kb.barrier()
        CH = 2048 if S >= 2048 else S
        TWO_PI = 2.0 * math.pi
        C1 = 6.28125
        C2 = TWO_PI - C1
        PI_LO = 3.1415925
        MAGIC = 12582912.0
        with (nc.sbuf_tensor("rp_pos", [128, CH], I32) as pos_i,
              nc.sbuf_tensor("rp_ang", [128, CH], F32) as ang,
              nc.sbuf_tensor("rp_k", [128, CH], F32) as kk,
              nc.sbuf_tensor("rp_r", [128, CH], F32) as rr,
              nc.sbuf_tensor("rp_r2", [128, CH], F32) as r2,
              nc.sbuf_tensor("rp_o", [128, 2, CH], F32) as oo):
            b_pos, b_ang, b_k, b_r, b_r2, b_o = (Buf(n) for n in ("pos", "ang", "k", "r", "r2", "o"))
            s_pos = kb.dma_sem("rp_pos")
            s_o = kb.dma_sem("rp_o")
            for ci in range(S // CH):
                t0 = ci * CH
                kb.dma("sp", s_pos, pos_i[:, :], positions[t0:t0 + CH].partition_broadcast(128), writes=[b_pos])
                kb.op("dve", lambda e: e.tensor_copy(ang[:, :], pos_i[:, :]), reads=[b_pos], writes=[b_ang])
                kb.op("dve", lambda e: e.tensor_scalar_mul(out=ang[:, :], in0=ang[:, :], scalar1=ropec[:, 0:1]),
                      reads=[b_ang, b_ropec], writes=[b_ang])
                kb.op("dve", lambda e: e.tensor_scalar(out=kk[:, :], in0=ang[:, :], scalar1=1.0 / TWO_PI,
                                                       scalar2=MAGIC, op0=ALU.mult, op1=ALU.add),
                      reads=[b_ang], writes=[b_k])
                kb.op("dve", lambda e: e.tensor_scalar_add(out=kk[:, :], in0=kk[:, :], scalar1=-MAGIC),
                      reads=[b_k], writes=[b_k])
                kb.op("dve", lambda e: e.scalar_tensor_tensor(out=rr[:, :], in0=kk[:, :], scalar=-C1, in1=ang[:, :],
                                                              op0=ALU.mult, op1=ALU.add),
                      reads=[b_k, b_ang], writes=[b_r])
                kb.op("dve", lambda e: e.scalar_tensor_tensor(out=rr[:, :], in0=kk[:, :], scalar=-C2, in1=rr[:, :],
                                                              op0=ALU.mult, op1=ALU.add),
                      reads=[b_k, b_r], writes=[b_r])
                kb.op("dve", lambda e: e.tensor_scalar(out=r2[:, :], in0=rr[:, :], scalar1=math.pi / 2,
                                                       scalar2=-TWO_PI, op0=ALU.is_gt, op1=ALU.mult),
                      reads=[b_r], writes=[b_r2])
                kb.op("dve", lambda e: e.scalar_tensor_tensor(out=r2[:, :], in0=rr[:, :], scalar=math.pi / 2,
                                                              in1=r2[:, :], op0=ALU.add, op1=ALU.add),
                      reads=[b_r, b_r2], writes=[b_r2])
                kb.op("dve", lambda e: e.tensor_scalar(out=rr[:, :], in0=rr[:, :], scalar1=PI_LO, scalar2=-PI_LO,
                                                       op0=ALU.min, op1=ALU.max), reads=[b_r], writes=[b_r])
                kb.op("dve", lambda e: e.tensor_scalar(out=r2[:, :], in0=r2[:, :], scalar1=PI_LO, scalar2=-PI_LO,
                                                       op0=ALU.min, op1=ALU.max), reads=[b_r2], writes=[b_r2])
                kb.op("act", lambda e: e.activation(out=oo[:, 0, :], in_=r2[:, :], func=AF.Sin),
                      reads=[b_r2], writes=[b_o])
                kb.op("act", lambda e: e.activation(out=oo[:, 1, :], in_=rr[:, :], func=AF.Sin,
                                                    scale=ropec[:, 1:2]),
                      reads=[b_r, b_ropec], writes=[b_o])
                kb.dma("sp", s_o, cosT[:, t0:t0 + CH], oo[:, 0, :], reads=[b_o], writes=[b_cs])
                kb.dma("sp", s_o, sinT[:, t0:t0 + CH], oo[:, 1, :], reads=[b_o], writes=[b_cs])
                kb.group([b_cs])
        kb.barrier()

    def proj_phase(l, src):
        kb.barrier()
        u = "_m1_%d" % l
        with (nc.sbuf_tensor("win" + u, [128, 8, PROJ], BF16) as win,
              nc.sbuf_tensor("qb" + u, [128, 2, TT], BF16) as qb,
              nc.sbuf_tensor("gbc" + u, [128, D], F32) as gbc,
              nc.sbuf_tensor("xblk" + u, [128, 2, D], F32) as xblk,
              nc.sbuf_tensor("hb" + u, [128, 2, D], BF16) as hb,
              nc.sbuf_tensor("hT" + u, [128, 8, TT], BF16) as hT,
              nc.sbuf_tensor("junk" + u, [128, D], BF16) as junk,
              nc.sbuf_tensor("st" + u, [128, 16], F32) as st,
              nc.sbuf_tensor("cs" + u, [128, 2, 2, TT], F32) as cs,
              nc.sbuf_tensor("t1" + u, [128, 2, TT], F32) as t1,
              nc.sbuf_tensor("t2" + u, [128, 2, TT], F32) as t2,
              nc.sbuf_tensor("qks" + u, [128, 2, 16, TT], BF16) as qks,
              nc.sbuf_tensor("vds" + u, [128, 2, 4, 512], BF16) as vds,
              nc.sbuf_tensor("vas" + u, [128, 4, 4, 2, 128], BF16) as vas,
              nc.psum_tensor("pT" + u, [128, 2, 8, 128], BF16) as pT,
              nc.psum_tensor("pq" + u, [128, 2, 512], F32) as pq,
              nc.psum_tensor("pr" + u, [128, 2, 512], F32) as pr,
              nc.psum_tensor("pv" + u, [128, 2, 512], F32) as pv):
            b_win = [[Buf("win%d_%d" % (g, k)) for k in range(8)] for g in range(6)]
            b_g = Buf("g")
            b_x = [Buf("x0"), Buf("x1")]
            s_x = [kb.dma_sem("x0"), kb.dma_sem("x1")]
            b_h = [Buf("h0"), Buf("h1")]
            b_hT4 = [Buf("hT%d" % b) for b in range(4)]
            b_st = Buf("st")
            b_junk = Buf("junk")
            b_pT = [Buf("pT0"), Buf("pT1")]
            b_pq = [Buf("pq0"), Buf("pq1")]
            b_pr = [Buf("pr0"), Buf("pr1")]
            b_pv = [Buf("pv0"), Buf("pv1")]
            b_csb = [Buf("cs0"), Buf("cs1")]
            s_cs = [kb.dma_sem("cs0"), kb.dma_sem("cs1")]
            b_t1 = [Buf("t10"), Buf("t11")]
            b_t2 = [Buf("t20"), Buf("t21")]
            b_qks = [Buf("qks0"), Buf("qks1")]
            s_qks = [kb.dma_sem("qks0"), kb.dma_sem("qks1")]
            b_vds = [Buf("vds0"), Buf("vds1")]
            s_vds = [kb.dma_sem("vds0"), kb.dma_sem("vds1")]
            b_vas = Buf("vas")
            s_vas = kb.dma_sem("vas")
            s_g = kb.dma_sem("g")
            kb.dma("sp", s_g, gbc[:, :], w["mix_norm"][l, :].partition_broadcast(128), writes=[b_g])
            for g in (0, 1, 3, 4, 2, 5):
                s_wi = kb.dma_sem("wgs%d" % (g % 4) if g < 4 else "wus%d" % (g % 4))
                for k in range(8):
                    kb.dma("pool", s_wi, win[:, k, g * 512:(g + 1) * 512],
                           w["w_in"][l, k * 128:(k + 1) * 128, g * 512:(g + 1) * 512], writes=[b_win[g][k]])
                kb.group(b_win[g])
            kb.op("pool", lambda e: e.memset(vas[:, :, :, :, :], 0.0), writes=[b_vas])
            blk_ctr = [0]
            pend = {}

            def norm_block(i, b):
                n = blk_ctr[0]
                blk_ctr[0] += 1
                s = n % 2
                r0 = i * TT + b * 128
                kb.dma("sp", s_x[s], xblk[:, s, :], src[r0:r0 + 128, :], writes=[b_x[s]])
                col = (n % 8)
                kb.op("act", lambda e: e.activation(out=junk[:, :], in_=xblk[:, s, :], func=AF.Square,
                                                    accum_out=st[:, col:col + 1]),
                      reads=[b_x[s]], writes=[b_junk, b_st])
                rstd_ops(st, col, b_st)
                kb.op("dve", lambda e: e.scalar_tensor_tensor(out=hb[:, s, :], in0=xblk[:, s, :],
                                                              scalar=st[:, 8 + col:9 + col], in1=gbc[:, :],
                                                              op0=ALU.mult, op1=ALU.mult),
                      reads=[b_x[s], b_st, b_g], writes=[b_h[s]])
                pend[(i, b)] = s

            def norm_post(i, b):
                s = pend.pop((i, b))
                for k in range(8):
                    kb.op("pe", lambda e, k=k: e.transpose(out=pT[:, s, k, :], in_=hb[:, s, k * 128:(k + 1) * 128],
                                                           identity=ident[:, :]),
                          reads=[b_h[s], b_ident], writes=[b_pT[s]], sig=(k == 7))
                kb.op("act", lambda e: e.copy(out=hT[:, :, b * 128:(b + 1) * 128], in_=pT[:, s, :, :]),
                      reads=[b_pT[s]], writes=[b_hT4[b]])

            b_qb = [Buf("qb0"), Buf("qb1")]

            def qk_mm(i, ci):
                s = ci % 2
                c0 = ci * 128 if ci < 8 else 1536 + (ci - 8) * 128
                for k in range(8):
                    kb.op("pe", lambda e, k=k: e.matmul(pq[:, s, :], lhsT=win[:, k, c0:c0 + 128], rhs=hT[:, k, :],
                                                        start=(k == 0), stop=(k == 7)),
                          reads=[b_win[c0 // 512][k]] + b_hT4, writes=[b_pq[s]], sig=(k == 7))
                kb.op("act", lambda e: e.copy(out=qb[:, s, :], in_=pq[:, s, :]), reads=[b_pq[s]], writes=[b_qb[s]])

            def qk_rot(i, ci, ts):
                s = ci % 2
                kb.op("pe", lambda e: e.matmul(pr[:, s, :], lhsT=prot[:, :], rhs=qb[:, s, :], start=True, stop=True),
                      reads=[b_qb[s], b_mask], writes=[b_pr[s]])
                kb.op("dve", lambda e: e.tensor_tensor(out=t1[:, s, :], in0=pq[:, s, :], in1=cs[:, ts, 0, :],
                                                       op=ALU.mult),
                      reads=[b_pq[s], b_csb[ts], b_qb[s]], writes=[b_t1[s]])
                kb.op("dve", lambda e: e.tensor_tensor(out=t2[:, s, :], in0=pr[:, s, :], in1=cs[:, ts, 1, :],
                                                       op=ALU.mult),
                      reads=[b_pr[s], b_csb[ts]], writes=[b_t2[s]])
                kb.op("dve", lambda e: e.tensor_tensor(out=qks[:, ts, ci, :], in0=t1[:, s, :], in1=t2[:, s, :],
                                                       op=ALU.add),
                      reads=[b_t1[s], b_t2[s]], writes=[b_qks[ts]])

            vc = [0]

            def v_block(i, b, ts):
                r0 = i * TT + b * 128
                s = vc[0] % 2
                vc[0] += 1
                for k in range(8):
                    kb.op("pe", lambda e, k=k: e.matmul(pv[:, s, :], lhsT=hT[:, k, b * 128:(b + 1) * 128],
                                                        rhs=win[:, k, 1024:1536], start=(k == 0), stop=(k == 7)),
                          reads=[b_win[2][k], b_hT4[b]], writes=[b_pv[s]], sig=(k == 7))
                kb.op("act", lambda e: e.copy(out=vds[:, ts, b, :], in_=pv[:, s, :]),
                      reads=[b_pv[s]], writes=[b_vds[ts]])
                s = vc[0] % 2
                vc[0] += 1
                for k in range(8):
                    kb.op("pe", lambda e, k=k: e.matmul(pv[:, s, :], lhsT=hT[:, k, b * 128:(b + 1) * 128],
                                                        rhs=win[:, k, 2560:3072], start=(k == 0), stop=(k == 7)),
                          reads=[b_win[5][k], b_hT4[b]], writes=[b_pv[s]], sig=(k == 7))
                pvv = pv[:, s, :].rearrange("p (j t d) -> p j t d", t=2, d=64)
                kb.op("act", lambda e: e.copy(out=vas[:, b, :, 0, 0:64], in_=pvv[:, :, 0, :]),
                      reads=[b_pv[s]], writes=[b_vas])
                kb.op("act", lambda e: e.copy(out=vas[:, b, :, 1, 64:128], in_=pvv[:, :, 1, :]),
                      reads=[b_pv[s]], writes=[b_vas])

            for b in range(4):
                norm_block(0, b)
                norm_post(0, b)
            def csload(i):
                ts = i % 2
                c0 = i * TT
                kb.dma("sp", s_cs[ts], cs[:, ts, 0, :], cosT[:, c0:c0 + TT], reads=[b_cs], writes=[b_csb[ts]])
                kb.dma("sp", s_cs[ts], cs[:, ts, 1, :], sinT[:, c0:c0 + TT], reads=[b_cs], writes=[b_csb[ts]])
                kb.group([b_csb[ts]])

            csload(0)
            for i in range(NT):
                ts = i % 2
                c0 = i * TT
                if i + 1 < NT:
                    csload(i + 1)
                qk_mm(i, 0)
                for ci in range(16):
                    if ci + 1 < 16:
                        qk_mm(i, ci + 1)
                    if ci < 15:
                        qk_rot(i, ci, ts)
                if i + 1 < NT:
                    norm_block(i + 1, 0)
                for b in range(4):
                    if i + 1 < NT and b + 1 < 4:
                        norm_block(i + 1, b + 1)
                    v_block(i, b, ts)
                    if b == 0:
                        qk_rot(i, 15, ts)
                        kb.dma("sp", s_qks[ts], qkT[:, :, c0:c0 + TT].rearrange("c p t -> p c t"), qks[:, ts, :, :],
                               reads=[b_qks[ts]], writes=[b_qkT])
                    if i + 1 < NT:
                        norm_post(i + 1, b)
                kb.dma("sp", s_vds[ts], vd[c0:c0 + TT, :].rearrange("(b p) e -> p b e", p=128), vds[:, ts, :, :],
                       reads=[b_vds[ts]], writes=[b_vd])
                kb.dma("sp", s_vas, va[c0:c0 + TT, :, :, :].rearrange("(b p) j t e -> p b (j t e)", p=128),
                       vas[:, :, :, :, :].rearrange("p b j t e -> p b (j t e)"),
                       reads=[b_vas], writes=[b_va])
            kb.barrier()

    def lam_ops(l, lam_init, lamt, b_lam):
        u = "_lam_%d" % l
        with (nc.sbuf_tensor("lv" + u, [128, 4, 64], F32) as lv,
              nc.sbuf_tensor("lt" + u, [128, 8], F32) as lt):
            b_lv = Buf("lv")
            s_lv = kb.dma_sem("lv")
            for i, nmv in enumerate(("lambda_q1", "lambda_k1", "lambda_q2", "lambda_k2")):
                kb.dma("sp", s_lv, lv[:, i, :], w[nmv][l, :].partition_broadcast(128), writes=[b_lv])
            kb.group([b_lv])
            kb.op("dve", lambda e: e.tensor_tensor(out=lv[:, 0, :], in0=lv[:, 0, :], in1=lv[:, 1, :], op=ALU.mult),
                  reads=[b_lv], writes=[b_lv])
            kb.op("dve", lambda e: e.tensor_tensor(out=lv[:, 2, :], in0=lv[:, 2, :], in1=lv[:, 3, :], op=ALU.mult),
                  reads=[b_lv], writes=[b_lv])
            kb.op("dve", lambda e: e.reduce_sum(out=lt[:, 0:1], in_=lv[:, 0, :], axis=AX.X), reads=[b_lv], writes=[b_lam])
            kb.op("dve", lambda e: e.reduce_sum(out=lt[:, 1:2], in_=lv[:, 2, :], axis=AX.X), reads=[b_lv], writes=[b_lam])
            kb.op("act", lambda e: e.activation(out=lt[:, 2:4], in_=lt[:, 0:2], func=AF.Exp), reads=[b_lam], writes=[b_lam])
            kb.op("dve", lambda e: e.scalar_tensor_tensor(out=lamt[:, 0:1], in0=lt[:, 3:4], scalar=-float(lam_init),
                                                          in1=lt[:, 2:3], op0=ALU.add, op1=ALU.subtract),
                  reads=[b_lam], writes=[b_lam])
            kb.barrier()

    def diff_phase(l, lam_init):
        kb.barrier()
        u = "_m2a_%d" % l
        NB = S // 128
        NQT = S // 512
        LOOK = 2
        with (nc.sbuf_tensor("kT" + u, [128, 2, S], BF16) as kT,
              nc.sbuf_tensor("v1" + u, [128, 2, NB, 130], BF16) as v1,
              nc.sbuf_tensor("qT" + u, [128, 2, 512], BF16) as qT,
              nc.sbuf_tensor("PT" + u, [128, 3, 2, 512], BF16) as PT,
              nc.sbuf_tensor("accs" + u, [128, 4, 3, 512], F32) as accs,
              nc.sbuf_tensor("lamt" + u, [128, 8], F32) as lamt,
              nc.sbuf_tensor("gsub" + u, [128, 128], F32) as gsub,
              nc.sbuf_tensor("fst" + u, [128, 4, 16], F32) as fst,
              nc.sbuf_tensor("fo" + u, [128, 4, 4, 128], F32) as fo,
              nc.sbuf_tensor("fj" + u, [128, 128], F32) as fj,
              nc.sbuf_tensor("fob" + u, [128, 4, 4, 128], BF16) as fob,
              nc.sbuf_tensor("otr" + u, [128, 2, 512], BF16) as otr,
              nc.psum_tensor("ps" + u, [128, 2, 2, 512], F32) as ps,
              nc.psum_tensor("acc" + u, [128, 3, 512], F32) as acc,
              nc.psum_tensor("ptr" + u, [128, 4, 128], BF16) as ptr):
            b_lam = Buf("lam")
            lam_ops(l, lam_init, lamt, b_lam)
            b_gs = Buf("gsub")
            s_gs = kb.dma_sem("g")
            kb.dma("sp", s_gs, gsub[:, :], w["subln_gain"][l, :].partition_broadcast(128), writes=[b_gs])
            kb.op("dve", lambda e: e.tensor_scalar_mul(out=gsub[:, :], in0=gsub[:, :], scalar1=float(1.0 - lam_init)),
                  reads=[b_gs], writes=[b_gs])
            b_kT = [Buf("kT0"), Buf("kT1")]
            s_kT = [kb.dma_sem("kT0"), kb.dma_sem("kT1")]
            b_v1 = [Buf("v10"), Buf("v11")]
            s_v1 = [kb.dma_sem("v10"), kb.dma_sem("v11")]
            b_qT = [Buf("qT0"), Buf("qT1")]
            s_qT = [kb.dma_sem("qT0"), kb.dma_sem("qT1")]
            b_ps = [Buf("ps0"), Buf("ps1")]
            b_PT = [Buf("PT0"), Buf("PT1"), Buf("PT2")]
            b_acc = Buf("acc")
            b_accs = [Buf("accs%d" % i) for i in range(4)]
            b_fst = [Buf("fst%d" % i) for i in range(4)]
            b_fj, b_ptr = Buf("fj"), Buf("ptr")
            b_fo = [Buf("fo%d" % i) for i in range(4)]
            b_fob = [Buf("fob%d" % i) for i in range(4)]
            b_otr = [Buf("otr0"), Buf("otr1")]
            s_otr = [kb.dma_sem("otr0"), kb.dma_sem("otr1")]
            for hs in range(2):
                kb.op("pool", lambda e, hs=hs: e.memset(v1[:, hs, :, 128:130], 1.0), writes=[b_v1[hs]])

            def acc_view(a):
                return acc[:, a // 3, (a % 3) * 160:(a % 3) * 160 + 129]

            def accs_view(sl, a, lo, hi):
                return accs[:, sl, a // 3, (a % 3) * 160 + lo:(a % 3) * 160 + hi]

            units = []
            for h in range(4):
                for qt in range(NQT):
                    nkb = 4 * qt + 4
                    for kbi in range(nkb):
                        units.append((h, qt, kbi, kbi == 0, kbi == nkb - 1))
            NU = len(units)

            def load_head(h):
                hs = h % 2
                kb.dma("sp", s_kT[hs], kT[:, hs, :], qkT[4 + h, :, :], reads=[b_qkT], writes=[b_kT[hs]])
                kb.dma("sp", s_v1[hs], v1[:, hs, :, 0:128],
                       vd[:, h * 128:(h + 1) * 128].rearrange("(b p) e -> p b e", p=128),
                       reads=[b_vd], writes=[b_v1[hs]])

            def load_q(gq):
                h, qt = gq // NQT, gq % NQT
                qs = gq % 2
                kb.dma("sp", s_qT[qs], qT[:, qs, :], qkT[h, :, qt * 512:(qt + 1) * 512], reads=[b_qkT],
                       writes=[b_qT[qs]])

            def qk_stage(n):
                h, qt, kbi, first, last = units[n]
                hs = h % 2
                gq = h * NQT + qt
                qs = gq % 2
                if first:
                    if qt == 0:
                        if h == 0:
                            load_head(0)
                            load_q(0)
                    if gq + 1 < 4 * NQT:
                        load_q(gq + 1)
                j = kbi - 4 * qt
                q0 = 128 * j if j > 0 else 0
                s = n % 2
                s3 = n % 3
                for cc in range(2):
                    kb.op("pe", lambda e, cc=cc: e.matmul(ps[:, s, cc, q0:512],
                                                          lhsT=kT[cc * 64:(cc + 1) * 64, hs, kbi * 128:(kbi + 1) * 128],
                                                          rhs=qT[cc * 64:(cc + 1) * 64, qs, q0:512],
                                                          start=True, stop=(j < 0), skip_group_check=True),
                          reads=[b_kT[hs], b_qT[qs]], writes=[b_ps[s]], sig=(cc == 1 and j < 0))
                if j >= 0:
                    for cc in range(2):
                        kb.op("pe", lambda e, cc=cc: e.matmul(ps[:, s, cc, q0:512], lhsT=ident[:, :],
                                                              rhs=negm[:, 0:512 - q0], start=False, stop=True,
                                                              skip_group_check=True),
                              reads=[b_ident, b_mask], writes=[b_ps[s]], sig=(cc == 1))
                kb.op("act", lambda e: e.activation(out=PT[:, s3, :, q0:512], in_=ps[:, s, :, q0:512],
                                                    func=AF.Exp, scale=0.125),
                      reads=[b_ps[s]], writes=[b_PT[s3]])

            def pv_stage(n):
                h, qt, kbi, first, last = units[n]
                hs = h % 2
                j = kbi - 4 * qt
                s3 = n % 3
                if first and qt == 0 and h + 1 < 4:
                    load_head(h + 1)
                for cc in range(2):
                    for jj in range(max(j, 0), 4):
                        a = cc * 4 + jj
                        kb.op("pe", lambda e, cc=cc, jj=jj, a=a: e.matmul(
                            acc_view(a), lhsT=PT[:, s3, cc, jj * 128:(jj + 1) * 128],
                            rhs=v1[:, hs, kbi, 0:129], start=(kbi == 0 and a % 3 == 0),
                            stop=(kbi == 4 * qt + jj), skip_group_check=True),
                              reads=[b_PT[s3], b_v1[hs]], writes=[b_acc], sig=(cc == 1 and jj == 3))
                if last:
                    finalize(h, qt)

            pending = []
            cur_n = [0]

            def finalize(h, qt):
                gq = h * NQT + qt
                sl = gq % 4
                for bk in range(3):
                    kb.op("dve", lambda e, bk=bk: e.tensor_copy(accs[:, sl, bk, :], acc[:, bk, :]),
                          reads=[b_acc], writes=[b_accs[sl]])
                for jj in range(4):
                    kb.op("dve", lambda e, jj=jj: e.reciprocal(out=fst[:, sl, jj:jj + 1],
                                                               in_=accs_view(sl, jj, 128, 129)),
                          reads=[b_accs[sl]], writes=[b_fst[sl]])
                    kb.op("dve", lambda e, jj=jj: e.reciprocal(out=fst[:, sl, 4 + jj:5 + jj],
                                                               in_=accs_view(sl, 4 + jj, 128, 129)),
                          reads=[b_accs[sl]], writes=[b_fst[sl]])
                kb.op("dve", lambda e: e.tensor_scalar_mul(out=fst[:, sl, 4:8], in0=fst[:, sl, 4:8],
                                                           scalar1=lamt[:, 0:1]),
                      reads=[b_fst[sl], b_lam], writes=[b_fst[sl]])
                for jj in range(4):
                    kb.op("dve", lambda e, jj=jj: e.tensor_scalar_mul(out=fo[:, sl, jj, :],
                                                                      in0=accs_view(sl, jj, 0, 128),
                                                                      scalar1=fst[:, sl, jj:jj + 1]),
                          reads=[b_accs[sl], b_fst[sl]], writes=[b_fo[sl]])
                for jj in range(4):
                    kb.op("dve", lambda e, jj=jj: e.scalar_tensor_tensor(out=fo[:, sl, jj, :],
                                                                         in0=accs_view(sl, 4 + jj, 0, 128),
                                                                         scalar=fst[:, sl, 4 + jj:5 + jj],
                                                                         in1=fo[:, sl, jj, :],
                                                                         op0=ALU.mult, op1=ALU.add),
                          reads=[b_accs[sl], b_fst[sl], b_fo[sl]], writes=[b_fo[sl]])
                for jj in range(4):
                    kb.op("dve", lambda e, jj=jj: e.tensor_tensor(out=fj[:, :], in0=fo[:, sl, jj, :],
                                                                  in1=fo[:, sl, jj, :], op=ALU.mult),
                          reads=[b_fo[sl]], writes=[b_fj])
                    kb.op("dve", lambda e, jj=jj: e.reduce_sum(out=fst[:, sl, 8 + jj:9 + jj], in_=fj[:, :], axis=AX.X),
                          reads=[b_fj], writes=[b_fst[sl]])
                kb.op("dve", lambda e: e.tensor_scalar(out=fst[:, sl, 12:16], in0=fst[:, sl, 8:12], scalar1=1.0 / 128,
                                                       scalar2=SUBLN_EPS, op0=ALU.mult, op1=ALU.add),
                      reads=[b_fst[sl]], writes=[b_fst[sl]])

                def stage_act():
                    kb.op("act", lambda e: e.activation(out=fst[:, sl, 12:16], in_=fst[:, sl, 12:16], func=AF.Ln),
                          reads=[b_fst[sl]], writes=[b_fst[sl]])
                    kb.op("act", lambda e: e.activation(out=fst[:, sl, 12:16], in_=fst[:, sl, 12:16], func=AF.Exp,
                                                        scale=-0.5),
                          reads=[b_fst[sl]], writes=[b_fst[sl]])

                def stage_dve():
                    for jj in range(4):
                        kb.op("dve", lambda e, jj=jj: e.scalar_tensor_tensor(
                            out=fob[:, sl, jj, :], in0=fo[:, sl, jj, :], scalar=fst[:, sl, 12 + jj:13 + jj],
                            in1=gsub[:, :], op0=ALU.mult, op1=ALU.mult),
                              reads=[b_fo[sl], b_fst[sl], b_gs], writes=[b_fob[sl]])

                def stage_pe():
                    for jj in range(4):
                        kb.op("pe", lambda e, jj=jj: e.transpose(out=ptr[:, jj, :], in_=fob[:, sl, jj, :],
                                                                 identity=ident[:, :]),
                              reads=[b_fob[sl], b_ident], writes=[b_ptr], sig=(jj == 3))

                def stage_out():
                    osl = gq % 2
                    kb.op("dve", lambda e: e.tensor_copy(otr[:, osl, :], ptr[:, :, :]),
                          reads=[b_ptr], writes=[b_otr[osl]])
                    kb.dma("sp", s_otr[osl], mT[h, :, qt * 512:(qt + 1) * 512], otr[:, osl, :],
                           reads=[b_otr[osl]], writes=[b_mT])

                n0 = cur_n[0]
                pending.append((n0 + 10, stage_act))
                pending.append((n0 + 12, stage_dve))
                pending.append((n0 + 14, stage_pe))
                pending.append((n0 + 16, stage_out))

            def flush(upto):
                while pending and (upto is None or pending[0][0] <= upto):
                    pending.pop(0)[1]()

            for n in range(NU + LOOK):
                cur_n[0] = n
                if n < NU:
                    qk_stage(n)
                if n - LOOK >= 0:
                    pv_stage(n - LOOK)
                flush(n)
            flush(None)
            kb.barrier()

    def dil_phase(l):
        kb.barrier()
        u = "_m2b_%d" % l
        LOOK = 2
        import os
        DIL = tuple(int(t) for t in os.environ.get("KDIL", "1,4,16").split(","))
        with (nc.sbuf_tensor("dk" + u, [128, 3, SPAN], BF16) as dk,
              nc.sbuf_tensor("dq" + u, [128, 2, SPAN], BF16) as dq,
              nc.sbuf_tensor("dkg" + u, [128, 2, 3, SPAN], BF16) as dkg,
              nc.sbuf_tensor("dqg" + u, [128, 2, 2, SPAN], BF16) as dqg,
              nc.sbuf_tensor("vg" + u, [128, 3, 3, 16, 256], BF16) as vg,
              nc.sbuf_tensor("dPT" + u, [128, 3, 2, 2, 128], BF16) as dPT,
              nc.sbuf_tensor("an" + u, [128, 2, SPAN], F32) as an,
              nc.sbuf_tensor("ad" + u, [128, 2, SPAN], F32) as ad,
              nc.psum_tensor("dps" + u, [128, 2, 2, 512], F32) as dps,
              nc.psum_tensor("dpo" + u, [128, 2, 512], F32) as dpo):
            b_dk = [Buf("dk%d" % i) for i in range(3)]
            s_dk = [kb.dma_sem("dk%d" % i) for i in range(3)]
            b_dq = [Buf("dq0"), Buf("dq1")]
            s_dq = [kb.dma_sem("dq0"), kb.dma_sem("dq1")]
            b_dkg = [[Buf("dkg%d%d" % (gi, sl)) for sl in range(3)] for gi in range(2)]
            b_dqg = [[Buf("dqg%d%d" % (gi, sl)) for sl in range(2)] for gi in range(2)]
            b_vg = [[Buf("vg%d%d" % (bi, sl)) for sl in range(3)] for bi in range(3)]
            s_vg = [[kb.dma_sem("vg%d%d" % (bi, sl)) for sl in range(3)] for bi in range(3)]
            b_dps = [Buf("dps0"), Buf("dps1")]
            b_dPT = [Buf("dPT0"), Buf("dPT1"), Buf("dPT2")]
            b_dpo = [Buf("dpo0"), Buf("dpo1")]
            b_an = [Buf("an0"), Buf("an1")]
            s_an = [kb.dma_sem("an0"), kb.dma_sem("an1")]
            b_ad = [Buf("ad0"), Buf("ad1")]
            iters = [(j, sp_) for j in range(4) for sp_ in range(NSPAN)]
            units = []
            for it, (j, sp_) in enumerate(iters):
                lst = []
                for bi, d in enumerate(DIL):
                    nbl = SPAN // (128 * d)
                    for r in range(d):
                        for bl in range(nbl):
                            lst.append([it, bi, d, r, bl, False, False])
                lst[0][5] = True
                lst[-1][6] = True
                units += lst
            NU = len(units)

            def loads(it):
                j, sp_ = iters[it]
                sl = it % 3
                ql = it % 2
                base = sp_ * SPAN
                kb.dma("sp", s_dk[sl], dk[:, sl, :], qkT[12 + j, :, base:base + SPAN], reads=[b_qkT],
                       writes=[b_dk[sl]])
                kb.dma("sp", s_dq[ql], dq[:, ql, :], qkT[8 + j, :, base:base + SPAN], reads=[b_qkT],
                       writes=[b_dq[ql]])
                for bi, d in enumerate(DIL):
                    nbl = SPAN // (128 * d)
                    srcv = va[base:base + SPAN, j, :, :].rearrange("(bl i r) t e -> i r bl (t e)", i=128, r=d)
                    dstv = vg[:, bi, sl, 0:d * nbl, :].rearrange("p (r bl) e -> p r bl e", r=d)
                    for r in range(d):
                        kb.dma("sp", s_vg[bi][sl], dstv[:, r, :, :], srcv[:, r, :, :], reads=[b_va],
                               writes=[b_vg[bi][sl]])
                    kb.group([b_vg[bi][sl]])
                for bi, d in enumerate(DIL):
                    if d == 1:
                        continue
                    gi = 0 if d == 4 else 1
                    kb.op("pool", lambda e, gi=gi, d=d: e.tensor_copy(
                        dkg[:, gi, sl, :].rearrange("p (r i) -> p r i", r=d),
                        dk[:, sl, :].rearrange("p (i r) -> p r i", r=d)),
                          reads=[b_dk[sl]], writes=[b_dkg[gi][sl]])
                    kb.op("pool", lambda e, gi=gi, d=d: e.tensor_copy(
                        dqg[:, gi, ql, :].rearrange("p (r i) -> p r i", r=d),
                        dq[:, ql, :].rearrange("p (i r) -> p r i", r=d)),
                          reads=[b_dq[ql]], writes=[b_dqg[gi][ql]])

            def operands(n):
                it, bi, d, r, bl, first, last = units[n]
                j, sp_ = iters[it]
                sl = it % 3
                pl = (it - 1) % 3
                ql = it % 2
                nbl = SPAN // (128 * d)
                gi = 0 if d == 4 else 1
                goff = r * (SPAN // d) + bl * 128
                o = Ctx()
                o.has_prev = (bl > 0) or (sp_ > 0)
                if d == 1:
                    o.qv = dq[:, ql, goff:goff + 128]
                    o.kcur = dk[:, sl, goff:goff + 128]
                    o.rd_cur = [b_dk[sl], b_dq[ql]]
                else:
                    o.qv = dqg[:, gi, ql, goff:goff + 128]
                    o.kcur = dkg[:, gi, sl, goff:goff + 128]
                    o.rd_cur = [b_dkg[gi][sl], b_dqg[gi][ql]]
                o.vcur = vg[:, bi, sl, r * nbl + bl, :]
                o.rd_vcur = [b_vg[bi][sl]]
                if bl > 0:
                    o.kprev = dk[:, sl, goff - 128:goff] if d == 1 else dkg[:, gi, sl, goff - 128:goff]
                    o.vprev = vg[:, bi, sl, r * nbl + bl - 1, :]
                    o.rd_kprev = []
                    o.rd_vprev = []
                elif sp_ > 0:
                    poff = r * (SPAN // d) + (nbl - 1) * 128
                    o.kprev = dk[:, pl, poff:poff + 128] if d == 1 else dkg[:, gi, pl, poff:poff + 128]
                    o.vprev = vg[:, bi, pl, r * nbl + nbl - 1, :]
                    o.rd_kprev = [b_dk[pl]] if d == 1 else [b_dkg[gi][pl]]
                    o.rd_vprev = [b_vg[bi][pl]]
                o.off = bl * 128 * d + r
                return o

            def qk_stage(n):
                it, bi, d, r, bl, first, last = units[n]
                if n == 0:
                    loads(0)
                o = operands(n)
                s = n % 2
                s3 = n % 3
                ksel = ([0] if o.has_prev else []) + [1]
                k0 = ksel[0]
                for idx, ks in enumerate(ksel):
                    for hh in range(2):
                        kk_ = o.kprev if ks == 0 else o.kcur
                        kb.op("pe", lambda e, hh=hh, ks=ks, kk_=kk_, idx=idx: e.matmul(
                            dps[:, s, hh, ks * 128:(ks + 1) * 128],
                            lhsT=kk_[hh * 64:(hh + 1) * 64, :], rhs=o.qv[hh * 64:(hh + 1) * 64, :],
                            start=(idx == 0), stop=False, skip_group_check=True),
                              reads=o.rd_cur + (o.rd_kprev if ks == 0 else []), writes=[b_dps[s]], sig=False)
                for hh in range(2):
                    kb.op("pe", lambda e, hh=hh: e.matmul(dps[:, s, hh, k0 * 128:256], lhsT=ident[:, :],
                                                          rhs=negd[:, k0 * 128:256], start=False, stop=True,
                                                          skip_group_check=True),
                          reads=[b_ident, b_mask], writes=[b_dps[s]], sig=(hh == 1))
                kb.op("act", lambda e: e.activation(out=dPT[:, s3, :, k0:2, :],
                                                    in_=dps[:, s, :, k0 * 128:256].rearrange(
                                                        "p h (k q) -> p h k q", q=128),
                                                    func=AF.Exp, scale=0.125),
                      reads=[b_dps[s]], writes=[b_dPT[s3]])

            def pv_stage(n):
                it, bi, d, r, bl, first, last = units[n]
                j, sp_ = iters[it]
                if first and it + 1 < len(iters):
                    loads(it + 1)
                o = operands(n)
                s = n % 2
                s3 = n % 3
                asl = it % 2
                ksel = ([0] if o.has_prev else []) + [1]
                firstmm = True
                for hh in range(2):
                    for ks in ksel:
                        vv = o.vprev if ks == 0 else o.vcur
                        kb.op("pe", lambda e, hh=hh, ks=ks, vv=vv, firstmm=firstmm: e.matmul(
                            dpo[:, s, 0:128], lhsT=vv[:, hh * 128:(hh + 1) * 128],
                            rhs=dPT[:, s3, hh, ks, :], start=firstmm, stop=False, skip_group_check=True),
                              reads=[b_dPT[s3]] + (o.rd_vprev if ks == 0 else o.rd_vcur), writes=[b_dpo[s]],
                              sig=False)
                        firstmm = False
                for hh in range(2):
                    for ks in ksel:
                        kb.op("pe", lambda e, hh=hh, ks=ks: e.matmul(
                            dpo[:, s, 128:256], lhsT=eones[:, hh, :],
                            rhs=dPT[:, s3, hh, ks, :], start=False, stop=False, skip_group_check=True),
                              reads=[b_dPT[s3], b_mask], writes=[b_dpo[s]], sig=(hh == 1 and ks == 1))
                off = o.off
                anv = an[:, asl, off:off + 127 * d + 1:d]
                adv = ad[:, asl, off:off + 127 * d + 1:d]
                if bi == 0:
                    kb.op("dve", lambda e: e.tensor_copy(anv, dpo[:, s, 0:128]),
                          reads=[b_dpo[s]], writes=[b_an[asl]])
                    kb.op("dve", lambda e: e.tensor_copy(adv, dpo[:, s, 128:256]),
                          reads=[b_dpo[s]], writes=[b_ad[asl]])
                else:
                    kb.op("dve", lambda e: e.tensor_tensor(out=anv, in0=dpo[:, s, 0:128], in1=anv, op=ALU.add),
                          reads=[b_dpo[s], b_an[asl]], writes=[b_an[asl]])
                    kb.op("dve", lambda e: e.tensor_tensor(out=adv, in0=dpo[:, s, 128:256], in1=adv, op=ALU.add),
                          reads=[b_dpo[s], b_ad[asl]], writes=[b_ad[asl]])
                if last:
                    base = sp_ * SPAN
                    kb.op("dve", lambda e: e.reciprocal(out=ad[:, asl, :], in_=ad[:, asl, :]),
                          reads=[b_ad[asl]], writes=[b_ad[asl]])
                    kb.op("pool", lambda e: e.tensor_tensor(out=an[:, asl, :], in0=an[:, asl, :], in1=ad[:, asl, :],
                                                            op=ALU.mult),
                          reads=[b_ad[asl], b_an[asl]], writes=[b_an[asl]])
                    kb.dma("sp", s_an[asl], aT[j, :, base:base + SPAN], an[:, asl, :], reads=[b_an[asl]],
                           writes=[b_aT])

            for n in range(NU + LOOK):
                if n < NU:
                    qk_stage(n)
                if n - LOOK >= 0:
                    pv_stage(n - LOOK)
            kb.barrier()

    def wout_phase(l, src, dst):
        kb.barrier()
        u = "_m3_%d" % l
        with (nc.sbuf_tensor("wo" + u, [128, 8, D], BF16) as wo,
              nc.sbuf_tensor("gd" + u, [128, 4], F32) as gd,
              nc.sbuf_tensor("mt" + u, [128, 3, 4, TT], BF16) as mt,
              nc.sbuf_tensor("at" + u, [128, 3, 4, TT], F32) as at,
              nc.sbuf_tensor("sq" + u, [128, 2, TT], BF16) as sq,
              nc.sbuf_tensor("rs" + u, [128, 2, TT], F32) as rs,
              nc.sbuf_tensor("m2" + u, [128, 2, 4, TT], BF16) as m2,
              nc.sbuf_tensor("xr" + u, [128, 2, D], F32) as xr,
              nc.psum_tensor("pss" + u, [128, 2, 512], F32) as pss,
              nc.psum_tensor("po" + u, [128, 2, 512], F32) as po):
            b_wo = [Buf("wo%d" % k) for k in range(8)]
            b_gd = Buf("gd")
            b_mt = [Buf("mt%d" % i) for i in range(3)]
            s_mt = [kb.dma_sem("mt%d" % i) for i in range(3)]
            b_at = [Buf("at%d" % i) for i in range(3)]
            s_at = [kb.dma_sem("at%d" % i) for i in range(3)]
            b_sq = [Buf("sq0"), Buf("sq1")]
            b_rs = [Buf("rs0"), Buf("rs1")]
            b_m2 = [Buf("m20"), Buf("m21")]
            b_xr = [Buf("xr0"), Buf("xr1")]
            s_xr = [kb.dma_sem("xr0"), kb.dma_sem("xr1")]
            s_xo = [kb.dma_sem("xo0"), kb.dma_sem("xo1")]
            b_pss = [Buf("pss0"), Buf("pss1")]
            b_po = [Buf("po0"), Buf("po1")]
            s_w = kb.dma_sem("wgs")
            for k in range(8):
                kb.dma("pool", s_w, wo[:, k, :], w["w_out"][l, k * 128:(k + 1) * 128, :], writes=[b_wo[k]])
            kb.group(b_wo)
            s_g = kb.dma_sem("g")
            kb.dma("sp", s_g, gd[:, :], w["dil_gain"][l, :].rearrange("(j p) -> p j", p=128), writes=[b_gd],
                   allow_slow_non_contiguous=True)
            sqc = [0]
            dn = [0]
            def wloads(i):
                t3 = i % 3
                c0 = i * TT
                kb.dma("sp", s_mt[t3], mt[:, t3, :, :], mT[:, :, c0:c0 + TT].rearrange("c p t -> p c t"),
                       reads=[b_mT], writes=[b_mt[t3]])
                kb.dma("sp", s_at[t3], at[:, t3, :, :], aT[:, :, c0:c0 + TT].rearrange("c p t -> p c t"),
                       reads=[b_aT], writes=[b_at[t3]])

            def chain(i):
                ts = i % 2
                t3 = i % 3
                for j in range(4):
                    s = sqc[0] % 2
                    sqc[0] += 1
                    kb.op("act", lambda e, j=j: e.activation(out=sq[:, s, :], in_=at[:, t3, j, :], func=AF.Square),
                          reads=[b_at[t3]], writes=[b_sq[s]])
                    kb.op("pe", lambda e, j=j: e.matmul(pss[:, ts, :], lhsT=ones_b[:, :], rhs=sq[:, s, :],
                                                        start=(j == 0), stop=(j == 3)),
                          reads=[b_sq[s], b_mask], writes=[b_pss[ts]])
                kb.op("dve", lambda e: e.tensor_scalar(out=rs[:, ts, :], in0=pss[:, ts, :], scalar1=1.0 / 512,
                                                       scalar2=EPS, op0=ALU.mult, op1=ALU.add),
                      reads=[b_pss[ts]], writes=[b_rs[ts]])
                kb.op("act", lambda e: e.activation(out=rs[:, ts, :], in_=rs[:, ts, :], func=AF.Sqrt),
                      reads=[b_rs[ts]], writes=[b_rs[ts]])
                kb.op("dve", lambda e: e.reciprocal(out=rs[:, ts, :], in_=rs[:, ts, :]),
                      reads=[b_rs[ts]], writes=[b_rs[ts]])
                for j in range(4):
                    kb.op("dve", lambda e, j=j: e.scalar_tensor_tensor(out=m2[:, ts, j, :], in0=at[:, t3, j, :],
                                                                       scalar=gd[:, j:j + 1], in1=rs[:, ts, :],
                                                                       op0=ALU.mult, op1=ALU.mult),
                          reads=[b_at[t3], b_gd, b_rs[ts]], writes=[b_m2[ts]])

            def main(i):
                ts = i % 2
                t3 = i % 3
                c0 = i * TT
                for b in range(4):
                    n = dn[0]
                    dn[0] += 1
                    s = n % 2
                    r0 = c0 + b * 128
                    kb.dma("sp", s_xr[s], xr[:, s, :], src[r0:r0 + 128, :], writes=[b_xr[s]])
                    for half in range(2):
                        for cch in range(8):
                            lhs = mt[:, t3, cch, b * 128:(b + 1) * 128] if cch < 4 else \
                                m2[:, ts, cch - 4, b * 128:(b + 1) * 128]
                            kb.op("pe", lambda e, cch=cch, lhs=lhs: e.matmul(
                                po[:, half, :], lhsT=lhs, rhs=wo[:, cch, half * 512:(half + 1) * 512],
                                start=(cch == 0), stop=(cch == 7)),
                                  reads=[b_wo[cch], b_mt[t3], b_m2[ts]], writes=[b_po[half]], sig=(cch == 7))
                        kb.op("dve", lambda e: e.tensor_tensor(out=xr[:, s, half * 512:(half + 1) * 512],
                                                               in0=po[:, half, :],
                                                               in1=xr[:, s, half * 512:(half + 1) * 512], op=ALU.add),
                              reads=[b_po[half], b_xr[s]], writes=[b_xr[s]])
                    kb.dma("sp", s_xo[s], dst[r0:r0 + 128, :], xr[:, s, :], reads=[b_xr[s]], writes=[])

            wloads(0)
            if NT > 1:
                wloads(1)
            chain(0)
            for i in range(NT):
                if i + 2 < NT:
                    wloads(i + 2)
                if i + 1 < NT:
                    chain(i + 1)
                main(i)
            kb.barrier()

    cur = x_in if first else xs
    def on(p):
        return phases is None or p in phases
    if on("rope"):
        rope_tables()
    for li, l in enumerate(layers):
        is_last = last and (li == len(layers) - 1)
        lam_init = 0.8 - 0.6 * math.exp(-0.3 * l)
        if on("ffn1"):
            ffn_phase(l, 1, cur, xs, False)
            cur = xs
        if on("proj"):
            proj_phase(l, cur)
        if on("diff"):
            diff_phase(l, lam_init)
        if on("dil"):
            dil_phase(l)
        if on("wout"):
            wout_phase(l, cur, xs)
            cur = xs
        if on("ffn2"):
            ffn_phase(l, 2, cur, y_out if is_last else xs, is_last)
            cur = xs
    kb.finish()
    c.n_inst = kb.n_inst
    return nc, c


def make_consts():
    p = np.arange(128)
    inv = (1.0 / (10000.0 ** (np.arange(0, 64, 2, dtype=np.float32) / np.float32(64)))).astype(np.float32)
    ropec = np.zeros((128, 2), np.float32)
    ropec[:, 0] = inv[p % 32]
    ropec[:, 1] = np.where((p % 64) < 32, -1.0, 1.0)
    i = p[:, None]
    masks = np.zeros((128, 1536), np.float32)
    masks[:, 0:512] = (np.arange(512)[None, :] >= i)
    cc = np.arange(128)[None, :]
    masks[:, 512:640] = (i >= cc)
    masks[:, 640:768] = (i <= cc)
    masks[:, 768:1536] = np.where(masks[:, 0:768] > 0, 0.0, -30000.0)
    eones = np.zeros((128, 256), np.float32)
    eones[:, 0:64] = 1.0
    eones[:, 128 + 64:256] = 1.0
    prot = np.zeros((128, 128), np.float32)
    prot[p ^ 32, p] = 1.0
    return {"ident": np.eye(128, dtype=np.float32), "ropec": ropec, "masks": masks, "eones": eones, "prot": prot}


W_NAMES = ("ffn1_norm", "ffn1_gate", "ffn1_up", "ffn1_down", "mix_norm", "w_in", "lambda_q1", "lambda_k1",
           "lambda_q2", "lambda_k2", "subln_gain", "dil_gain", "w_out", "ffn2_norm", "ffn2_gate", "ffn2_up",
           "ffn2_down")

_NC_CACHE = {}


def run_layers(x, inputs, layers, first, last, n_cores=8):
    B, S, _ = x.shape
    LT = int(np.asarray(inputs["w_in"]).shape[0])
    key = (S, tuple(layers), first, last, LT)
    if key not in _NC_CACHE:
        _NC_CACHE[key] = build_nc(S, list(layers), n_layers_total=LT, first=first, last=last)[0]
    nc = _NC_CACHE[key]
    consts = make_consts()
    shared = {k: np.ascontiguousarray(np.asarray(inputs[k], dtype=np.float32)) for k in W_NAMES}
    shared["final_norm"] = np.ascontiguousarray(np.asarray(inputs["final_norm"], dtype=np.float32).reshape(1, D))
    shared["positions"] = np.ascontiguousarray(np.asarray(inputs["positions"], dtype=np.int32))
    shared.update(consts)
    in_maps = []
    for b in range(B):
        m = dict(shared)
        m["x"] = np.ascontiguousarray(x[b])
        in_maps.append(m)
    res = run_bass_kernel_spmd(nc, in_maps, core_ids=list(range(B)))
    return np.stack([np.asarray(r["y"]) for r in res.results], axis=0)


def kernel(**inputs):
    x = np.asarray(inputs["x"], dtype=np.float32)
    return run_layers(x, inputs, list(range(DEPTH)), True, True).astype(np.float32)
```

```python
import contextlib
import math
import numpy as np
import concourse.bass as bass
import concourse.mybir as mybir
from concourse.bass_utils import run_bass_kernel_spmd

F32 = mybir.dt.float32
BF16 = mybir.dt.bfloat16
I32 = mybir.dt.int32
AF = mybir.ActivationFunctionType
ALU = mybir.AluOpType
AX = mybir.AxisListType

D = 1024
DFF = 2816
NFC = DFF // 128
DEPTH = 4
PROJ = 3072
EPS = 1e-6
SUBLN_EPS = 1e-5
TT = 512


class Buf:
    __slots__ = ("name", "w", "r")

    def __init__(self, name):
        self.name = name
        self.w = None
        self.r = {}


class KB:
    def __init__(self, nc):
        self.nc = nc
        self.eng = {"pe": nc.tensor, "act": nc.scalar, "dve": nc.vector, "pool": nc.gpsimd, "sp": nc.sync}
        self.psem = {}
        self.cnt = {}
        self.seen = {e: {} for e in self.eng}
        self.sems = []
        for e in ("pe", "act", "dve", "pool"):
            self.psem[e] = self.new_sem("prog_" + e)
            self.cnt[e] = 0
        self.named = {}
        self.unsig = {e: False for e in self.eng}
        self.dcnt = {}
        self.dma_sems = []
        self.n_inst = 0

    def new_sem(self, name):
        cm = self.nc.semaphore(name)
        s = cm.__enter__()
        self.sems.append(cm)
        return s

    def dma_sem(self, name):
        if name in self.named:
            return self.named[name]
        s = self.new_sem("d_" + name)
        self.dcnt[id(s)] = 0
        self.dma_sems.append(s)
        self.named[name] = s
        return s

    def group(self, bufs):
        last = None
        for b in bufs:
            if b.w is not None and (last is None or b.w[1] > last[1]):
                last = b.w
        for b in bufs:
            b.w = last

    def _wait(self, e, tok):
        if tok is None:
            return
        sem, val, src = tok
        if src == e and e == "pe":
            return
        seen = self.seen[e]
        if seen.get(id(sem), 0) >= val:
            return
        seen[id(sem)] = val
        self.eng[e].wait_ge(sem, val)
        self.n_inst += 1

    def _deps(self, e, reads, writes):
        for b in reads:
            self._wait(e, b.w)
        for b in writes:
            self._wait(e, b.w)
            for t in b.r.values():
                self._wait(e, t)

    def _mark(self, tok, reads, writes):
        for b in reads:
            b.r[id(tok[0])] = tok
        for b in writes:
            b.w = tok
            b.r = {}

    def op(self, e, fn, reads=(), writes=(), sig=True):
        self._deps(e, reads, writes)
        ins = fn(self.eng[e])
        if sig:
            self.cnt[e] += 1
            ins.then_inc(self.psem[e], 1)
            tok = (self.psem[e], self.cnt[e], e)
            self.unsig[e] = False
        else:
            tok = (self.psem[e], self.cnt[e] + 1, e)
            self.unsig[e] = True
        self._mark(tok, reads, writes)
        self.n_inst += 1
        return tok

    def dma(self, e, sem, out, in_, reads=(), writes=(), **kw):
        self._deps(e, reads, writes)
        ins = self.eng[e].dma_start(out=out, in_=in_, **kw)
        self.dcnt[id(sem)] += 16
        ins.then_inc(sem, 16)
        tok = (sem, self.dcnt[id(sem)], "dma")
        self._mark(tok, reads, writes)
        self.n_inst += 1
        return tok

    def barrier(self):
        assert not any(self.unsig.values()), self.unsig
        toks = [(self.psem[e], self.cnt[e], e) for e in self.psem if self.cnt[e] > 0]
        toks += [(s, self.dcnt[id(s)], "dma") for s in self.dma_sems if self.dcnt[id(s)] > 0]
        for e in self.eng:
            for t in toks:
                if t[2] == e:
                    sem, val, _ = t
                    if self.seen[e].get(id(sem), 0) < val:
                        self.seen[e][id(sem)] = val
                        self.eng[e].wait_ge(sem, val)
                else:
                    self._wait(e, t)

    def finish(self):
        self.barrier()


class Ctx:
    pass


def build_nc(S, layers, n_layers_total=DEPTH, first=True, last=True, phases=None):
    nc = bass.Bass("TRN2", target_bir_lowering=False)
    NT = S // TT
    c = Ctx()
    c.nc = nc
    c.S = S
    x_in = nc.dram_tensor("x", [S, D], F32, kind="ExternalInput").ap()
    y_out = nc.dram_tensor("y", [S, D], F32, kind="ExternalOutput").ap()
    L = n_layers_total
    w = {}
    for nm, shp in (("ffn1_norm", [L, D]), ("ffn1_gate", [L, D, DFF]), ("ffn1_up", [L, D, DFF]),
                    ("ffn1_down", [L, DFF, D]), ("ffn2_norm", [L, D]), ("ffn2_gate", [L, D, DFF]),
                    ("ffn2_up", [L, D, DFF]), ("ffn2_down", [L, DFF, D]), ("final_norm", [1, D])):
        w[nm] = nc.dram_tensor(nm, shp, F32, kind="ExternalInput").ap()
    for nm, shp in (("mix_norm", [L, D]), ("w_in", [L, D, PROJ]), ("lambda_q1", [L, 64]), ("lambda_k1", [L, 64]),
                    ("lambda_q2", [L, 64]), ("lambda_k2", [L, 64]), ("subln_gain", [L, 128]),
                    ("dil_gain", [L, 512]), ("w_out", [L, D, D])):
        w[nm] = nc.dram_tensor(nm, shp, F32, kind="ExternalInput").ap()
    positions = nc.dram_tensor("positions", [S], I32, kind="ExternalInput").ap()
    ident_d = nc.dram_tensor("ident", [128, 128], F32, kind="ExternalInput").ap()
    ropec_d = nc.dram_tensor("ropec", [128, 2], F32, kind="ExternalInput").ap()
    masks_d = nc.dram_tensor("masks", [128, 1536], F32, kind="ExternalInput").ap()
    eones_d = nc.dram_tensor("eones", [128, 256], F32, kind="ExternalInput").ap()
    prot_d = nc.dram_tensor("prot", [128, 128], F32, kind="ExternalInput").ap()
    xs = nc.dram_tensor("xs", [S, D], F32, kind="Internal").ap()
    cosT = nc.dram_tensor("cosT", [128, S], F32, kind="Internal").ap()
    sinT = nc.dram_tensor("sinT", [128, S], F32, kind="Internal").ap()
    qkT = nc.dram_tensor("qkT", [16, 128, S], BF16, kind="Internal").ap()
    vd = nc.dram_tensor("vd", [S, 512], BF16, kind="Internal").ap()
    va = nc.dram_tensor("va", [S, 4, 2, 128], BF16, kind="Internal").ap()
    mT = nc.dram_tensor("mT", [4, 128, S], BF16, kind="Internal").ap()
    aT = nc.dram_tensor("aT", [4, 128, S], F32, kind="Internal").ap()
    b_cs, b_qkT, b_vd, b_va, b_mT, b_aT = (Buf(n) for n in ("cs", "qkT", "vd", "va", "mT", "aT"))

    kb = KB(nc)
    c.kb = kb

    ident = nc.alloc_sbuf_tensor("ident_b", [128, 128], BF16)
    ropec = nc.alloc_sbuf_tensor("ropec_sb", [128, 2], F32)
    negm = nc.alloc_sbuf_tensor("negm", [128, 512], BF16)
    negd = nc.alloc_sbuf_tensor("negd", [128, 256], BF16)
    eones = nc.alloc_sbuf_tensor("eones_b", [128, 2, 128], BF16)
    ones_b = nc.alloc_sbuf_tensor("ones_b", [128, 128], BF16)
    prot = nc.alloc_sbuf_tensor("prot_b", [128, 128], BF16)
    b_ident, b_ropec, b_mask = Buf("ident"), Buf("ropec"), Buf("mask")
    s_const = kb.dma_sem("const")
    with (nc.sbuf_tensor("ident_f", [128, 128], F32) as ident_f,
          nc.sbuf_tensor("masks_f", [128, 1536], F32) as masks_f,
          nc.sbuf_tensor("eones_f", [128, 256], F32) as eones_f,
          nc.sbuf_tensor("prot_f", [128, 128], F32) as prot_f):
        kb.dma("sp", s_const, ident_f[:, :], ident_d[:, :], writes=[b_ident])
        kb.dma("sp", s_const, ropec[:, :], ropec_d[:, :], writes=[b_ropec])
        kb.dma("sp", s_const, masks_f[:, :], masks_d[:, :], writes=[b_mask])
        kb.dma("sp", s_const, eones_f[:, :], eones_d[:, :], writes=[b_mask])
        kb.dma("sp", s_const, prot_f[:, :], prot_d[:, :], writes=[b_mask])
        kb.group([b_ident, b_ropec, b_mask])
        kb.op("dve", lambda e: e.tensor_copy(ident[:, :], ident_f[:, :]), reads=[b_ident], writes=[b_ident])
        kb.op("dve", lambda e: e.tensor_copy(eones[:, :, :], eones_f[:, :].rearrange("p (k q) -> p k q", q=128)),
              reads=[b_mask], writes=[b_mask])
        kb.op("dve", lambda e: e.memset(ones_b[:, :], 1.0), reads=[b_mask], writes=[b_mask])
        kb.op("dve", lambda e: e.tensor_copy(prot[:, :], prot_f[:, :]), reads=[b_mask], writes=[b_mask])
        kb.op("dve", lambda e: e.tensor_copy(negm[:, :], masks_f[:, 768:1280]), reads=[b_mask], writes=[b_mask])
        kb.op("dve", lambda e: e.tensor_copy(negd[:, :], masks_f[:, 1280:1536]), reads=[b_mask], writes=[b_mask])
        kb.barrier()

    def rstd_ops(st, col, b_st):
        kb.op("dve", lambda e: e.tensor_scalar(out=st[:, 8 + col:9 + col], in0=st[:, col:col + 1],
                                               scalar1=1.0 / D, scalar2=EPS, op0=ALU.mult, op1=ALU.add),
              reads=[b_st], writes=[b_st])
        kb.op("act", lambda e: e.activation(out=st[:, 8 + col:9 + col], in_=st[:, 8 + col:9 + col], func=AF.Sqrt),
              reads=[b_st], writes=[b_st])
        kb.op("dve", lambda e: e.reciprocal(out=st[:, 8 + col:9 + col], in_=st[:, 8 + col:9 + col]),
              reads=[b_st], writes=[b_st])

    def ffn_phase(l, which, src, dst, final):
        kb.barrier()
        nm = "ffn%d" % which
        u = "_%d_%d" % (l, which)
        with (nc.sbuf_tensor("wg" + u, [128, 8, DFF], BF16) as wg,
              nc.sbuf_tensor("wu" + u, [128, 8, DFF], BF16) as wu,
              nc.sbuf_tensor("wd" + u, [128, NFC, D], BF16) as wd,
              nc.sbuf_tensor("gbc" + u, [128, D], F32) as gbc,
              nc.sbuf_tensor("gfin" + u, [128, D], F32) as gfin,
              nc.sbuf_tensor("xblk" + u, [128, 2, D], F32) as xblk,
              nc.sbuf_tensor("hb" + u, [128, 2, D], BF16) as hb,
              nc.sbuf_tensor("hT" + u, [128, 8, TT], BF16) as hT,
              nc.sbuf_tensor("actb" + u, [128, NFC, TT], BF16) as actb,
              nc.sbuf_tensor("sg" + u, [128, 2, TT], F32) as sg,
              nc.sbuf_tensor("xr" + u, [128, 2, D], F32) as xr,
              nc.sbuf_tensor("junk" + u, [128, D], BF16) as junk,
              nc.sbuf_tensor("st" + u, [128, 16], F32) as st,
              nc.psum_tensor("pT" + u, [128, 2, 8, 128], BF16) as pT,
              nc.psum_tensor("pg" + u, [128, 2, 512], F32) as pg,
              nc.psum_tensor("pu" + u, [128, 2, 512], F32) as pu,
              nc.psum_tensor("po" + u, [128, 2, 512], F32) as po):
            b_wg = [Buf("wg%d" % k) for k in range(8)]
            b_wu = [Buf("wu%d" % k) for k in range(8)]
            b_wd = [Buf("wd%d" % k) for k in range(NFC)]
            b_g = Buf("g")
            b_x = [Buf("x0"), Buf("x1")]
            s_x = [kb.dma_sem("x0"), kb.dma_sem("x1")]
            b_h = [Buf("h0"), Buf("h1")]
            b_hT4 = [Buf("hT%d" % b) for b in range(4)]
            b_act = [Buf("act%d" % f) for f in range(NFC)]
            b_sg = [Buf("sg0"), Buf("sg1")]
            b_xr = [Buf("xr0"), Buf("xr1")]
            s_xr = [kb.dma_sem("xr0"), kb.dma_sem("xr1")]
            s_xo = [kb.dma_sem("xo0"), kb.dma_sem("xo1")]
            b_st = Buf("st")
            b_junk = Buf("junk")
            b_pT = [Buf("pT0"), Buf("pT1")]
            b_pg = [Buf("pg0"), Buf("pg1")]
            b_pu = [Buf("pu0"), Buf("pu1")]
            b_po = [Buf("po0"), Buf("po1")]
            s_g = kb.dma_sem("g")
            b_gf = Buf("gf")
            kb.dma("sp", s_g, gbc[:, :], w[nm + "_norm"][l, :].partition_broadcast(128), writes=[b_g])
            if final:
                kb.dma("sp", s_g, gfin[:, :], w["final_norm"][0, :].partition_broadcast(128), writes=[b_gf])
                kb.group([b_g, b_gf])
            s_wgs, s_wus, s_wds = kb.dma_sem("wgs"), kb.dma_sem("wus"), kb.dma_sem("wds")
            for k in range(8):
                kb.dma("pool", s_wgs, wg[:, k, :], w[nm + "_gate"][l, k * 128:(k + 1) * 128, :], writes=[b_wg[k]])
            kb.group(b_wg)
            for k in range(8):
                kb.dma("pool", s_wus, wu[:, k, :], w[nm + "_up"][l, k * 128:(k + 1) * 128, :], writes=[b_wu[k]])
            kb.group(b_wu)
            for f in range(NFC):
                kb.dma("pool", s_wds, wd[:, f, :], w[nm + "_down"][l, f * 128:(f + 1) * 128, :], writes=[b_wd[f]])
            kb.group(b_wd)

            blk_ctr = [0]
            pend = {}

            def norm_block(i, b):
                n = blk_ctr[0]
                blk_ctr[0] += 1
                s = n % 2
                r0 = i * TT + b * 128
                kb.dma("sp", s_x[s], xblk[:, s, :], src[r0:r0 + 128, :], writes=[b_x[s]])
                col = (n % 8)
                kb.op("act", lambda e: e.activation(out=junk[:, :], in_=xblk[:, s, :], func=AF.Square,
                                                    accum_out=st[:, col:col + 1]),
                      reads=[b_x[s]], writes=[b_junk, b_st])
                rstd_ops(st, col, b_st)
                kb.op("dve", lambda e: e.scalar_tensor_tensor(out=hb[:, s, :], in0=xblk[:, s, :],
                                                              scalar=st[:, 8 + col:9 + col], in1=gbc[:, :],
                                                              op0=ALU.mult, op1=ALU.mult),
                      reads=[b_x[s], b_st, b_g], writes=[b_h[s]])
                pend[(i, b)] = s

            def norm_post(i, b):
                s = pend.pop((i, b))
                for k in range(8):
                    kb.op("pe", lambda e, k=k: e.transpose(out=pT[:, s, k, :], in_=hb[:, s, k * 128:(k + 1) * 128],
                                                           identity=ident[:, :]),
                          reads=[b_h[s], b_ident], writes=[b_pT[s]], sig=(k == 7))
                kb.op("act", lambda e: e.copy(out=hT[:, :, b * 128:(b + 1) * 128], in_=pT[:, s, :, :]),
                      reads=[b_pT[s]], writes=[b_hT4[b]])

            gu_ctr = [0]

            def gate_up(fc):
                n = gu_ctr[0]
                gu_ctr[0] += 1
                s = n % 2
                for k in range(8):
                    kb.op("pe", lambda e, k=k: e.matmul(pg[:, s, :], lhsT=wg[:, k, fc * 128:(fc + 1) * 128],
                                                        rhs=hT[:, k, :], start=(k == 0), stop=(k == 7)),
                          reads=[b_wg[k]] + b_hT4, writes=[b_pg[s]], sig=(k == 7))
                for k in range(8):
                    kb.op("pe", lambda e, k=k: e.matmul(pu[:, s, :], lhsT=wu[:, k, fc * 128:(fc + 1) * 128],
                                                        rhs=hT[:, k, :], start=(k == 0), stop=(k == 7)),
                          reads=[b_wu[k]] + b_hT4, writes=[b_pu[s]], sig=(k == 7))
                kb.op("act", lambda e: e.activation(out=sg[:, s, :], in_=pg[:, s, :], func=AF.Silu),
                      reads=[b_pg[s]], writes=[b_sg[s]])
                kb.op("dve", lambda e: e.tensor_tensor(out=actb[:, fc, :], in0=pu[:, s, :], in1=sg[:, s, :],
                                                       op=ALU.mult),
                      reads=[b_pu[s], b_sg[s]], writes=[b_act[fc]])

            dn_ctr = [0]

            def down_block(i, b):
                n = dn_ctr[0]
                dn_ctr[0] += 1
                s = n % 2
                r0 = i * TT + b * 128
                kb.dma("sp", s_xr[s], xr[:, s, :], src[r0:r0 + 128, :], reads=[], writes=[b_xr[s]])
                for half in range(2):
                    for f in range(NFC):
                        kb.op("pe", lambda e, f=f: e.matmul(po[:, half, :], lhsT=actb[:, f, b * 128:(b + 1) * 128],
                                                            rhs=wd[:, f, half * 512:(half + 1) * 512],
                                                            start=(f == 0), stop=(f == NFC - 1)),
                              reads=[b_act[f], b_wd[f]], writes=[b_po[half]], sig=(f == NFC - 1))
                    kb.op("dve", lambda e: e.scalar_tensor_tensor(out=xr[:, s, half * 512:(half + 1) * 512],
                                                                  in0=po[:, half, :], scalar=0.5,
                                                                  in1=xr[:, s, half * 512:(half + 1) * 512],
                                                                  op0=ALU.mult, op1=ALU.add),
                          reads=[b_po[half], b_xr[s]], writes=[b_xr[s]])
                if final:
                    col = 4 + (n % 4)
                    kb.op("act", lambda e: e.activation(out=junk[:, :], in_=xr[:, s, :], func=AF.Square,
                                                        accum_out=st[:, col:col + 1]),
                          reads=[b_xr[s]], writes=[b_junk, b_st])
                    rstd_ops(st, col, b_st)
                    kb.op("dve", lambda e: e.scalar_tensor_tensor(out=xr[:, s, :], in0=xr[:, s, :],
                                                                  scalar=st[:, 8 + col:9 + col], in1=gfin[:, :],
                                                                  op0=ALU.mult, op1=ALU.mult),
                          reads=[b_xr[s], b_st, b_gf], writes=[b_xr[s]])
                kb.dma("sp", s_xo[s], dst[r0:r0 + 128, :], xr[:, s, :], reads=[b_xr[s]], writes=[])

            for b in range(4):
                norm_block(0, b)
                norm_post(0, b)
            for i in range(NT):
                for fc in range(NFC):
                    gate_up(fc)
                    if i + 1 < NT and fc == NFC - 2:
                        norm_block(i + 1, 0)
                for b in range(4):
                    if i + 1 < NT and b + 1 < 4:
                        norm_block(i + 1, b + 1)
                    down_block(i, b)
                    if i + 1 < NT:
                        norm_post(i + 1, b)
            kb.barrier()


    NSPAN = max(1, S // 2048)
    SPAN = min(S, 2048)

    def rope_tables():
        kb.barrier()
        CH = 2048 if S >= 2048 else S
        TWO_PI = 2.0 * math.pi
        C1 = 6.28125
        C2 = TWO_PI - C1
        PI_LO = 3.1415925
        MAGIC = 12582912.0
        with (nc.sbuf_tensor("rp_pos", [128, CH], I32) as pos_i,
              nc.sbuf_tensor("rp_ang", [128, CH], F32) as ang,
              nc.sbuf_tensor("rp_k", [128, CH], F32) as kk,
              nc.sbuf_tensor("rp_r", [128, CH], F32) as rr,
              nc.sbuf_tensor("rp_r2", [128, CH], F32) as r2,
              nc.sbuf_tensor("rp_o", [128, 2, CH], F32) as oo):
            b_pos, b_ang, b_k, b_r, b_r2, b_o = (Buf(n) for n in ("pos", "ang", "k", "r", "r2", "o"))
            s_pos = kb.dma_sem("rp_pos")
            s_o = kb.dma_sem("rp_o")
            for ci in range(S // CH):
                t0 = ci * CH
                kb.dma("sp", s_pos, pos_i[:, :], positions[t0:t0 + CH].partition_broadcast(128), writes=[b_pos])
                kb.op("dve", lambda e: e.tensor_copy(ang[:, :], pos_i[:, :]), reads=[b_pos], writes=[b_ang])
                kb.op("dve", lambda e: e.tensor_scalar_mul(out=ang[:, :], in0=ang[:, :], scalar1=ropec[:, 0:1]),
                      reads=[b_ang, b_ropec], writes=[b_ang])
                kb.op("dve", lambda e: e.tensor_scalar(out=kk[:, :], in0=ang[:, :], scalar1=1.0 / TWO_PI,
                                                       scalar2=MAGIC, op0=ALU.mult, op1=ALU.add),
                      reads=[b_ang], writes=[b_k])
                kb.op("dve", lambda e: e.tensor_scalar_add(out=kk[:, :], in0=kk[:, :], scalar1=-MAGIC),
                      reads=[b_k], writes=[b_k])
                kb.op("dve", lambda e: e.scalar_tensor_tensor(out=rr[:, :], in0=kk[:, :], scalar=-C1, in1=ang[:, :],
                                                              op0=ALU.mult, op1=ALU.add),
                      reads=[b_k, b_ang], writes=[b_r])
                kb.op("dve", lambda e: e.scalar_tensor_tensor(out=rr[:, :], in0=kk[:, :], scalar=-C2, in1=rr[:, :],
                                                              op0=ALU.mult, op1=ALU.add),
                      reads=[b_k, b_r], writes=[b_r])
                kb.op("dve", lambda e: e.tensor_scalar(out=r2[:, :], in0=rr[:, :], scalar1=math.pi / 2,
                                                       scalar2=-TWO_PI, op0=ALU.is_gt, op1=ALU.mult),
                      reads=[b_r], writes=[b_r2])
                kb.op("dve", lambda e: e.scalar_tensor_tensor(out=r2[:, :], in0=rr[:, :], scalar=math.pi / 2,
                                                              in1=r2[:, :], op0=ALU.add, op1=ALU.add),
                      reads=[b_r, b_r2], writes=[b_r2])
                kb.op("dve", lambda e: e.tensor_scalar(out=rr[:, :], in0=rr[:, :], scalar1=PI_LO, scalar2=-PI_LO,
                                                       op0=ALU.min, op1=ALU.max), reads=[b_r], writes=[b_r])
                kb.op("dve", lambda e: e.tensor_scalar(out=r2[:, :], in0=r2[:, :], scalar1=PI_LO, scalar2=-PI_LO,
                                                       op0=ALU.min, op1=ALU.max), reads=[b_r2], writes=[b_r2])
                kb.op("act", lambda e: e.activation(out=oo[:, 0, :], in_=r2[:, :], func=AF.Sin),
                      reads=[b_r2], writes=[b_o])
                kb.op("act", lambda e: e.activation(out=oo[:, 1, :], in_=rr[:, :], func=AF.Sin,
                                                    scale=ropec[:, 1:2]),
                      reads=[b_r, b_ropec], writes=[b_o])
                kb.dma("sp", s_o, cosT[:, t0:t0 + CH], oo[:, 0, :], reads=[b_o], writes=[b_cs])
                kb.dma("sp", s_o, sinT[:, t0:t0 + CH], oo[:, 1, :], reads=[b_o], writes=[b_cs])
                kb.group([b_cs])
        kb.barrier()

    def proj_phase(l, src):
        kb.barrier()
        u = "_m1_%d" % l
        with (nc.sbuf_tensor("win" + u, [128, 8, PROJ], BF16) as win,
              nc.sbuf_tensor("qb" + u, [128, 2, TT], BF16) as qb,
              nc.sbuf_tensor("gbc" + u, [128, D], F32) as gbc,
              nc.sbuf_tensor("xblk" + u, [128, 2, D], F32) as xblk,
              nc.sbuf_tensor("hb" + u, [128, 2, D], BF16) as hb,
              nc.sbuf_tensor("hT" + u, [128, 8, TT], BF16) as hT,
              nc.sbuf_tensor("junk" + u, [128, D], BF16) as junk,
              nc.sbuf_tensor("st" + u, [128, 16], F32) as st,
              nc.sbuf_tensor("cs" + u, [128, 2, 2, TT], F32) as cs,
              nc.sbuf_tensor("t1" + u, [128, 2, TT], F32) as t1,
              nc.sbuf_tensor("t2" + u, [128, 2, TT], F32) as t2,
              nc.sbuf_tensor("qks" + u, [128, 2, 16, TT], BF16) as qks,
              nc.sbuf_tensor("vds" + u, [128, 2, 4, 512], BF16) as vds,
              nc.sbuf_tensor("vas" + u, [128, 4, 4, 2, 128], BF16) as vas,
              nc.psum_tensor("pT" + u, [128, 2, 8, 128], BF16) as pT,
              nc.psum_tensor("pq" + u, [128, 2, 512], F32) as pq,
              nc.psum_tensor("pr" + u, [128, 2, 512], F32) as pr,
              nc.psum_tensor("pv" + u, [128, 2, 512], F32) as pv):
            b_win = [Buf("win%d" % k) for k in range(8)]
            b_g = Buf("g")
            b_x = [Buf("x0"), Buf("x1")]
            s_x = [kb.dma_sem("x0"), kb.dma_sem("x1")]
            b_h = [Buf("h0"), Buf("h1")]
            b_hT4 = [Buf("hT%d" % b) for b in range(4)]
            b_st = Buf("st")
            b_junk = Buf("junk")
            b_pT = [Buf("pT0"), Buf("pT1")]
            b_pq = [Buf("pq0"), Buf("pq1")]
            b_pr = [Buf("pr0"), Buf("pr1")]
            b_pv = [Buf("pv0"), Buf("pv1")]
            b_csb = [Buf("cs0"), Buf("cs1")]
            s_cs = [kb.dma_sem("cs0"), kb.dma_sem("cs1")]
            b_t1 = [Buf("t10"), Buf("t11")]
            b_t2 = [Buf("t20"), Buf("t21")]
            b_qks = [Buf("qks0"), Buf("qks1")]
            s_qks = [kb.dma_sem("qks0"), kb.dma_sem("qks1")]
            b_vds = [Buf("vds0"), Buf("vds1")]
            s_vds = [kb.dma_sem("vds0"), kb.dma_sem("vds1")]
            b_vas = Buf("vas")
            s_vas = kb.dma_sem("vas")
            s_g = kb.dma_sem("g")
            kb.dma("sp", s_g, gbc[:, :], w["mix_norm"][l, :].partition_broadcast(128), writes=[b_g])
            s_wi = kb.dma_sem("wgs")
            for k in range(8):
                kb.dma("pool", s_wi, win[:, k, :], w["w_in"][l, k * 128:(k + 1) * 128, :], writes=[b_win[k]])
            kb.group(b_win)
            kb.op("pool", lambda e: e.memset(vas[:, :, :, :, :], 0.0), writes=[b_vas])
            blk_ctr = [0]
            pend = {}

            def norm_block(i, b):
                n = blk_ctr[0]
                blk_ctr[0] += 1
                s = n % 2
                r0 = i * TT + b * 128
                kb.dma("sp", s_x[s], xblk[:, s, :], src[r0:r0 + 128, :], writes=[b_x[s]])
                col = (n % 8)
                kb.op("act", lambda e: e.activation(out=junk[:, :], in_=xblk[:, s, :], func=AF.Square,
                                                    accum_out=st[:, col:col + 1]),
                      reads=[b_x[s]], writes=[b_junk, b_st])
                rstd_ops(st, col, b_st)
                kb.op("dve", lambda e: e.scalar_tensor_tensor(out=hb[:, s, :], in0=xblk[:, s, :],
                                                              scalar=st[:, 8 + col:9 + col], in1=gbc[:, :],
                                                              op0=ALU.mult, op1=ALU.mult),
                      reads=[b_x[s], b_st, b_g], writes=[b_h[s]])
                pend[(i, b)] = s

            def norm_post(i, b):
                s = pend.pop((i, b))
                for k in range(8):
                    kb.op("pe", lambda e, k=k: e.transpose(out=pT[:, s, k, :], in_=hb[:, s, k * 128:(k + 1) * 128],
                                                           identity=ident[:, :]),
                          reads=[b_h[s], b_ident], writes=[b_pT[s]], sig=(k == 7))
                kb.op("act", lambda e: e.copy(out=hT[:, :, b * 128:(b + 1) * 128], in_=pT[:, s, :, :]),
                      reads=[b_pT[s]], writes=[b_hT4[b]])

            b_qb = [Buf("qb0"), Buf("qb1")]

            def qk_mm(i, ci):
                s = ci % 2
                c0 = ci * 128 if ci < 8 else 1536 + (ci - 8) * 128
                for k in range(8):
                    kb.op("pe", lambda e, k=k: e.matmul(pq[:, s, :], lhsT=win[:, k, c0:c0 + 128], rhs=hT[:, k, :],
                                                        start=(k == 0), stop=(k == 7)),
                          reads=[b_win[k]] + b_hT4, writes=[b_pq[s]], sig=(k == 7))
                kb.op("act", lambda e: e.copy(out=qb[:, s, :], in_=pq[:, s, :]), reads=[b_pq[s]], writes=[b_qb[s]])

            def qk_rot(i, ci, ts):
                s = ci % 2
                kb.op("pe", lambda e: e.matmul(pr[:, s, :], lhsT=prot[:, :], rhs=qb[:, s, :], start=True, stop=True),
                      reads=[b_qb[s], b_mask], writes=[b_pr[s]])
                kb.op("dve", lambda e: e.tensor_tensor(out=t1[:, s, :], in0=pq[:, s, :], in1=cs[:, ts, 0, :],
                                                       op=ALU.mult),
                      reads=[b_pq[s], b_csb[ts], b_qb[s]], writes=[b_t1[s]])
                kb.op("dve", lambda e: e.tensor_tensor(out=t2[:, s, :], in0=pr[:, s, :], in1=cs[:, ts, 1, :],
                                                       op=ALU.mult),
                      reads=[b_pr[s], b_csb[ts]], writes=[b_t2[s]])
                kb.op("dve", lambda e: e.tensor_tensor(out=qks[:, ts, ci, :], in0=t1[:, s, :], in1=t2[:, s, :],
                                                       op=ALU.add),
                      reads=[b_t1[s], b_t2[s]], writes=[b_qks[ts]])

            vc = [0]

            def v_block(i, b, ts):
                r0 = i * TT + b * 128
                s = vc[0] % 2
                vc[0] += 1
                for k in range(8):
                    kb.op("pe", lambda e, k=k: e.matmul(pv[:, s, :], lhsT=hT[:, k, b * 128:(b + 1) * 128],
                                                        rhs=win[:, k, 1024:1536], start=(k == 0), stop=(k == 7)),
                          reads=[b_win[k], b_hT4[b]], writes=[b_pv[s]], sig=(k == 7))
                kb.op("act", lambda e: e.copy(out=vds[:, ts, b, :], in_=pv[:, s, :]),
                      reads=[b_pv[s]], writes=[b_vds[ts]])
                s = vc[0] % 2
                vc[0] += 1
                for k in range(8):
                    kb.op("pe", lambda e, k=k: e.matmul(pv[:, s, :], lhsT=hT[:, k, b * 128:(b + 1) * 128],
                                                        rhs=win[:, k, 2560:3072], start=(k == 0), stop=(k == 7)),
                          reads=[b_win[k], b_hT4[b]], writes=[b_pv[s]], sig=(k == 7))
                pvv = pv[:, s, :].rearrange("p (j t d) -> p j t d", t=2, d=64)
                kb.op("act", lambda e: e.copy(out=vas[:, b, :, 0, 0:64], in_=pvv[:, :, 0, :]),
                      reads=[b_pv[s]], writes=[b_vas])
                kb.op("act", lambda e: e.copy(out=vas[:, b, :, 1, 64:128], in_=pvv[:, :, 1, :]),
                      reads=[b_pv[s]], writes=[b_vas])

            for b in range(4):
                norm_block(0, b)
                norm_post(0, b)
            def csload(i):
                ts = i % 2
                c0 = i * TT
                kb.dma("sp", s_cs[ts], cs[:, ts, 0, :], cosT[:, c0:c0 + TT], reads=[b_cs], writes=[b_csb[ts]])
                kb.dma("sp", s_cs[ts], cs[:, ts, 1, :], sinT[:, c0:c0 + TT], reads=[b_cs], writes=[b_csb[ts]])
                kb.group([b_csb[ts]])

            csload(0)
            for i in range(NT):
                ts = i % 2
                c0 = i * TT
                if i + 1 < NT:
                    csload(i + 1)
                qk_mm(i, 0)
                for ci in range(16):
                    if ci + 1 < 16:
                        qk_mm(i, ci + 1)
                    if ci < 15:
                        qk_rot(i, ci, ts)
                if i + 1 < NT:
                    norm_block(i + 1, 0)
                for b in range(4):
                    if i + 1 < NT and b + 1 < 4:
                        norm_block(i + 1, b + 1)
                    v_block(i, b, ts)
                    if b == 0:
                        qk_rot(i, 15, ts)
                        kb.dma("sp", s_qks[ts], qkT[:, :, c0:c0 + TT].rearrange("c p t -> p c t"), qks[:, ts, :, :],
                               reads=[b_qks[ts]], writes=[b_qkT])
                    if i + 1 < NT:
                        norm_post(i + 1, b)
                kb.dma("sp", s_vds[ts], vd[c0:c0 + TT, :].rearrange("(b p) e -> p b e", p=128), vds[:, ts, :, :],
                       reads=[b_vds[ts]], writes=[b_vd])
                kb.dma("sp", s_vas, va[c0:c0 + TT, :, :, :].rearrange("(b p) j t e -> p b (j t e)", p=128),
                       vas[:, :, :, :, :].rearrange("p b j t e -> p b (j t e)"),
                       reads=[b_vas], writes=[b_va])
            kb.barrier()

    def lam_ops(l, lam_init, lamt, b_lam):
        u = "_lam_%d" % l
        with (nc.sbuf_tensor("lv" + u, [128, 4, 64], F32) as lv,
              nc.sbuf_tensor("lt" + u, [128, 8], F32) as lt):
            b_lv = Buf("lv")
            s_lv = kb.dma_sem("lv")
            for i, nmv in enumerate(("lambda_q1", "lambda_k1", "lambda_q2", "lambda_k2")):
                kb.dma("sp", s_lv, lv[:, i, :], w[nmv][l, :].partition_broadcast(128), writes=[b_lv])
            kb.group([b_lv])
            kb.op("dve", lambda e: e.tensor_tensor(out=lv[:, 0, :], in0=lv[:, 0, :], in1=lv[:, 1, :], op=ALU.mult),
                  reads=[b_lv], writes=[b_lv])
            kb.op("dve", lambda e: e.tensor_tensor(out=lv[:, 2, :], in0=lv[:, 2, :], in1=lv[:, 3, :], op=ALU.mult),
                  reads=[b_lv], writes=[b_lv])
            kb.op("dve", lambda e: e.reduce_sum(out=lt[:, 0:1], in_=lv[:, 0, :], axis=AX.X), reads=[b_lv], writes=[b_lam])
            kb.op("dve", lambda e: e.reduce_sum(out=lt[:, 1:2], in_=lv[:, 2, :], axis=AX.X), reads=[b_lv], writes=[b_lam])
            kb.op("act", lambda e: e.activation(out=lt[:, 2:4], in_=lt[:, 0:2], func=AF.Exp), reads=[b_lam], writes=[b_lam])
            kb.op("dve", lambda e: e.scalar_tensor_tensor(out=lamt[:, 0:1], in0=lt[:, 3:4], scalar=-float(lam_init),
                                                          in1=lt[:, 2:3], op0=ALU.add, op1=ALU.subtract),
                  reads=[b_lam], writes=[b_lam])
            kb.barrier()

    def diff_phase(l, lam_init):
        kb.barrier()
        u = "_m2a_%d" % l
        NB = S // 128
        NQT = S // 512
        LOOK = 2
        with (nc.sbuf_tensor("kT" + u, [128, 2, S], BF16) as kT,
              nc.sbuf_tensor("v1" + u, [128, 2, NB, 130], BF16) as v1,
              nc.sbuf_tensor("qT" + u, [128, 2, 512], BF16) as qT,
              nc.sbuf_tensor("PT" + u, [128, 3, 2, 512], BF16) as PT,
              nc.sbuf_tensor("accs" + u, [128, 4, 3, 512], F32) as accs,
              nc.sbuf_tensor("lamt" + u, [128, 8], F32) as lamt,
              nc.sbuf_tensor("gsub" + u, [128, 128], F32) as gsub,
              nc.sbuf_tensor("fst" + u, [128, 4, 16], F32) as fst,
              nc.sbuf_tensor("fo" + u, [128, 4, 4, 128], F32) as fo,
              nc.sbuf_tensor("fj" + u, [128, 128], F32) as fj,
              nc.sbuf_tensor("fob" + u, [128, 4, 4, 128], BF16) as fob,
              nc.sbuf_tensor("otr" + u, [128, 2, 512], BF16) as otr,
              nc.psum_tensor("ps" + u, [128, 2, 2, 512], F32) as ps,
              nc.psum_tensor("acc" + u, [128, 3, 512], F32) as acc,
              nc.psum_tensor("ptr" + u, [128, 4, 128], BF16) as ptr):
            b_lam = Buf("lam")
            lam_ops(l, lam_init, lamt, b_lam)
            b_gs = Buf("gsub")
            s_gs = kb.dma_sem("g")
            kb.dma("sp", s_gs, gsub[:, :], w["subln_gain"][l, :].partition_broadcast(128), writes=[b_gs])
            kb.op("dve", lambda e: e.tensor_scalar_mul(out=gsub[:, :], in0=gsub[:, :], scalar1=float(1.0 - lam_init)),
                  reads=[b_gs], writes=[b_gs])
            b_kT = [Buf("kT0"), Buf("kT1")]
            s_kT = [kb.dma_sem("kT0"), kb.dma_sem("kT1")]
            b_v1 = [Buf("v10"), Buf("v11")]
            s_v1 = [kb.dma_sem("v10"), kb.dma_sem("v11")]
            b_qT = [Buf("qT0"), Buf("qT1")]
            s_qT = [kb.dma_sem("qT0"), kb.dma_sem("qT1")]
            b_ps = [Buf("ps0"), Buf("ps1")]
            b_PT = [Buf("PT0"), Buf("PT1"), Buf("PT2")]
            b_acc = Buf("acc")
            b_accs = [Buf("accs%d" % i) for i in range(4)]
            b_fst = [Buf("fst%d" % i) for i in range(4)]
            b_fj, b_ptr = Buf("fj"), Buf("ptr")
            b_fo = [Buf("fo%d" % i) for i in range(4)]
            b_fob = [Buf("fob%d" % i) for i in range(4)]
            b_otr = [Buf("otr0"), Buf("otr1")]
            s_otr = [kb.dma_sem("otr0"), kb.dma_sem("otr1")]
            for hs in range(2):
                kb.op("pool", lambda e, hs=hs: e.memset(v1[:, hs, :, 128:130], 1.0), writes=[b_v1[hs]])

            def acc_view(a):
                return acc[:, a // 3, (a % 3) * 160:(a % 3) * 160 + 129]

            def accs_view(sl, a, lo, hi):
                return accs[:, sl, a // 3, (a % 3) * 160 + lo:(a % 3) * 160 + hi]

            units = []
            for h in range(4):
                for qt in range(NQT):
                    nkb = 4 * qt + 4
                    for kbi in range(nkb):
                        units.append((h, qt, kbi, kbi == 0, kbi == nkb - 1))
            NU = len(units)

            def load_head(h):
                hs = h % 2
                kb.dma("sp", s_kT[hs], kT[:, hs, :], qkT[4 + h, :, :], reads=[b_qkT], writes=[b_kT[hs]])
                kb.dma("sp", s_v1[hs], v1[:, hs, :, 0:128],
                       vd[:, h * 128:(h + 1) * 128].rearrange("(b p) e -> p b e", p=128),
                       reads=[b_vd], writes=[b_v1[hs]])

            def load_q(gq):
                h, qt = gq // NQT, gq % NQT
                qs = gq % 2
                kb.dma("sp", s_qT[qs], qT[:, qs, :], qkT[h, :, qt * 512:(qt + 1) * 512], reads=[b_qkT],
                       writes=[b_qT[qs]])

            def qk_stage(n):
                h, qt, kbi, first, last = units[n]
                hs = h % 2
                gq = h * NQT + qt
                qs = gq % 2
                if first:
                    if qt == 0:
                        if h == 0:
                            load_head(0)
                            load_q(0)
                    if gq + 1 < 4 * NQT:
                        load_q(gq + 1)
                j = kbi - 4 * qt
                q0 = 128 * j if j > 0 else 0
                s = n % 2
                s3 = n % 3
                for cc in range(2):
                    kb.op("pe", lambda e, cc=cc: e.matmul(ps[:, s, cc, q0:512],
                                                          lhsT=kT[cc * 64:(cc + 1) * 64, hs, kbi * 128:(kbi + 1) * 128],
                                                          rhs=qT[cc * 64:(cc + 1) * 64, qs, q0:512],
                                                          start=True, stop=(j < 0), skip_group_check=True),
                          reads=[b_kT[hs], b_qT[qs]], writes=[b_ps[s]], sig=(cc == 1 and j < 0))
                if j >= 0:
                    for cc in range(2):
                        kb.op("pe", lambda e, cc=cc: e.matmul(ps[:, s, cc, q0:512], lhsT=ident[:, :],
                                                              rhs=negm[:, 0:512 - q0], start=False, stop=True,
                                                              skip_group_check=True),
                              reads=[b_ident, b_mask], writes=[b_ps[s]], sig=(cc == 1))
                kb.op("act", lambda e: e.activation(out=PT[:, s3, :, q0:512], in_=ps[:, s, :, q0:512],
                                                    func=AF.Exp, scale=0.125),
                      reads=[b_ps[s]], writes=[b_PT[s3]])

            def pv_stage(n):
                h, qt, kbi, first, last = units[n]
                hs = h % 2
                j = kbi - 4 * qt
                s3 = n % 3
                if first and qt == 0 and h + 1 < 4:
                    load_head(h + 1)
                for cc in range(2):
                    for jj in range(max(j, 0), 4):
                        a = cc * 4 + jj
                        kb.op("pe", lambda e, cc=cc, jj=jj, a=a: e.matmul(
                            acc_view(a), lhsT=PT[:, s3, cc, jj * 128:(jj + 1) * 128],
                            rhs=v1[:, hs, kbi, 0:129], start=(kbi == 0 and a % 3 == 0),
                            stop=(kbi == 4 * qt + jj), skip_group_check=True),
                              reads=[b_PT[s3], b_v1[hs]], writes=[b_acc], sig=(cc == 1 and jj == 3))
                if last:
                    finalize(h, qt)

            pending = []
            cur_n = [0]

            def finalize(h, qt):
                gq = h * NQT + qt
                sl = gq % 4
                for bk in range(3):
                    na = 3 if bk < 2 else 2
                    kb.op("dve", lambda e, bk=bk, na=na: e.tensor_copy(
                        accs[:, sl, bk, 0:480].rearrange("p (a c) -> p a c", c=160)[:, 0:na, 0:129],
                        acc[:, bk, 0:480].rearrange("p (a c) -> p a c", c=160)[:, 0:na, 0:129]),
                          reads=[b_acc], writes=[b_accs[sl]])
                for jj in range(4):
                    kb.op("dve", lambda e, jj=jj: e.reciprocal(out=fst[:, sl, jj:jj + 1],
                                                               in_=accs_view(sl, jj, 128, 129)),
                          reads=[b_accs[sl]], writes=[b_fst[sl]])
                    kb.op("dve", lambda e, jj=jj: e.reciprocal(out=fst[:, sl, 4 + jj:5 + jj],
                                                               in_=accs_view(sl, 4 + jj, 128, 129)),
                          reads=[b_accs[sl]], writes=[b_fst[sl]])
                kb.op("dve", lambda e: e.tensor_scalar_mul(out=fst[:, sl, 4:8], in0=fst[:, sl, 4:8],
                                                           scalar1=lamt[:, 0:1]),
                      reads=[b_fst[sl], b_lam], writes=[b_fst[sl]])
                for jj in range(4):
                    kb.op("dve", lambda e, jj=jj: e.tensor_scalar_mul(out=fo[:, sl, jj, :],
                                                                      in0=accs_view(sl, jj, 0, 128),
                                                                      scalar1=fst[:, sl, jj:jj + 1]),
                          reads=[b_accs[sl], b_fst[sl]], writes=[b_fo[sl]])
                for jj in range(4):
                    kb.op("dve", lambda e, jj=jj: e.scalar_tensor_tensor(out=fo[:, sl, jj, :],
                                                                         in0=accs_view(sl, 4 + jj, 0, 128),
                                                                         scalar=fst[:, sl, 4 + jj:5 + jj],
                                                                         in1=fo[:, sl, jj, :],
                                                                         op0=ALU.mult, op1=ALU.add),
                          reads=[b_accs[sl], b_fst[sl], b_fo[sl]], writes=[b_fo[sl]])
                for jj in range(4):
                    kb.op("dve", lambda e, jj=jj: e.tensor_tensor(out=fj[:, :], in0=fo[:, sl, jj, :],
                                                                  in1=fo[:, sl, jj, :], op=ALU.mult),
                          reads=[b_fo[sl]], writes=[b_fj])
                    kb.op("dve", lambda e, jj=jj: e.reduce_sum(out=fst[:, sl, 8 + jj:9 + jj], in_=fj[:, :], axis=AX.X),
                          reads=[b_fj], writes=[b_fst[sl]])
                kb.op("dve", lambda e: e.tensor_scalar(out=fst[:, sl, 12:16], in0=fst[:, sl, 8:12], scalar1=1.0 / 128,
                                                       scalar2=SUBLN_EPS, op0=ALU.mult, op1=ALU.add),
                      reads=[b_fst[sl]], writes=[b_fst[sl]])

                def stage_act():
                    kb.op("act", lambda e: e.activation(out=fst[:, sl, 12:16], in_=fst[:, sl, 12:16], func=AF.Ln),
                          reads=[b_fst[sl]], writes=[b_fst[sl]])
                    kb.op("act", lambda e: e.activation(out=fst[:, sl, 12:16], in_=fst[:, sl, 12:16], func=AF.Exp,
                                                        scale=-0.5),
                          reads=[b_fst[sl]], writes=[b_fst[sl]])

                def stage_dve():
                    for jj in range(4):
                        kb.op("dve", lambda e, jj=jj: e.scalar_tensor_tensor(
                            out=fob[:, sl, jj, :], in0=fo[:, sl, jj, :], scalar=fst[:, sl, 12 + jj:13 + jj],
                            in1=gsub[:, :], op0=ALU.mult, op1=ALU.mult),
                              reads=[b_fo[sl], b_fst[sl], b_gs], writes=[b_fob[sl]])

                def stage_pe():
                    for jj in range(4):
                        kb.op("pe", lambda e, jj=jj: e.transpose(out=ptr[:, jj, :], in_=fob[:, sl, jj, :],
                                                                 identity=ident[:, :]),
                              reads=[b_fob[sl], b_ident], writes=[b_ptr], sig=(jj == 3))

                def stage_out():
                    osl = gq % 2
                    kb.op("dve", lambda e: e.tensor_copy(otr[:, osl, :], ptr[:, :, :]),
                          reads=[b_ptr], writes=[b_otr[osl]])
                    kb.dma("sp", s_otr[osl], mT[h, :, qt * 512:(qt + 1) * 512], otr[:, osl, :],
                           reads=[b_otr[osl]], writes=[b_mT])

                n0 = cur_n[0]
                pending.append((n0 + 10, stage_act))
                pending.append((n0 + 12, stage_dve))
                pending.append((n0 + 14, stage_pe))
                pending.append((n0 + 16, stage_out))

            def flush(upto):
                while pending and (upto is None or pending[0][0] <= upto):
                    pending.pop(0)[1]()

            for n in range(NU + LOOK):
                cur_n[0] = n
                if n < NU:
                    qk_stage(n)
                if n - LOOK >= 0:
                    pv_stage(n - LOOK)
                flush(n)
            flush(None)
            kb.barrier()

    def dil_phase(l):
        kb.barrier()
        u = "_m2b_%d" % l
        LOOK = 2
        import os
        DIL = tuple(int(t) for t in os.environ.get("KDIL", "1,4,16").split(","))
        with (nc.sbuf_tensor("dk" + u, [128, 3, SPAN], BF16) as dk,
              nc.sbuf_tensor("dq" + u, [128, 2, SPAN], BF16) as dq,
              nc.sbuf_tensor("dkg" + u, [128, 2, 3, SPAN], BF16) as dkg,
              nc.sbuf_tensor("dqg" + u, [128, 2, 2, SPAN], BF16) as dqg,
              nc.sbuf_tensor("vg" + u, [128, 3, 3, 16, 256], BF16) as vg,
              nc.sbuf_tensor("dPT" + u, [128, 3, 2, 2, 128], BF16) as dPT,
              nc.sbuf_tensor("an" + u, [128, 2, SPAN], F32) as an,
              nc.sbuf_tensor("ad" + u, [128, 2, SPAN], F32) as ad,
              nc.psum_tensor("dps" + u, [128, 2, 2, 512], F32) as dps,
              nc.psum_tensor("dpo" + u, [128, 2, 512], F32) as dpo):
            b_dk = [Buf("dk%d" % i) for i in range(3)]
            s_dk = [kb.dma_sem("dk%d" % i) for i in range(3)]
            b_dq = [Buf("dq0"), Buf("dq1")]
            s_dq = [kb.dma_sem("dq0"), kb.dma_sem("dq1")]
            b_dkg = [[Buf("dkg%d%d" % (gi, sl)) for sl in range(3)] for gi in range(2)]
            b_dqg = [[Buf("dqg%d%d" % (gi, sl)) for sl in range(2)] for gi in range(2)]
            b_vg = [[Buf("vg%d%d" % (bi, sl)) for sl in range(3)] for bi in range(3)]
            s_vg = [[kb.dma_sem("vg%d%d" % (bi, sl)) for sl in range(3)] for bi in range(3)]
            b_dps = [Buf("dps0"), Buf("dps1")]
            b_dPT = [Buf("dPT0"), Buf("dPT1"), Buf("dPT2")]
            b_dpo = [Buf("dpo0"), Buf("dpo1")]
            b_an = [Buf("an0"), Buf("an1")]
            s_an = [kb.dma_sem("an0"), kb.dma_sem("an1")]
            b_ad = [Buf("ad0"), Buf("ad1")]
            iters = [(j, sp_) for j in range(4) for sp_ in range(NSPAN)]
            units = []
            for it, (j, sp_) in enumerate(iters):
                lst = []
                for bi, d in enumerate(DIL):
                    nbl = SPAN // (128 * d)
                    for r in range(d):
                        for bl in range(nbl):
                            lst.append([it, bi, d, r, bl, False, False])
                lst[0][5] = True
                lst[-1][6] = True
                units += lst
            NU = len(units)

            def loads(it):
                j, sp_ = iters[it]
                sl = it % 3
                ql = it % 2
                base = sp_ * SPAN
                kb.dma("sp", s_dk[sl], dk[:, sl, :], qkT[12 + j, :, base:base + SPAN], reads=[b_qkT],
                       writes=[b_dk[sl]])
                kb.dma("sp", s_dq[ql], dq[:, ql, :], qkT[8 + j, :, base:base + SPAN], reads=[b_qkT],
                       writes=[b_dq[ql]])
                for bi, d in enumerate(DIL):
                    nbl = SPAN // (128 * d)
                    srcv = va[base:base + SPAN, j, :, :].rearrange("(bl i r) t e -> i r bl (t e)", i=128, r=d)
                    dstv = vg[:, bi, sl, 0:d * nbl, :].rearrange("p (r bl) e -> p r bl e", r=d)
                    for r in range(d):
                        kb.dma("sp", s_vg[bi][sl], dstv[:, r, :, :], srcv[:, r, :, :], reads=[b_va],
                               writes=[b_vg[bi][sl]])
                    kb.group([b_vg[bi][sl]])
                for bi, d in enumerate(DIL):
                    if d == 1:
                        continue
                    gi = 0 if d == 4 else 1
                    kb.op("pool", lambda e, gi=gi, d=d: e.tensor_copy(
                        dkg[:, gi, sl, :].rearrange("p (r i) -> p r i", r=d),
                        dk[:, sl, :].rearrange("p (i r) -> p r i", r=d)),
                          reads=[b_dk[sl]], writes=[b_dkg[gi][sl]])
                    kb.op("pool", lambda e, gi=gi, d=d: e.tensor_copy(
                        dqg[:, gi, ql, :].rearrange("p (r i) -> p r i", r=d),
                        dq[:, ql, :].rearrange("p (i r) -> p r i", r=d)),
                          reads=[b_dq[ql]], writes=[b_dqg[gi][ql]])

            def operands(n):
                it, bi, d, r, bl, first, last = units[n]
                j, sp_ = iters[it]
                sl = it % 3
                pl = (it - 1) % 3
                ql = it % 2
                nbl = SPAN // (128 * d)
                gi = 0 if d == 4 else 1
                goff = r * (SPAN // d) + bl * 128
                o = Ctx()
                o.has_prev = (bl > 0) or (sp_ > 0)
                if d == 1:
                    o.qv = dq[:, ql, goff:goff + 128]
                    o.kcur = dk[:, sl, goff:goff + 128]
                    o.rd_cur = [b_dk[sl], b_dq[ql]]
                else:
                    o.qv = dqg[:, gi, ql, goff:goff + 128]
                    o.kcur = dkg[:, gi, sl, goff:goff + 128]
                    o.rd_cur = [b_dkg[gi][sl], b_dqg[gi][ql]]
                o.vcur = vg[:, bi, sl, r * nbl + bl, :]
                o.rd_vcur = [b_vg[bi][sl]]
                if bl > 0:
                    o.kprev = dk[:, sl, goff - 128:goff] if d == 1 else dkg[:, gi, sl, goff - 128:goff]
                    o.vprev = vg[:, bi, sl, r * nbl + bl - 1, :]
                    o.rd_kprev = []
                    o.rd_vprev = []
                elif sp_ > 0:
                    poff = r * (SPAN // d) + (nbl - 1) * 128
                    o.kprev = dk[:, pl, poff:poff + 128] if d == 1 else dkg[:, gi, pl, poff:poff + 128]
                    o.vprev = vg[:, bi, pl, r * nbl + nbl - 1, :]
                    o.rd_kprev = [b_dk[pl]] if d == 1 else [b_dkg[gi][pl]]
                    o.rd_vprev = [b_vg[bi][pl]]
                o.off = bl * 128 * d + r
                return o

            def qk_stage(n):
                it, bi, d, r, bl, first, last = units[n]
                if n == 0:
                    loads(0)
                o = operands(n)
                s = n % 2
                s3 = n % 3
                ksel = ([0] if o.has_prev else []) + [1]
                k0 = ksel[0]
                for idx, ks in enumerate(ksel):
                    for hh in range(2):
                        kk_ = o.kprev if ks == 0 else o.kcur
                        kb.op("pe", lambda e, hh=hh, ks=ks, kk_=kk_, idx=idx: e.matmul(
                            dps[:, s, hh, ks * 128:(ks + 1) * 128],
                            lhsT=kk_[hh * 64:(hh + 1) * 64, :], rhs=o.qv[hh * 64:(hh + 1) * 64, :],
                            start=(idx == 0), stop=False, skip_group_check=True),
                              reads=o.rd_cur + (o.rd_kprev if ks == 0 else []), writes=[b_dps[s]], sig=False)
                for hh in range(2):
                    kb.op("pe", lambda e, hh=hh: e.matmul(dps[:, s, hh, k0 * 128:256], lhsT=ident[:, :],
                                                          rhs=negd[:, k0 * 128:256], start=False, stop=True,
                                                          skip_group_check=True),
                          reads=[b_ident, b_mask], writes=[b_dps[s]], sig=(hh == 1))
                kb.op("act", lambda e: e.activation(out=dPT[:, s3, :, k0:2, :],
                                                    in_=dps[:, s, :, k0 * 128:256].rearrange(
                                                        "p h (k q) -> p h k q", q=128),
                                                    func=AF.Exp, scale=0.125),
                      reads=[b_dps[s]], writes=[b_dPT[s3]])

            def pv_stage(n):
                it, bi, d, r, bl, first, last = units[n]
                j, sp_ = iters[it]
                if first and it + 1 < len(iters):
                    loads(it + 1)
                o = operands(n)
                s = n % 2
                s3 = n % 3
                asl = it % 2
                ksel = ([0] if o.has_prev else []) + [1]
                firstmm = True
                for hh in range(2):
                    for ks in ksel:
                        vv = o.vprev if ks == 0 else o.vcur
                        kb.op("pe", lambda e, hh=hh, ks=ks, vv=vv, firstmm=firstmm: e.matmul(
                            dpo[:, s, 0:128], lhsT=vv[:, hh * 128:(hh + 1) * 128],
                            rhs=dPT[:, s3, hh, ks, :], start=firstmm, stop=False, skip_group_check=True),
                              reads=[b_dPT[s3]] + (o.rd_vprev if ks == 0 else o.rd_vcur), writes=[b_dpo[s]],
                              sig=False)
                        firstmm = False
                for hh in range(2):
                    for ks in ksel:
                        kb.op("pe", lambda e, hh=hh, ks=ks: e.matmul(
                            dpo[:, s, 128:256], lhsT=eones[:, hh, :],
                            rhs=dPT[:, s3, hh, ks, :], start=False, stop=False, skip_group_check=True),
                              reads=[b_dPT[s3], b_mask], writes=[b_dpo[s]], sig=(hh == 1 and ks == 1))
                off = o.off
                anv = an[:, asl, off:off + 127 * d + 1:d]
                adv = ad[:, asl, off:off + 127 * d + 1:d]
                if bi == 0:
                    kb.op("dve", lambda e: e.tensor_copy(anv, dpo[:, s, 0:128]),
                          reads=[b_dpo[s]], writes=[b_an[asl]])
                    kb.op("dve", lambda e: e.tensor_copy(adv, dpo[:, s, 128:256]),
                          reads=[b_dpo[s]], writes=[b_ad[asl]])
                else:
                    kb.op("dve", lambda e: e.tensor_tensor(out=anv, in0=dpo[:, s, 0:128], in1=anv, op=ALU.add),
                          reads=[b_dpo[s], b_an[asl]], writes=[b_an[asl]])
                    kb.op("dve", lambda e: e.tensor_tensor(out=adv, in0=dpo[:, s, 128:256], in1=adv, op=ALU.add),
                          reads=[b_dpo[s], b_ad[asl]], writes=[b_ad[asl]])
                if last:
                    base = sp_ * SPAN
                    kb.op("dve", lambda e: e.reciprocal(out=ad[:, asl, :], in_=ad[:, asl, :]),
                          reads=[b_ad[asl]], writes=[b_ad[asl]])
                    kb.op("pool", lambda e: e.tensor_tensor(out=an[:, asl, :], in0=an[:, asl, :], in1=ad[:, asl, :],
                                                            op=ALU.mult),
                          reads=[b_ad[asl], b_an[asl]], writes=[b_an[asl]])
                    kb.dma("sp", s_an[asl], aT[j, :, base:base + SPAN], an[:, asl, :], reads=[b_an[asl]],
                           writes=[b_aT])

            for n in range(NU + LOOK):
                if n < NU:
                    qk_stage(n)
                if n - LOOK >= 0:
                    pv_stage(n - LOOK)
            kb.barrier()

    def wout_phase(l, src, dst):
        kb.barrier()
        u = "_m3_%d" % l
        with (nc.sbuf_tensor("wo" + u, [128, 8, D], BF16) as wo,
              nc.sbuf_tensor("gd" + u, [128, 4], F32) as gd,
              nc.sbuf_tensor("mt" + u, [128, 3, 4, TT], BF16) as mt,
              nc.sbuf_tensor("at" + u, [128, 3, 4, TT], F32) as at,
              nc.sbuf_tensor("sq" + u, [128, 2, TT], BF16) as sq,
              nc.sbuf_tensor("rs" + u, [128, 2, TT], F32) as rs,
              nc.sbuf_tensor("m2" + u, [128, 2, 4, TT], BF16) as m2,
              nc.sbuf_tensor("xr" + u, [128, 2, D], F32) as xr,
              nc.psum_tensor("pss" + u, [128, 2, 512], F32) as pss,
              nc.psum_tensor("po" + u, [128, 2, 512], F32) as po):
            b_wo = [Buf("wo%d" % k) for k in range(8)]
            b_gd = Buf("gd")
            b_mt = [Buf("mt%d" % i) for i in range(3)]
            s_mt = [kb.dma_sem("mt%d" % i) for i in range(3)]
            b_at = [Buf("at%d" % i) for i in range(3)]
            s_at = [kb.dma_sem("at%d" % i) for i in range(3)]
            b_sq = [Buf("sq0"), Buf("sq1")]
            b_rs = [Buf("rs0"), Buf("rs1")]
            b_m2 = [Buf("m20"), Buf("m21")]
            b_xr = [Buf("xr0"), Buf("xr1")]
            s_xr = [kb.dma_sem("xr0"), kb.dma_sem("xr1")]
            s_xo = [kb.dma_sem("xo0"), kb.dma_sem("xo1")]
            b_pss = [Buf("pss0"), Buf("pss1")]
            b_po = [Buf("po0"), Buf("po1")]
            s_w = kb.dma_sem("wgs")
            for k in range(8):
                kb.dma("pool", s_w, wo[:, k, :], w["w_out"][l, k * 128:(k + 1) * 128, :], writes=[b_wo[k]])
            kb.group(b_wo)
            s_g = kb.dma_sem("g")
            kb.dma("sp", s_g, gd[:, :], w["dil_gain"][l, :].rearrange("(j p) -> p j", p=128), writes=[b_gd],
                   allow_slow_non_contiguous=True)
            sqc = [0]
            dn = [0]
            def wloads(i):
                t3 = i % 3
                c0 = i * TT
                kb.dma("sp", s_mt[t3], mt[:, t3, :, :], mT[:, :, c0:c0 + TT].rearrange("c p t -> p c t"),
                       reads=[b_mT], writes=[b_mt[t3]])
                kb.dma("sp", s_at[t3], at[:, t3, :, :], aT[:, :, c0:c0 + TT].rearrange("c p t -> p c t"),
                       reads=[b_aT], writes=[b_at[t3]])

            def chain(i):
                ts = i % 2
                t3 = i % 3
                for j in range(4):
                    s = sqc[0] % 2
                    sqc[0] += 1
                    kb.op("act", lambda e, j=j: e.activation(out=sq[:, s, :], in_=at[:, t3, j, :], func=AF.Square),
                          reads=[b_at[t3]], writes=[b_sq[s]])
                    kb.op("pe", lambda e, j=j: e.matmul(pss[:, ts, :], lhsT=ones_b[:, :], rhs=sq[:, s, :],
                                                        start=(j == 0), stop=(j == 3)),
                          reads=[b_sq[s], b_mask], writes=[b_pss[ts]])
                kb.op("dve", lambda e: e.tensor_scalar(out=rs[:, ts, :], in0=pss[:, ts, :], scalar1=1.0 / 512,
                                                       scalar2=EPS, op0=ALU.mult, op1=ALU.add),
                      reads=[b_pss[ts]], writes=[b_rs[ts]])
                kb.op("act", lambda e: e.activation(out=rs[:, ts, :], in_=rs[:, ts, :], func=AF.Sqrt),
                      reads=[b_rs[ts]], writes=[b_rs[ts]])
                kb.op("dve", lambda e: e.reciprocal(out=rs[:, ts, :], in_=rs[:, ts, :]),
                      reads=[b_rs[ts]], writes=[b_rs[ts]])
                for j in range(4):
                    kb.op("dve", lambda e, j=j: e.scalar_tensor_tensor(out=m2[:, ts, j, :], in0=at[:, t3, j, :],
                                                                       scalar=gd[:, j:j + 1], in1=rs[:, ts, :],
                                                                       op0=ALU.mult, op1=ALU.mult),
                          reads=[b_at[t3], b_gd, b_rs[ts]], writes=[b_m2[ts]])

            def main(i):
                ts = i % 2
                t3 = i % 3
                c0 = i * TT
                for b in range(4):
                    n = dn[0]
                    dn[0] += 1
                    s = n % 2
                    r0 = c0 + b * 128
                    kb.dma("sp", s_xr[s], xr[:, s, :], src[r0:r0 + 128, :], writes=[b_xr[s]])
                    for half in range(2):
                        for cch in range(8):
                            lhs = mt[:, t3, cch, b * 128:(b + 1) * 128] if cch < 4 else \
                                m2[:, ts, cch - 4, b * 128:(b + 1) * 128]
                            kb.op("pe", lambda e, cch=cch, lhs=lhs: e.matmul(
                                po[:, half, :], lhsT=lhs, rhs=wo[:, cch, half * 512:(half + 1) * 512],
                                start=(cch == 0), stop=(cch == 7)),
                                  reads=[b_wo[cch], b_mt[t3], b_m2[ts]], writes=[b_po[half]], sig=(cch == 7))
                        kb.op("dve", lambda e: e.tensor_tensor(out=xr[:, s, half * 512:(half + 1) * 512],
                                                               in0=po[:, half, :],
                                                               in1=xr[:, s, half * 512:(half + 1) * 512], op=ALU.add),
                              reads=[b_po[half], b_xr[s]], writes=[b_xr[s]])
                    kb.dma("sp", s_xo[s], dst[r0:r0 + 128, :], xr[:, s, :], reads=[b_xr[s]], writes=[])

            wloads(0)
            if NT > 1:
                wloads(1)
            chain(0)
            for i in range(NT):
                if i + 2 < NT:
                    wloads(i + 2)
                if i + 1 < NT:
                    chain(i + 1)
                main(i)
            kb.barrier()

    cur = x_in if first else xs
    def on(p):
        return phases is None or p in phases
    if on("rope"):
        rope_tables()
    for li, l in enumerate(layers):
        is_last = last and (li == len(layers) - 1)
        lam_init = 0.8 - 0.6 * math.exp(-0.3 * l)
        if on("ffn1"):
            ffn_phase(l, 1, cur, xs, False)
            cur = xs
        if on("proj"):
            proj_phase(l, cur)
        if on("diff"):
            diff_phase(l, lam_init)
        if on("dil"):
            dil_phase(l)
        if on("wout"):
            wout_phase(l, cur, xs)
            cur = xs
        if on("ffn2"):
            ffn_phase(l, 2, cur, y_out if is_last else xs, is_last)
            cur = xs
    kb.finish()
    c.n_inst = kb.n_inst
    return nc, c


def make_consts():
    p = np.arange(128)
    inv = (1.0 / (10000.0 ** (np.arange(0, 64, 2, dtype=np.float32) / np.float32(64)))).astype(np.float32)
    ropec = np.zeros((128, 2), np.float32)
    ropec[:, 0] = inv[p % 32]
    ropec[:, 1] = np.where((p % 64) < 32, -1.0, 1.0)
    i = p[:, None]
    masks = np.zeros((128, 1536), np.float32)
    masks[:, 0:512] = (np.arange(512)[None, :] >= i)
    cc = np.arange(128)[None, :]
    masks[:, 512:640] = (i >= cc)
    masks[:, 640:768] = (i <= cc)
    masks[:, 768:1536] = np.where(masks[:, 0:768] > 0, 0.0, -30000.0)
    eones = np.zeros((128, 256), np.float32)
    eones[:, 0:64] = 1.0
    eones[:, 128 + 64:256] = 1.0
    prot = np.zeros((128, 128), np.float32)
    prot[p ^ 32, p] = 1.0
    return {"ident": np.eye(128, dtype=np.float32), "ropec": ropec, "masks": masks, "eones": eones, "prot": prot}


W_NAMES = ("ffn1_norm", "ffn1_gate", "ffn1_up", "ffn1_down", "mix_norm", "w_in", "lambda_q1", "lambda_k1",
           "lambda_q2", "lambda_k2", "subln_gain", "dil_gain", "w_out", "ffn2_norm", "ffn2_gate", "ffn2_up",
           "ffn2_down")

_NC_CACHE = {}


def run_layers(x, inputs, layers, first, last, n_cores=8):
    B, S, _ = x.shape
    LT = int(np.asarray(inputs["w_in"]).shape[0])
    key = (S, tuple(layers), first, last, LT)
    if key not in _NC_CACHE:
        _NC_CACHE[key] = build_nc(S, list(layers), n_layers_total=LT, first=first, last=last)[0]
    nc = _NC_CACHE[key]
    consts = make_consts()
    shared = {k: np.ascontiguousarray(np.asarray(inputs[k], dtype=np.float32)) for k in W_NAMES}
    shared["final_norm"] = np.ascontiguousarray(np.asarray(inputs["final_norm"], dtype=np.float32).reshape(1, D))
    shared["positions"] = np.ascontiguousarray(np.asarray(inputs["positions"], dtype=np.int32))
    shared.update(consts)
    in_maps = []
    for b in range(B):
        m = dict(shared)
        m["x"] = np.ascontiguousarray(x[b])
        in_maps.append(m)
    res = run_bass_kernel_spmd(nc, in_maps, core_ids=list(range(B)))
    return np.stack([np.asarray(r["y"]) for r in res.results], axis=0)


def kernel(**inputs):
    x = np.asarray(inputs["x"], dtype=np.float32)
    return run_layers(x, inputs, list(range(DEPTH)), True, True).astype(np.float32)
```
